# Optimizing a Trainium2 kernel written in Bass

```python
import jax, jax.numpy as jnp
from jax import lax
import numpy as np

D_MODEL = 1024
BATCH = 32
SEQ = 256
DEPTH = 2
DEC_BATCH = 4
DEC_SEQ = 4096
PAST_LEN = 512

GRID_W = 64
HEAD_DIM = 64
BRANCH_WIDTH = D_MODEL // 2
A_HEADS = BRANCH_WIDTH // HEAD_DIM
A_KV_HEADS = A_HEADS // 4
A_GROUP = A_HEADS // A_KV_HEADS
B_GROUPS = 4
B_GROUP_WIDTH = BRANCH_WIDTH // B_GROUPS
POOL_WINDOWS = (2, 4, 8, 16)
C_HEADS = BRANCH_WIDTH // HEAD_DIM
C_WIDTH = C_HEADS * HEAD_DIM
NA_ROWS = 8
NA_COLS = 16
D_HEADS = 8
D_NOPE = 64
D_ROPE = 32
D_V = 64
D_WIDTH = D_HEADS * D_V
D_Q_LORA = 384
D_KV_LORA = 256
Q_BLOCK = 128
ROPE_THETA = 10000.0
EPS = 1e-6
N_EVEN = (DEPTH + 1) // 2
N_ODD = DEPTH // 2
EVEN_IN_SIZES = (A_HEADS * HEAD_DIM, A_KV_HEADS * HEAD_DIM, A_KV_HEADS * HEAD_DIM, BRANCH_WIDTH, BRANCH_WIDTH, BRANCH_WIDTH)
ODD_IN_SIZES = (C_WIDTH, C_WIDTH, C_WIDTH, C_WIDTH, D_Q_LORA, D_KV_LORA, D_ROPE, D_WIDTH)
EVEN_IN = sum(EVEN_IN_SIZES)
ODD_IN = sum(ODD_IN_SIZES)

kernel_name = 'hybrid_flow_prefix_trunk_step'


def split_last(z, sizes):
    out, off = [], 0
    for n in sizes:
        out.append(z[..., off:off + n])
        off += n
    return out


def rms_norm(x, g):
    xf = x.astype(jnp.float32)
    y = xf * lax.rsqrt(jnp.mean(xf * xf, axis=-1, keepdims=True) + EPS)
    return (y * g.astype(jnp.float32)).astype(x.dtype)


def adaln(cond, w, b):
    m = (jax.nn.silu(cond) @ w + b)[:, None, :]
    return split_last(m, (D_MODEL, D_MODEL, D_MODEL))


def pre_norm(x, g, mod):
    shift, scale, _ = mod
    return rms_norm(x, g) * (1 + scale) + shift


def post_residual(x, y, g, mod):
    return x + mod[2] * rms_norm(y, g)


def _rotate(x, ang):
    n = x.shape[-1]
    cos = jnp.cos(ang)[None, :, None, :]
    sin = jnp.sin(ang)[None, :, None, :]
    x1, x2 = x[..., :n // 2], x[..., n // 2:]
    return jnp.concatenate([x1 * cos - x2 * sin, x2 * cos + x1 * sin], axis=-1)


def axial_rope(x):
    S, R = x.shape[1], x.shape[-1]
    half = R // 2
    t = jnp.arange(S)
    inv = ROPE_THETA ** (-jnp.arange(0, half, 2, dtype=jnp.float32) / half)
    xf = x.astype(jnp.float32)
    out_r = _rotate(xf[..., :half], (t // GRID_W).astype(jnp.float32)[:, None] * inv)
    out_c = _rotate(xf[..., half:], (t % GRID_W).astype(jnp.float32)[:, None] * inv)
    return jnp.concatenate([out_r, out_c], axis=-1).astype(x.dtype)


def blocked_attention(q, k, v):
    B, S, KV, G, Dk = q.shape
    nb = S // Q_BLOCK
    scale = Dk ** -0.5
    qb = q.reshape(B, nb, Q_BLOCK, KV, G, Dk).swapaxes(0, 1)

    def one(qblk):
        s = jnp.einsum('bqkgd,bnkd->bkgqn', qblk, k, preferred_element_type=jnp.float32) * scale
        p = jax.nn.softmax(s, axis=-1).astype(v.dtype)
        return jnp.einsum('bkgqn,bnkd->bqkgd', p, v)

    o = lax.map(one, qb)
    return o.swapaxes(0, 1).reshape(B, S, KV, G, v.shape[-1])


def multiscale_pool(u, b_map, b_scale):
    B, S, _ = u.shape
    ug = u.reshape(B, S, B_GROUPS, B_GROUP_WIDTH)
    csum = jnp.pad(jnp.cumsum(ug.astype(jnp.float32), axis=1), ((0, 0), (1, 0), (0, 0), (0, 0)))
    win = jnp.array(POOL_WINDOWS, dtype=jnp.int32)[None, :]
    t = jnp.arange(S)[:, None]
    lo = jnp.clip(t - win // 2, 0, S)
    hi = jnp.clip(t + win - win // 2, 0, S)
    g = jnp.arange(B_GROUPS)[None, :]
    mean = (csum[:, hi, g] - csum[:, lo, g]) / (hi - lo).astype(jnp.float32)[None, :, :, None]
    diff = (mean - ug.astype(jnp.float32)).astype(u.dtype)
    y = jnp.einsum('bsgc,gcd->bsgd', diff, b_map)
    return y.reshape(B, S, BRANCH_WIDTH) * b_scale


def neighborhood_attention(q, k, v, k_ctx, v_ctx, rpb):
    B, S, H, D = q.shape
    L = k_ctx.shape[1]
    rows = S // GRID_W
    wr = min(NA_ROWS, rows)
    nk = wr * NA_COLS
    t = jnp.arange(S)
    qr, qc = t // GRID_W, t % GRID_W
    r0 = jnp.clip(qr - wr // 2, 0, rows - wr)
    c0 = jnp.clip(qc - NA_COLS // 2, 0, GRID_W - NA_COLS)
    kr = r0[:, None, None] + jnp.arange(wr)[None, :, None]
    kc = c0[:, None, None] + jnp.arange(NA_COLS)[None, None, :]
    idx = (kr * GRID_W + kc).reshape(S, nk)
    dr = jnp.broadcast_to(kr - qr[:, None, None] + NA_ROWS - 1, (S, wr, NA_COLS)).reshape(S, nk)
    dc = jnp.broadcast_to(kc - qc[:, None, None] + NA_COLS - 1, (S, wr, NA_COLS)).reshape(S, nk)
    nb = S // Q_BLOCK
    scale = D ** -0.5
    qb = q.reshape(B, nb, Q_BLOCK, H, D).swapaxes(0, 1)

    def one(args):
        qblk, idx_b, dr_b, dc_b = args
        kn = k[:, idx_b]
        vn = v[:, idx_b]
        s_nb = jnp.einsum('bqhd,bqnhd->bhqn', qblk, kn, preferred_element_type=jnp.float32) * scale
        s_nb = s_nb + rpb[:, dr_b, dc_b].astype(jnp.float32)
        s_cx = jnp.einsum('bqhd,blhd->bhql', qblk, k_ctx, preferred_element_type=jnp.float32) * scale
        p = jax.nn.softmax(jnp.concatenate([s_cx, s_nb], axis=-1), axis=-1).astype(v.dtype)
        return (jnp.einsum('bhql,blhd->bqhd', p[..., :L], v_ctx)
                + jnp.einsum('bhqn,bqnhd->bqhd', p[..., L:], vn))

    o = lax.map(one, (qb, idx.reshape(nb, Q_BLOCK, nk), dr.reshape(nb, Q_BLOCK, nk), dc.reshape(nb, Q_BLOCK, nk)))
    return o.swapaxes(0, 1).reshape(B, S, H, D)


def even_mixer(h, ctx_k, ctx_v, w_in, q_norm, k_norm, b_map, b_scale, w_out):
    B, S, _ = h.shape
    q, k, v, ga, ub, gb = split_last(h @ w_in, EVEN_IN_SIZES)
    q = rms_norm(q.reshape(B, S, A_HEADS, HEAD_DIM), q_norm)
    k = rms_norm(k.reshape(B, S, A_KV_HEADS, HEAD_DIM), k_norm)
    v = v.reshape(B, S, A_KV_HEADS, HEAD_DIM)
    if ctx_k is None:
        k_all, v_all = k, v
    else:
        q, k = axial_rope(q), axial_rope(k)
        k_all = jnp.concatenate([ctx_k, k], axis=1)
        v_all = jnp.concatenate([ctx_v, v], axis=1)
    o_a = blocked_attention(q.reshape(B, S, A_KV_HEADS, A_GROUP, HEAD_DIM), k_all, v_all).reshape(B, S, BRANCH_WIDTH)
    o_b = multiscale_pool(ub, b_map, b_scale)
    y = jnp.concatenate([o_a * jax.nn.silu(ga), o_b * jax.nn.silu(gb)], axis=-1) @ w_out
    return y, k, v


def odd_mixer(h, ctx_ck, ctx_cv, ctx_ckv, ctx_kpe, w_in, rpb, q_norm, w_uq, kv_norm, w_ukv, w_out):
    B, S, _ = h.shape
    qc, kc, vc, gc, cq, ckv, kpe, gd = split_last(h @ w_in, ODD_IN_SIZES)
    qc = qc.reshape(B, S, C_HEADS, HEAD_DIM)
    kc = kc.reshape(B, S, C_HEADS, HEAD_DIM)
    vc = vc.reshape(B, S, C_HEADS, HEAD_DIM)
    qd = (rms_norm(cq, q_norm) @ w_uq).reshape(B, S, D_HEADS, D_NOPE + D_ROPE)
    q_nope, q_pe = qd[..., :D_NOPE], qd[..., D_NOPE:]
    ckv = rms_norm(ckv, kv_norm)
    if ctx_ck is None:
        o_c = blocked_attention(qc[:, :, :, None, :], kc, vc)[:, :, :, 0, :]
        ckv_all, kpe_all = ckv, kpe
    else:
        o_c = neighborhood_attention(qc, kc, vc, ctx_ck, ctx_cv, rpb)
        q_pe = axial_rope(q_pe)
        kpe_rot = axial_rope(kpe[:, :, None, :])[:, :, 0, :]
        ckv_all = jnp.concatenate([ctx_ckv, ckv], axis=1)
        kpe_all = jnp.concatenate([ctx_kpe, kpe_rot], axis=1)
    N = ckv_all.shape[1]
    kv = (ckv_all @ w_ukv).reshape(B, N, D_HEADS, D_NOPE + D_V)
    k_d = jnp.concatenate([kv[..., :D_NOPE], jnp.broadcast_to(kpe_all[:, :, None, :], (B, N, D_HEADS, D_ROPE))], axis=-1)
    v_d = kv[..., D_NOPE:]
    q_d = jnp.concatenate([q_nope, q_pe], axis=-1)
    o_d = blocked_attention(q_d[:, :, :, None, :], k_d, v_d)[:, :, :, 0, :].reshape(B, S, D_WIDTH)
    o_c = o_c.reshape(B, S, C_WIDTH)
    y = jnp.concatenate([o_c * jax.nn.silu(gc), o_d * jax.nn.silu(gd)], axis=-1) @ w_out
    return y, kc, vc, ckv, kpe


def setup_inputs(seed: int = 0) -> dict:
    key = jax.random.key(seed)
    ks = jax.random.split(key, 27)
    f32 = jnp.float32

    def nrm(k, shape, scale):
        return jax.random.normal(k, shape, f32) * scale

    return {
        'x_prompt': nrm(ks[0], (BATCH, SEQ, D_MODEL), 1.0),
        'x_sample': nrm(ks[1], (DEC_BATCH, DEC_SEQ, D_MODEL), 1.0),
        'cache_a_k': nrm(ks[2], (DEC_BATCH, N_EVEN, PAST_LEN, A_KV_HEADS, HEAD_DIM), 1.0),
        'cache_a_v': nrm(ks[3], (DEC_BATCH, N_EVEN, PAST_LEN, A_KV_HEADS, HEAD_DIM), 1.0),
        'cache_c_k': nrm(ks[4], (DEC_BATCH, N_ODD, PAST_LEN, C_HEADS, HEAD_DIM), 1.0),
        'cache_c_v': nrm(ks[5], (DEC_BATCH, N_ODD, PAST_LEN, C_HEADS, HEAD_DIM), 1.0),
        'cache_d_ckv': nrm(ks[6], (DEC_BATCH, N_ODD, PAST_LEN, D_KV_LORA), 1.0),
        'cache_d_kpe': nrm(ks[7], (DEC_BATCH, N_ODD, PAST_LEN, D_ROPE), 1.0),
        'c': nrm(ks[8], (DEC_BATCH, D_MODEL), 1.0),
        'c_ctx': nrm(ks[9], (D_MODEL,), 1.0),
        'w_mod': nrm(ks[10], (DEPTH, D_MODEL, 3 * D_MODEL), D_MODEL ** -0.5),
        'b_mod': nrm(ks[11], (DEPTH, 3 * D_MODEL), 0.02),
        'g_pre': 1.0 + nrm(ks[12], (DEPTH, D_MODEL), 0.05),
        'g_post': 1.0 + nrm(ks[13], (DEPTH, D_MODEL), 0.05),
        'w_in_e': nrm(ks[14], (N_EVEN, D_MODEL, EVEN_IN), D_MODEL ** -0.5),
        'a_q_norm': 1.0 + nrm(ks[15], (N_EVEN, HEAD_DIM), 0.05),
        'a_k_norm': 1.0 + nrm(ks[16], (N_EVEN, HEAD_DIM), 0.05),
        'b_map': nrm(ks[17], (N_EVEN, B_GROUPS, B_GROUP_WIDTH, B_GROUP_WIDTH), B_GROUP_WIDTH ** -0.5),
        'b_scale': 1.0 + nrm(ks[18], (N_EVEN, BRANCH_WIDTH), 0.1),
        'w_out_e': nrm(ks[19], (N_EVEN, 2 * BRANCH_WIDTH, D_MODEL), (2 * BRANCH_WIDTH) ** -0.5),
        'w_in_o': nrm(ks[20], (N_ODD, D_MODEL, ODD_IN), D_MODEL ** -0.5),
        'c_rpb': nrm(ks[21], (N_ODD, C_HEADS, 2 * NA_ROWS - 1, 2 * NA_COLS - 1), 0.2),
        'd_q_norm': 1.0 + nrm(ks[22], (N_ODD, D_Q_LORA), 0.05),
        'd_w_uq': nrm(ks[23], (N_ODD, D_Q_LORA, D_HEADS * (D_NOPE + D_ROPE)), D_Q_LORA ** -0.5),
        'd_kv_norm': 1.0 + nrm(ks[24], (N_ODD, D_KV_LORA), 0.05),
        'd_w_ukv': nrm(ks[25], (N_ODD, D_KV_LORA, D_HEADS * (D_NOPE + D_V)), D_KV_LORA ** -0.5),
        'w_out_o': nrm(ks[26], (N_ODD, C_WIDTH + D_WIDTH, D_MODEL), (C_WIDTH + D_WIDTH) ** -0.5),
    }


def reference(x_prompt, x_sample, cache_a_k, cache_a_v, cache_c_k, cache_c_v, cache_d_ckv, cache_d_kpe,
              c, c_ctx, w_mod, b_mod, g_pre, g_post, w_in_e, a_q_norm, a_k_norm, b_map, b_scale, w_out_e,
              w_in_o, c_rpb, d_q_norm, d_w_uq, d_kv_norm, d_w_ukv, w_out_o):
    xp, xs = x_prompt, x_sample
    new_a_k, new_a_v, new_c_k, new_c_v, new_d_ckv, new_d_kpe = [], [], [], [], [], []
    for layer in range(DEPTH):
        j = layer // 2
        mod_p = adaln(c_ctx[None, :], w_mod[layer], b_mod[layer])
        mod_s = adaln(c, w_mod[layer], b_mod[layer])
        hp = pre_norm(xp, g_pre[layer], mod_p)
        hs = pre_norm(xs, g_pre[layer], mod_s)
        if layer % 2 == 0:
            yp, k_a, v_a = even_mixer(hp, None, None, w_in_e[j], a_q_norm[j], a_k_norm[j],
                                      b_map[j], b_scale[j], w_out_e[j])
            ys = even_mixer(hs, cache_a_k[:, j], cache_a_v[:, j], w_in_e[j], a_q_norm[j], a_k_norm[j],
                            b_map[j], b_scale[j], w_out_e[j])[0]
            new_a_k.append(k_a)
            new_a_v.append(v_a)
        else:
            yp, k_c, v_c, ckv, kpe = odd_mixer(hp, None, None, None, None, w_in_o[j], c_rpb[j], d_q_norm[j],
                                               d_w_uq[j], d_kv_norm[j], d_w_ukv[j], w_out_o[j])
            ys = odd_mixer(hs, cache_c_k[:, j], cache_c_v[:, j], cache_d_ckv[:, j], cache_d_kpe[:, j],
                           w_in_o[j], c_rpb[j], d_q_norm[j], d_w_uq[j], d_kv_norm[j], d_w_ukv[j], w_out_o[j])[0]
            new_c_k.append(k_c)
            new_c_v.append(v_c)
            new_d_ckv.append(ckv)
            new_d_kpe.append(kpe)
        xp = post_residual(xp, yp, g_post[layer], mod_p)
        xs = post_residual(xs, ys, g_post[layer], mod_s)
    return (xp, xs, jnp.stack(new_a_k, axis=1), jnp.stack(new_a_v, axis=1), jnp.stack(new_c_k, axis=1),
            jnp.stack(new_c_v, axis=1), jnp.stack(new_d_ckv, axis=1), jnp.stack(new_d_kpe, axis=1))
```

```python
import numpy as np
from contextlib import ExitStack
import concourse.bass as bass
import concourse.mybir as mybir
from concourse.bass_utils import run_bass_kernel_spmd

F32 = mybir.dt.float32
BF = mybir.dt.bfloat16
AF = mybir.ActivationFunctionType
ALU = mybir.AluOpType
AX = mybir.AxisListType
P = 128


class FW:
    ND = 24
    NHW = 16

    def __init__(self, nc, es):
        self.nc = nc
        self.eng = {'pe': nc.tensor, 'act': nc.scalar, 'dve': nc.vector, 'pool': nc.gpsimd, 'sp': nc.sync}
        self.sem = {e: es.enter_context(nc.semaphore('s_' + e)) for e in ('pe', 'act', 'dve', 'pool')}
        self.n = {e: 0 for e in self.sem}
        self.dsem = [es.enter_context(nc.semaphore('d%d' % i)) for i in range(self.ND)]
        self.dn = [0] * self.ND
        self.rr = 0
        self.rr_sw = 0
        self.waited = {e: {} for e in self.eng}
        self.state = {}
        self.ns = None
        self.nskeys = set()

    def _k(self, keys):
        if self.ns is None:
            return list(keys)
        out = []
        for k in keys:
            base = k if isinstance(k, str) else k[0]
            out.append(('ns', self.ns, k) if base in self.nskeys else k)
        return out

    def _wait(self, eng, tid, seq):
        if self.waited[eng].get(tid, 0) >= seq:
            return
        self.waited[eng][tid] = seq
        if tid[0] == 'e':
            assert seq <= self.n[tid[1]], (eng, tid, seq, self.n[tid[1]])
            self.eng[eng].wait_ge(self.sem[tid[1]], seq)
        elif tid[0] == 'c':
            sem, amount = self.csem[tid[1]]
            self.eng[eng].wait_ge(sem, amount * seq)
        else:
            self.eng[eng].wait_ge(self.dsem[tid[1]], 16 * seq)

    def _dep(self, eng, tok, raw, strict):
        tid, seq = tok
        if tid == ('e', eng) and not strict:
            if eng == 'pe':
                return
        self._wait(eng, tid, seq)

    def _sync(self, eng, r, w, strict=False):
        for k in r:
            st = self.state.get(k)
            if st and st[0]:
                self._dep(eng, st[0], True, strict)
        for k in w:
            st = self.state.get(k)
            if st:
                if st[0]:
                    self._dep(eng, st[0], True, strict)
                for tid, seq in st[1].items():
                    self._dep(eng, (tid, seq), False, strict)

    def _mark(self, tid, seq, r, w):
        for k in r:
            st = self.state.setdefault(k, [None, {}])
            st[1][tid] = seq
        for k in w:
            self.state[k] = [(tid, seq), {}]

    @staticmethod
    def _excl(r, w):
        px = [k for k in r if k == 'psT' or (isinstance(k, tuple) and k[0] in ('A', 'S', 'O'))]
        if not px:
            return r, w
        return [k for k in r if k not in px], list(w) + px

    def op(self, eng, fn, r=(), w=()):
        r, w = self._excl(self._k(r), self._k(w))
        self._sync(eng, r, w)
        ins = fn()
        self.n[eng] += 1
        ins.then_inc(self.sem[eng], 1)
        self._mark(('e', eng), self.n[eng], r, w)
        return ins

    def op_noinc(self, eng, fn, r=(), w=()):
        r, w = self._excl(self._k(r), self._k(w))
        self._sync(eng, r, w)
        fn()
        self._mark(('e', eng), self.n[eng] + 1, r, w)

    def dma(self, q, out, in_, r=(), w=(), **kw):
        r, w = self._k(r), self._k(w)
        self._sync(q, r, w, strict=True)
        if q == 'pool':
            i = self.NHW + self.rr_sw
            self.rr_sw = (self.rr_sw + 1) % (self.ND - self.NHW)
        else:
            i = self.rr
            self.rr = (i + 1) % self.NHW
        if self.dn[i] > 0:
            self._wait(q, ('d', i), self.dn[i])
        ins = self.eng[q].dma_start(out=out, in_=in_, **kw)
        self.dn[i] += 1
        ins.then_inc(self.dsem[i], 16)
        self._mark(('d', i), self.dn[i], r, w)
        return ins

    def custom(self, eng, fn, sem, amount, r=(), w=()):
        self._sync(eng, r, w, strict=True)
        ins = fn()
        ins.then_inc(sem, amount)
        self.csem = getattr(self, 'csem', {})
        cid = len(self.csem)
        self.csem[cid] = (sem, amount)
        self._mark(('c', cid), 1, r, w)
        return ins

    def barrier(self):
        for e in self.eng:
            for e2 in self.sem:
                if e2 != e and self.n[e2] > 0:
                    self._wait(e, ('e', e2), self.n[e2])
            for i in range(self.ND):
                if self.dn[i] > 0:
                    self._wait(e, ('d', i), self.dn[i])
        self.state.clear()

    def finish(self):
        for i in range(self.ND):
            if self.dn[i] > 0:
                self._wait('sp', ('d', i), self.dn[i])
        for cid in getattr(self, 'csem', {}):
            self._wait('sp', ('c', cid), 1)


D_MODEL = 1024
NEG = -30000.0
EPS = 1e-6
PERM_A = np.concatenate([np.r_[j * 64:(j + 1) * 64, (4 + j) * 64:(5 + j) * 64] for j in range(4)])
POOL_W = (2, 4, 8, 16)
XIN_ROWS = 288
XINB_ROWS = 256
OFF_KPE = 256 * 2048
OFF_KC = 0
OFF_VC = 512 * 512


class Ctx:
    pass


_SBT_N = [0]


def sbt(nc, es, name, shape, dt):
    _SBT_N[0] += 1
    return es.enter_context(nc.sbuf_tensor('sb%d_%s' % (_SBT_N[0], name), list(shape), dt))


def build_program(stop_after=None):
    nc = bass.Bass("TRN2", target_bir_lowering=False)
    D = {}

    def din(name, shape, dt=F32):
        D[name] = nc.dram_tensor(name, list(shape), dt, kind="ExternalInput").ap()

    def dout(name, shape):
        D[name] = nc.dram_tensor(name, list(shape), F32, kind="ExternalOutput").ap()

    def dscr(name, shape, dt):
        if stop_after is not None and name in ('x1s', 'x1p'):
            D[name] = nc.dram_tensor(name, list(shape), dt, kind="ExternalOutput").ap()
        else:
            D[name] = nc.dram_tensor(name, list(shape), dt).ap()

    din('xs', [4096, 1024]); din('xp', [1024, 1024])
    din('condT', [128, 16]); din('w_mod', [2, 1024, 3072]); din('b_mod', [2, 3072])
    din('g_preT', [128, 32]); din('g_post', [2, 1024]); din('sel', [2, 256])
    din('w_in_e', [1024, 2304]); din('gain_a', [128, 640]); din('b_mapT', [128, 512]); din('b_scaleT', [128, 4])
    din('w_out_e', [1024, 1024]); din('rope_a', [4096, 128]); din('prc_s', [128, 4 * 2048]); din('prc_p', [128, 4 * 256])
    din('hmask', [128, 2]); din('cak_T', [128, 512]); din('cav', [512, 128])
    din('w1_fm', [1024, 2048]); din('w1_tm', [1024, 1696]); din('w_kpe2', [1024, 192])
    din('gain_q', [128, 384]); din('gain_kv', [128, 256])
    din('w_uq_ab', [384, 1536]); din('w_uk', [256, 512]); din('w_uv', [256, 512]); din('w_out_o', [1024, 1024])
    din('rope_d', [128, 2 * 2048]); din('cck_T', [512, 512]); din('ccv', [512, 512]); din('cckv_T', [256, 512])
    din('ckpe_T', [128, 512]); din('nabias', [8, 128, 4480])
    dout('y_s', [2048, 1024]); dout('y_p', [1024, 1024]); dout('nak', [1024, 128]); dout('nav', [1024, 128])
    dout('nck', [1024, 512]); dout('ncv', [1024, 512]); dout('nckv', [1024, 256]); dout('nkpe', [1024, 32])
    dscr('x1s', [2048, 1024], F32); dscr('x1p', [1024, 1024], F32)
    dscr('s_qc', [512, 2048], BF); dscr('s_kc', [512, 2048], BF); dscr('s_gc', [512, 2048], BF); dscr('s_gd', [512, 2048], BF)
    dscr('s_cq', [384, 2048], BF); dscr('s_vc', [2048, 512], BF)
    dscr('xin', [XIN_ROWS, 2048], BF); dscr('xout', [2 * XIN_ROWS, 2048], BF)
    dscr('xinB', [XINB_ROWS, 2048], BF); dscr('xoutB', [2 * XINB_ROWS, 2048], BF)

    C = Ctx()
    C.nc = nc; C.D = D
    with ExitStack() as gs:
        fw = FW(nc, gs)
        C.fw = fw
        C.cc_sem = gs.enter_context(nc.semaphore('cc_sem'))
        C.cc_sem2 = gs.enter_context(nc.semaphore('cc_sem2'))
        C.ident_b = sbt(nc, gs, 'ident_b', [128, 128], BF)
        C.ident_f = sbt(nc, gs, 'ident_f', [128, 128], F32)
        C.modS = sbt(nc, gs, 'modS', [128, 32], F32)
        C.modH = sbt(nc, gs, 'modH', [128, 32], F32)
        C.gg = sbt(nc, gs, 'gg', [128, 4, 1024], F32)
        C.stat = sbt(nc, gs, 'stat', [128, 64], F32)
        C.stat_i = 0
        C.xbuf = sbt(nc, gs, 'xbuf', [128, 4, 1024], F32)
        C.xb_i = 0
        C.xn = sbt(nc, gs, 'xn', [128, 4, 1024], BF)
        C.junk = sbt(nc, gs, 'junk', [128, 1024], BF)
        C.ty = sbt(nc, gs, 'ty', [128, 1024], F32)
        C.pbuf = sbt(nc, gs, 'pbuf', [128, 6, 512], BF)
        C.p_i = 0
        C.rcb = sbt(nc, gs, 'rcb', [128, 2, 512], F32)
        C.qz = sbt(nc, gs, 'qz', [128, 2, 2, 512], BF)
        C.qz_i = [0, 0]
        C.psS = [gs.enter_context(nc.psum_tensor('psS%d' % i, [128, 512], F32)) for i in range(3)]
        C.s_i = 0
        C.psO = [gs.enter_context(nc.psum_tensor('psO%d' % i, [128, 512], F32)) for i in range(2)]
        C.o_i = 0
        C.psA = gs.enter_context(nc.psum_tensor('psA', [128, 1024], F32))
        C.a_i = 0
        C.a_wide = True
        C.psT = gs.enter_context(nc.psum_tensor('psT', [128, 1024], BF))
        fw.op('pool', lambda: nc.gpsimd.memset(C.qz[:], 0.0), w=[('qz', a, b) for a in range(2) for b in range(2)])
        for t, k in ((C.ident_b, 'ident_b'), (C.ident_f, 'ident_f')):
            fw.op('pool', lambda: nc.gpsimd.memset(t[:], 1.0), w=[k])
            fw.op('pool', lambda: nc.gpsimd.affine_select(out=t[:], in_=t[:], pattern=[[-1, 128]], compare_op=ALU.is_equal,
                                                          fill=0.0, base=0, channel_multiplier=1), r=[k], w=[k])
        import os as _os
        if _os.environ.get('KSKIP_L0'):
            phase_setup(C, (0,))
        if stop_after != 'setup' and not _os.environ.get('KSKIP_L0'):
            phase_l0(C, stop_after)
            fw.barrier()
        if stop_after is None or stop_after.startswith('l1'):
            phase_l1(C, stop_after)
            fw.barrier()
        fw.finish()
    return nc


def stat_col(C):
    i = C.stat_i
    C.stat_i = (i + 1) % 64
    return ('st', i), C.stat[:, i:i + 1]


def stat_cols(C, n):
    if C.stat_i + n > 64:
        C.stat_i = 0
    i = C.stat_i
    C.stat_i = (i + n) % 64
    return [('st', j) for j in range(i, i + n)], C.stat[:, i:i + n]


def next_A(C):
    if getattr(C, 'a_wide', False):
        i = C.a_i % 5
        C.a_i = i + 1
        if i < 2:
            return ('A', i), C.psA[:, i * 512:(i + 1) * 512]
        return ('S', i - 2), C.psS[i - 2][:, :]
    i = C.a_i % 2
    C.a_i = i + 1
    return ('A', i), C.psA[:, i * 512:(i + 1) * 512]


def phase_setup(C, layers=(0, 1), after_issue=None, stack=None):
    nc, fw, D = C.nc, C.fw, C.D
    V, A, PE = nc.vector, nc.scalar, nc.tensor
    own = ExitStack() if stack is None else None
    with (own if own is not None else ExitStack()) as _tmp:
        s0 = own if own is not None else stack
        condT = sbt(nc, s0, 'condT', [128, 16], F32); scT = sbt(nc, s0, 'scT', [128, 16], BF)
        wm = sbt(nc, s0, 'wm', [128, 8, 1024], BF)
        mrow = sbt(nc, s0, 'mrow', [2, 3072], F32); brow = sbt(nc, s0, 'brow', [2, 3072], F32)
        gpT = sbt(nc, s0, 'gpT', [128, 32], F32); sels = sbt(nc, s0, 'sels', [2, 256], F32)
        gpost = sbt(nc, s0, 'gpost', [128, 1024], F32)
        fw.dma('sp', condT[:], D['condT'], w=['condT'])
        fw.dma('sp', gpT[:], D['g_preT'], w=['gpT'])
        fw.dma('sp', sels[:], D['sel'], w=['sels'])
        fw.op('act', lambda: A.activation(out=scT[:], in_=condT[:], func=AF.Silu), r=['condT'], w=['scT'])
        for l in layers:
            fw.dma('sp', brow[:], D['b_mod'][l:l + 1, :].partition_broadcast(2), w=['brow'])
            fw.dma('sp', gpost[:], D['g_post'][l:l + 1, :].partition_broadcast(128), w=['gpost'])
            wsrc = D['w_mod'][l].rearrange("(c p) n -> p c n", p=128)
            for blk in range(3):
                for c in range(8):
                    fw.dma('pool', wm[:, c, :], wsrc[:, c, blk * 1024:(blk + 1) * 1024], w=[('wm', c)])
                for sub in range(2):
                    col0 = blk * 1024 + sub * 512
                    ka, pa = next_A(C)
                    for c in range(8):
                        f = lambda: PE.matmul(pa[0:2, :], lhsT=scT[:, 2 * c:2 * c + 2], rhs=wm[:, c, sub * 512:(sub + 1) * 512],
                                              start=(c == 0), stop=(c == 7))
                        if c < 7:
                            fw.op_noinc('pe', f, r=['scT', ('wm', c)], w=[ka])
                        else:
                            fw.op('pe', f, r=['scT', ('wm', c)], w=[ka])
                    fw.op('dve', lambda: V.tensor_tensor(out=mrow[0:2, col0:col0 + 512], in0=pa[0:2, :], in1=brow[0:2, col0:col0 + 512],
                                                         op=ALU.add), r=[ka, 'brow'], w=['mrow'])
            ka, pa = next_A(C)
            for c in range(16):
                f = lambda: PE.transpose(out=pa[:, 2 * c:2 * c + 2], in_=mrow[0:2, c * 128:(c + 1) * 128], identity=C.ident_f[0:2, 0:2])
                if c < 15:
                    fw.op_noinc('pe', f, r=['mrow', 'ident_f'], w=[ka])
                else:
                    fw.op('pe', f, r=['mrow', 'ident_f'], w=[ka])
            fw.op('dve', lambda: V.tensor_copy(out=C.modH[:, l * 16:(l + 1) * 16], in_=pa[:, 0:16]), r=[ka], w=['modH'])
            fw.op('dve', lambda: V.scalar_tensor_tensor(out=C.modS[:, l * 16:(l + 1) * 16], in0=pa[:, 16:32], scalar=1.0,
                                                        in1=gpT[:, l * 16:(l + 1) * 16], op0=ALU.add, op1=ALU.mult),
                  r=[ka, 'gpT'], w=['modS'])
            for g in range(2):
                for half in range(2):
                    ka, pa = next_A(C)
                    fw.op('pe', lambda: PE.matmul(pa, lhsT=sels[:, g * 128:(g + 1) * 128],
                                                  rhs=mrow[0:2, 2048 + half * 512:2048 + (half + 1) * 512], start=True, stop=True),
                          r=['sels', 'mrow'], w=[ka])
                    fw.op('dve', lambda: V.tensor_tensor(out=C.gg[:, l * 2 + g, half * 512:(half + 1) * 512], in0=pa,
                                                         in1=gpost[:, half * 512:(half + 1) * 512], op=ALU.mult),
                          r=[ka, 'gpost'], w=[('gg', l * 2 + g)])
        if after_issue is not None:
            after_issue()
        if own is not None:
            fw.barrier()


def load_x(C, src_rows):
    i = C.xb_i
    C.xb_i = (i + 1) % 4
    C.fw.dma('sp', C.xbuf[:, i, :], src_rows, w=[('xb', i)])
    return ('xb', i), C.xbuf[:, i, :]


def rstd_col(C, ss_key, ss_ap, n, width):
    nc, fw = C.nc, C.fw
    keys, rs = stat_cols(C, n)
    fw.op('act', lambda: nc.scalar.activation(out=rs, in_=ss_ap, func=AF.Sqrt, scale=1.0 / width, bias=EPS), r=ss_key, w=keys)
    fw.op('dve', lambda: nc.vector.reciprocal(out=rs, in_=rs), r=keys, w=keys)
    return keys, rs


def prenorm(C, l, g, xts, hT, hkey):
    nc, fw = C.nc, C.fw
    V, A, PE = nc.vector, nc.scalar, nc.tensor
    n = len(xts)
    for i, (xk, xt) in enumerate(xts):
        ks, ss = stat_col(C)
        fw.op('act', lambda: A.activation(out=C.junk[:], in_=xt, func=AF.Square, accum_out=ss), r=[xk], w=[ks])
        kr, rs = rstd_col(C, [ks], ss, 1, 1024.0)
        fw.op('dve', lambda: V.tensor_scalar(out=C.xn[:, i, :], in0=xt, scalar1=rs, scalar2=None, op0=ALU.mult),
              r=[xk] + kr, w=[('xn', i)])
    for cp in range(4):
        for c in (2 * cp, 2 * cp + 1):
            for i in range(n):
                f = lambda: PE.transpose(out=C.psT[:, (c % 2) * 512 + i * 128:(c % 2) * 512 + (i + 1) * 128],
                                         in_=C.xn[:, i, c * 128:(c + 1) * 128], identity=C.ident_b[:])
                if c == 2 * cp + 1 and i == n - 1:
                    fw.op('pe', f, r=[('xn', i), 'ident_b'], w=['psT'])
                else:
                    fw.op_noinc('pe', f, r=[('xn', i), 'ident_b'], w=['psT'])
        for c in (2 * cp, 2 * cp + 1):
            j = l * 16 + c * 2 + g
            fw.op('dve', lambda: V.tensor_scalar(out=hT[:, c, 0:n * 128], in0=C.psT[:, (c % 2) * 512:(c % 2) * 512 + n * 128],
                                                 scalar1=C.modS[:, j:j + 1], scalar2=C.modH[:, j:j + 1], op0=ALU.mult, op1=ALU.add),
                  r=['psT', 'modS', 'modH'], w=[(hkey, c)])


def mm_acc(C, out_ap, out_key, pairs, rkeys):
    fw, PE = C.fw, C.nc.tensor
    n = len(pairs)
    for i, (l_, r_) in enumerate(pairs):
        f = lambda: PE.matmul(out_ap, lhsT=l_, rhs=r_, start=(i == 0), stop=(i == n - 1))
        if i < n - 1:
            fw.op_noinc('pe', f, r=rkeys, w=[out_key])
        else:
            fw.op('pe', f, r=rkeys, w=[out_key])


def load_w_cast(C, dst, src, nchunk, ncols, key):
    for c in range(nchunk):
        for c0 in range(0, ncols, 1024):
            c1 = min(ncols, c0 + 1024)
            C.fw.dma('pool', dst[:, c, c0:c1], src[c * 128:(c + 1) * 128, c0:c1], w=[key])


def pad_q(C, qT, qkeys, side, nq):
    fw, G = C.fw, C.nc.gpsimd
    rows = slice(0, 64) if side == 0 else slice(64, 128)
    b = C.qz_i[side]
    C.qz_i[side] = 1 - b
    fw.op('pool', lambda: G.tensor_copy(out=C.qz[rows, side, b, 0:nq], in_=qT), r=qkeys, w=[('qz', side, b)])
    return C.qz[:, side, b, 0:nq], [('qz', side, b)]


def run_attn_calls(C, calls, la=2, banks=None, extra=None):
    nxt = pad_q(C, calls[0]['q64'], calls[0]['qkeys'], calls[0]['side'], calls[0]['nq'])
    for i, c in enumerate(calls):
        cur = nxt
        box = {}
        n_ = calls[i + 1] if i + 1 < len(calls) else None
        ex = extra[i] if (extra is not None and i < len(extra)) else None

        def fl(n_=n_, box=box, ex=ex):
            if n_ is not None:
                box['v'] = pad_q(C, n_['q64'], n_['qkeys'], n_['side'], n_['nq'])
            if ex is not None:
                ex()
        attention(C, cur[0], c['nq'], cur[1], c['kts'], c['scale'], c['side'], c['dst'], c['gate'], c['dkeys'], c['gkeys'], filler=fl, la=la, banks=banks)
        nxt = box.get('v')
    if extra is not None:
        for ex in extra[len(calls):]:
            ex()


def attention(C, qT, nq, qkeys, keytiles, scale, side, dst, gate, dkeys, gkeys, pad=False, filler=None, la=2, banks=None):
    nc, fw = C.nc, C.fw
    V, A, PE, G = nc.vector, nc.scalar, nc.tensor, nc.gpsimd
    wide_saved = getattr(C, 'a_wide', False)
    C.a_wide = False
    oi = C.o_i
    C.o_i = 1 - oi
    psO = C.psO[oi]; okey = ('O', oi)
    nk = len(keytiles)
    orows = slice(0, 64) if side == 0 else slice(64, 128)
    srows = slice(64, 128) if side == 0 else slice(0, 64)
    pend = []
    if pad:
        qT, qkeys = pad_q(C, qT, qkeys, side, nq)

    def qk(idx):
        kT, va, eb, rk = keytiles[idx][:4]
        a, b_ = keytiles[idx][4] if len(keytiles[idx]) > 4 else (0, nq)
        bk = banks if banks is not None else [(C.psS[i][:, :], ('S', i)) for i in range(3)]
        si = C.s_i % len(bk)
        C.s_i = si + 1
        sap, skey = bk[si]
        fw.op('pe', lambda: PE.matmul(sap[:, a:b_], lhsT=kT, rhs=qT[:, a:b_], start=True, stop=True), r=rk + qkeys, w=[skey])
        pi = C.p_i
        C.p_i = (pi + 1) % 6
        fw.op('act', lambda: A.activation(out=C.pbuf[:, pi, a:b_], in_=sap[:, a:b_], func=AF.Exp, scale=scale),
              r=[skey], w=[('p', pi)])
        if eb is not None:
            for (ebap, c0, c1, ekeys, eng) in eb:
                e = V if eng == 'dve' else G
                fw.op(eng, lambda: e.tensor_tensor(out=C.pbuf[:, pi, c0:c1], in0=C.pbuf[:, pi, c0:c1], in1=ebap, op=ALU.mult),
                      r=[('p', pi)] + ekeys, w=[('p', pi)])
        pend.append((idx, pi))

    def pv():
        idx, pi = pend.pop(0)
        kT, va, eb, rk = keytiles[idx][:4]
        a, b_ = keytiles[idx][4] if len(keytiles[idx]) > 4 else (0, nq)
        assert idx > 0 or (a, b_) == (0, nq)
        f = lambda: PE.matmul(psO[:, a:b_], lhsT=va, rhs=C.pbuf[:, pi, a:b_], start=(idx == 0), stop=(idx == nk - 1))
        if idx == nk - 1:
            fw.op('pe', f, r=rk + [('p', pi)], w=[okey])
        else:
            fw.op_noinc('pe', f, r=rk + [('p', pi)], w=[okey])

    for idx in range(nk):
        qk(idx)
        if idx >= la:
            pv()
        if filler is not None and idx == min(3, nk - 1):
            filler()
    while pend:
        pv()
    fw.op('dve', lambda: V.reciprocal(out=C.rcb[srows, oi, 0:nq], in_=psO[srows, 0:nq]), r=[okey], w=[('rcb', oi)])
    fw.op('dve', lambda: V.tensor_tensor(out=C.rcb[orows, oi, 0:nq], in0=psO[orows, 0:nq], in1=C.rcb[srows, oi, 0:nq], op=ALU.mult),
          r=[okey, ('rcb', oi)], w=[('rcb', oi)])
    fw.op('dve', lambda: V.tensor_tensor(out=dst, in0=C.rcb[orows, oi, 0:nq], in1=gate, op=ALU.mult),
          r=[('rcb', oi)] + gkeys, w=dkeys)
    C.a_wide = wide_saved


def outproj_residual(C, l, g, AO, ao_keys, wo, n, xts, dst_rows_fn, ao_fn=None):
    nc, fw = C.nc, C.fw
    V, A, PE = nc.vector, nc.scalar, nc.tensor
    sets = [[(C.psA[:, 0:512], ('A', 0)), (C.psA[:, 512:1024], ('A', 1))],
            [(C.psS[0][:, :], ('S', 0)), (C.psS[1][:, :], ('S', 1))]]
    for i, (xk, xt) in enumerate(xts):
        hs = sets[i % 2]
        for half in range(2):
            mm_acc(C, hs[half][0], hs[half][1],
                   [((ao_fn(c, i) if ao_fn else AO[:, c, i * 128:(i + 1) * 128]), wo[:, c, half * 512:(half + 1) * 512]) for c in range(8)],
                   ao_keys + ['wo'])
        ks, ss2 = stat_cols(C, 2)
        for half in range(2):
            fw.op('act', lambda: A.activation(out=C.junk[:, 0:512], in_=hs[half][0], func=AF.Square, accum_out=ss2[:, half:half + 1]),
                  r=[hs[half][1]], w=[ks[half]])
        k1, ss = stat_col(C)
        fw.op('dve', lambda: V.tensor_tensor(out=ss, in0=ss2[:, 0:1], in1=ss2[:, 1:2], op=ALU.add), r=ks, w=[k1])
        kr, rs = rstd_col(C, [k1], ss, 1, 1024.0)
        for half in range(2):
            fw.op('dve', lambda: V.tensor_tensor(out=C.ty[:, half * 512:(half + 1) * 512], in0=hs[half][0],
                                                 in1=C.gg[:, l * 2 + g, half * 512:(half + 1) * 512], op=ALU.mult),
                  r=[hs[half][1], ('gg', l * 2 + g)], w=[('ty', half)])
        fw.op('dve', lambda: V.scalar_tensor_tensor(out=xt, in0=C.ty[:], scalar=rs, in1=xt, op0=ALU.mult, op1=ALU.add),
              r=[('ty', 0), ('ty', 1), xk] + kr, w=[xk])
        fw.dma('sp', dst_rows_fn(i), xt, r=[xk])


def normrope(C, L, z_ps, zkey, H, gain, rope, out_bf, okeys):
    nc, fw = C.nc, C.fw
    V, A = nc.vector, nc.scalar
    W = H * 64
    v3 = lambda ap: ap.rearrange("p (h d) -> p h d", d=64)
    fw.op('act', lambda: A.activation(out=L.sq[:, 0:W], in_=z_ps, func=AF.Square), r=[zkey], w=['sq'])
    ks, ssq = stat_cols(C, H)
    fw.op('dve', lambda: V.tensor_reduce(out=ssq, in_=v3(L.sq[:, 0:W]), axis=AX.X, op=ALU.add), r=['sq'], w=ks)
    kr, rs = rstd_col(C, ks, ssq, H, 64.0)
    fw.op('dve', lambda: V.tensor_tensor(out=v3(L.zg[:, 0:W]), in0=v3(z_ps), in1=rs.unsqueeze(2).broadcast_to([128, H, 64]), op=ALU.mult),
          r=[zkey] + kr, w=['zg'])
    fw.op('dve', lambda: V.tensor_tensor(out=L.zg[:, 0:W], in0=L.zg[:, 0:W], in1=gain, op=ALU.mult), r=['zg', 'gainA'], w=['zg'])
    if rope is None:
        fw.op('act', lambda: A.copy(out=out_bf, in_=L.zg[:, 0:W]), r=['zg'], w=okeys)
        return
    rk, rt = rope[0], rope[1]
    fw.op('dve', lambda: V.tensor_tensor(out=v3(L.t1[:, 0:W]), in0=v3(L.zg[:, 0:W]),
                                         in1=rt[:, 0:64].unsqueeze(1).broadcast_to([128, H, 64]), op=ALU.mult),
          r=['zg', rk], w=['t1'])
    z4 = L.zg[:, 0:W].rearrange("p (h b s d) -> p h b s d", b=2, s=2, d=16)
    t4 = L.t2[:, 0:W].rearrange("p (h b s d) -> p h b s d", b=2, s=2, d=16)
    s4 = rt[:, 64:128].rearrange("p (b s d) -> p b s d", b=2, s=2, d=16)
    for so, si in ((0, 1), (1, 0)):
        fw.op('pool', lambda: nc.gpsimd.tensor_tensor(out=t4[:, :, :, so, :], in0=z4[:, :, :, si, :],
                                                      in1=s4[:, :, so, :].unsqueeze(1).broadcast_to([128, H, 2, 16]), op=ALU.mult),
              r=['zg', rk], w=[('t2', so)])
    fw.op('dve', lambda: V.tensor_tensor(out=out_bf, in0=L.t1[:, 0:W], in1=L.t2[:, 0:W], op=ALU.add),
          r=['t1', ('t2', 0), ('t2', 1)], w=okeys)


def hkeys(name):
    return [(name, c) for c in range(8)]


def l0_kv(C, L, n, kcol0, kt0, kkey, ropes, out_row0):
    nc, fw, D = C.nc, C.fw, C.D
    V, A, PE, G = nc.vector, nc.scalar, nc.tensor, nc.gpsimd
    akeys = [('A', 0)] + ([('A', 1)] if n > 2 else [])
    for i in range(n):
        mm_acc(C, C.psA[:, i * 256:(i + 1) * 256], ('A', i // 2),
               [(L.hT[:, c, i * 128:(i + 1) * 128], L.w0[:, c, 512:768]) for c in range(8)], hkeys('hT') + ['w0'])
    W = n * 128
    pk = C.psA[:, 0:n * 256].rearrange("p (i s h d) -> p i s h d", s=2, h=2, d=64)
    zk = pk[:, :, 0, :, :]
    v4 = lambda t: t[:, 0:W].rearrange("p (i h d) -> p i h d", h=2, d=64)
    fw.op('act', lambda: A.activation(out=v4(L.sq), in_=zk, func=AF.Square), r=akeys, w=['sq'])
    ks, ssq = stat_cols(C, 2 * n)
    fw.op('dve', lambda: V.tensor_reduce(out=ssq, in_=L.sq[:, 0:W].rearrange("p (g d) -> p g d", d=64), axis=AX.X, op=ALU.add), r=['sq'], w=ks)
    kr, rs = rstd_col(C, ks, ssq, 2 * n, 64.0)
    fw.op('dve', lambda: V.tensor_tensor(out=v4(L.zg), in0=zk, in1=rs.rearrange("p (i h) -> p i h", h=2).unsqueeze(3).broadcast_to([128, n, 2, 64]),
                                         op=ALU.mult), r=akeys + kr, w=['zg'])
    z3 = L.zg[:, 0:W].rearrange("p (i c) -> p i c", c=128)
    fw.op('dve', lambda: V.tensor_tensor(out=z3, in0=z3, in1=L.gainA[:, 512:640].unsqueeze(1).broadcast_to([128, n, 128]), op=ALU.mult),
          r=['zg', 'gainA'], w=['zg'])
    if ropes is None:
        fw.op('act', lambda: A.copy(out=L.zb[:, 0:W], in_=L.zg[:, 0:W]), r=['zg'], w=['zb'])
    else:
        rk = ropes[0][0]
        rt = ropes[0][2]
        fw.op('dve', lambda: V.tensor_tensor(out=v4(L.t1), in0=v4(L.zg), in1=rt[:, 0:n, 0:64].unsqueeze(2).broadcast_to([128, n, 2, 64]),
                                             op=ALU.mult), r=['zg', rk], w=['t1'])
        for i in range(n):
            z4 = L.zg[:, i * 128:(i + 1) * 128].rearrange("p (h b s d) -> p h b s d", b=2, s=2, d=16)
            t4 = L.t2[:, i * 128:(i + 1) * 128].rearrange("p (h b s d) -> p h b s d", b=2, s=2, d=16)
            s4 = rt[:, i, 64:128].rearrange("p (b s d) -> p b s d", b=2, s=2, d=16)
            for so, si in ((0, 1), (1, 0)):
                fw.op('pool', lambda: G.tensor_tensor(out=t4[:, :, :, so, :], in0=z4[:, :, :, si, :],
                                                      in1=s4[:, :, so, :].unsqueeze(1).broadcast_to([128, 2, 2, 16]), op=ALU.mult),
                      r=['zg', rk], w=[('t2', so)])
        fw.op('dve', lambda: V.tensor_tensor(out=L.zb[:, 0:W], in0=L.t1[:, 0:W], in1=L.t2[:, 0:W], op=ALU.add),
              r=['t1', ('t2', 0), ('t2', 1)], w=['zb'])
    if out_row0 is not None:
        for i in range(n):
            fw.dma('sp', D['nak'][out_row0 + i * 128:out_row0 + (i + 1) * 128, :], L.zg[:, i * 128:(i + 1) * 128], r=['zg'])
        fw.op('act', lambda: A.copy(out=L.vt32[:, 0:W].rearrange("p (i c) -> p i c", c=128), in_=pk[:, :, 1, :, :].rearrange("p i h d -> p i (h d)")),
              r=akeys, w=['vt32'])
        for i in range(n):
            fw.dma('sp', D['nav'][out_row0 + i * 128:out_row0 + (i + 1) * 128, :], L.vt32[:, i * 128:(i + 1) * 128], r=['vt32'])
    for i in range(n):
        f = lambda: PE.transpose(out=C.psT[:, i * 128:(i + 1) * 128], in_=L.zb[:, i * 128:(i + 1) * 128], identity=C.ident_b[:])
        if i < n - 1:
            fw.op_noinc('pe', f, r=['zb', 'ident_b'], w=['psT'])
        else:
            fw.op('pe', f, r=['zb', 'ident_b'], w=['psT'])
    fw.op('act', lambda: A.copy(out=L.kT[:, kcol0:kcol0 + W], in_=C.psT[:, 0:W]), r=['psT'], w=[kkey])
    fw.op('dve', lambda: V.tensor_copy(out=L.Vt[:, kt0:kt0 + n, 0:64], in_=pk[:, :, 1, 0, :]), r=akeys, w=[kkey])
    fw.op('dve', lambda: V.tensor_copy(out=L.Vt[:, kt0:kt0 + n, 128:192], in_=pk[:, :, 1, 1, :]), r=akeys, w=[kkey])


def l0_q(C, L, n, ropes):
    nc, fw = C.nc, C.fw
    V, PE = nc.vector, nc.tensor
    for i in range(n):
        ka, pa = next_A(C)
        mm_acc(C, pa, ka, [(L.hT[:, c, i * 128:(i + 1) * 128], L.w0[:, c, 0:512]) for c in range(8)], hkeys('hT') + ['w0'])
        normrope(C, L, pa, ka, 8, L.gainA[:, 0:512], None if ropes is None else ropes[i], L.zb[:, 0:512], ['zb'])
        for j in range(4):
            f = lambda: PE.transpose(out=C.psT[:, j * 128:(j + 1) * 128], in_=L.zb[:, j * 128:(j + 1) * 128], identity=C.ident_b[:])
            if j < 3:
                fw.op_noinc('pe', f, r=['zb', 'ident_b'], w=['psT'])
            else:
                fw.op('pe', f, r=['zb', 'ident_b'], w=['psT'])
        fw.op('dve', lambda: V.tensor_copy(out=L.qT[:, :, i * 128:(i + 1) * 128], in_=C.psT[:, 0:512].rearrange("p (j t) -> p j t", j=4)),
              r=['psT'], w=['qT'])


def fm_proj(C, w, wkey, col0, nch, hT, hname, ntok, evac):
    for j in range(nch):
        ka, pa = next_A(C)
        mm_acc(C, pa[:, 0:ntok], ka, [(w[:, c, col0 + j * 128:col0 + (j + 1) * 128], hT[:, c, 0:ntok]) for c in range(8)],
               hkeys(hname) + [wkey])
        evac(j, ka, pa[:, 0:ntok])


def pool_b_steps(C, L, off, ntok, prc):
    nc, fw = C.nc, C.fw
    V, G, PE = nc.vector, nc.gpsimd, nc.tensor
    add = lambda o, a, b, r, w: fw.op('pool', lambda: G.tensor_tensor(out=o, in0=a, in1=b, op=ALU.add), r=r, w=w)
    a, b, s = L.t1, L.t2, L.sq

    def chain(g):
        u = L.ubT[:, g, :]
        w_ = POOL_W[g]
        if w_ == 2:
            add(s[:, 0:ntok], u[:, off - 1:off - 1 + ntok], u[:, off:off + ntok], ['ubT'], ['sq'])
        else:
            add(a[:, 0:ntok + 15], u[:, off - 8:off + ntok + 7], u[:, off - 7:off + ntok + 8], ['ubT'], ['t1'])
            if w_ == 4:
                add(s[:, 0:ntok], a[:, 6:6 + ntok], a[:, 8:8 + ntok], ['t1'], ['sq'])
            else:
                add(b[:, 0:ntok + 13], a[:, 0:ntok + 13], a[:, 2:ntok + 15], ['t1'], [('t2', 0), ('t2', 1)])
                if w_ == 8:
                    add(s[:, 0:ntok], b[:, 4:4 + ntok], b[:, 8:8 + ntok], [('t2', 0), ('t2', 1)], ['sq'])
                else:
                    add(a[:, 0:ntok + 9], b[:, 0:ntok + 9], b[:, 4:ntok + 13], [('t2', 0), ('t2', 1)], ['t1'])
                    add(s[:, 0:ntok], a[:, 0:ntok], a[:, 8:8 + ntok], ['t1'], ['sq'])
        fw.op('dve', lambda: V.tensor_tensor(out=s[:, 0:ntok], in0=s[:, 0:ntok], in1=prc[:, g, 0:ntok], op=ALU.mult), r=['sq', 'prc'], w=['sq'])
        fw.op('dve', lambda: V.tensor_tensor(out=L.zb[:, 0:ntok], in0=s[:, 0:ntok], in1=u[:, off:off + ntok], op=ALU.subtract),
              r=['sq', 'ubT'], w=['zb'])

    def mm(g):
        ka, pa = next_A(C)
        fw.op('pe', lambda: PE.matmul(pa[:, 0:ntok], lhsT=L.bm[:, g * 128:(g + 1) * 128], rhs=L.zb[:, 0:ntok], start=True, stop=True),
              r=['bm', 'zb'], w=[ka])
        fw.op('dve', lambda: V.scalar_tensor_tensor(out=L.AO[:, 4 + g, 0:ntok], in0=pa[:, 0:ntok], scalar=L.bsc[:, g:g + 1],
                                                    in1=L.gbT[:, g, 0:ntok], op0=ALU.mult, op1=ALU.mult),
              r=[ka, 'bsc', 'gbT'], w=[('AO', 4 + g)])

    return [lambda: chain(0)] + [(lambda g=g: (mm(g), chain(g + 1))) for g in range(3)] + [lambda: mm(3)]


def l0_attn(C, L, nq, ktlist, kkey, extra=None):
    calls = []
    for j in range(4):
        for side in (0, 1):
            rows = slice(side * 64, (side + 1) * 64)
            kts = [(L.kT[:, kt * 128:(kt + 1) * 128], L.Vt[:, kt, side * 64:side * 64 + 128], None, [kkey]) for kt in ktlist]
            calls.append(dict(q64=L.qT[rows, j, 0:nq], qkeys=['qT'], nq=nq, kts=kts, scale=0.125, side=side,
                              dst=L.AO[rows, j, 0:nq], gate=L.gaT[rows, j, 0:nq], dkeys=[('AO', j)], gkeys=['gaT']))
    run_attn_calls(C, calls, extra=extra)


def phase_l0(C, stop_after):
    nc, fw, D = C.nc, C.fw, C.D
    V, A, PE, G = nc.vector, nc.scalar, nc.tensor, nc.gpsimd
    L = Ctx()
    with ExitStack() as s1:
        L.w0 = sbt(nc, s1, 'w0', [128, 8, 2304], BF); L.wo = sbt(nc, s1, 'wo0', [128, 8, 1024], BF)
        phase_setup(C, (0,), after_issue=lambda: (load_w_cast(C, L.w0, D['w_in_e'], 8, 2304, 'w0'),
                                                   load_w_cast(C, L.wo, D['w_out_e'], 8, 1024, 'wo')))
        L.bm = sbt(nc, s1, 'bm', [128, 512], BF); L.bsc = sbt(nc, s1, 'bsc', [128, 4], F32)
        L.gainA = sbt(nc, s1, 'gainA', [128, 640], F32); L.hmask = sbt(nc, s1, 'hmask', [128, 2], F32)
        L.kT = sbt(nc, s1, 'kT', [128, 4608], BF); L.Vt = sbt(nc, s1, 'Vt', [128, 36, 192], BF)
        L.ubT = sbt(nc, s1, 'ubT', [128, 4, 2064], BF); L.hT = sbt(nc, s1, 'hT', [128, 8, 512], BF)
        L.qT = sbt(nc, s1, 'qT', [128, 4, 512], BF); L.gaT = sbt(nc, s1, 'gaT', [128, 4, 512], BF)
        L.gbT = sbt(nc, s1, 'gbT', [128, 4, 512], BF); L.AO = sbt(nc, s1, 'AO', [128, 8, 512], BF)
        L.sq = sbt(nc, s1, 'sq', [128, 640], F32); L.zg = sbt(nc, s1, 'zg', [128, 640], F32)
        L.t1 = sbt(nc, s1, 't1', [128, 640], F32); L.t2 = sbt(nc, s1, 't2', [128, 640], F32)
        L.zb = sbt(nc, s1, 'zb', [128, 640], BF); L.vt32 = sbt(nc, s1, 'vt32', [128, 256], F32)
        L.ropet = sbt(nc, s1, 'ropet', [128, 2, 4, 128], F32); L.prc = sbt(nc, s1, 'prc', [128, 4, 512], F32)
        print('sbuf remaining (l0):', nc.sbuf_bytes_remaining)
        fw.dma('pool', L.bm[:], D['b_mapT'], w=['bm'])
        fw.dma('sp', L.bsc[:], D['b_scaleT'], w=['bsc'])
        fw.dma('sp', L.gainA[:], D['gain_a'], w=['gainA'])
        fw.dma('sp', L.hmask[:], D['hmask'], w=['hmask'])
        fw.op('dve', lambda: V.memset(L.Vt[:, :, 64:128], 1.0), w=[('kv', 'all')])
        fw.dma('pool', L.kT[:, 0:512], D['cak_T'], w=[('kv', 'all')])
        cav = D['cav'].rearrange("(t p) d -> p t d", p=128)
        fw.dma('pool', L.Vt[:, 0:4, 0:64], cav[:, :, 0:64], w=[('kv', 'all')])
        fw.dma('pool', L.Vt[:, 0:4, 128:192], cav[:, :, 64:128], w=[('kv', 'all')])
        rope_src = D['rope_a'].rearrange("(g t p) d -> g p t d", t=4, p=128)
        rb = 0

        def load_rope(grp):
            nonlocal rb
            rb = 1 - rb
            fw.dma('sp', L.ropet[:, rb, :, :], rope_src[grp], w=[('rope', rb)])
            return [(('rope', rb), L.ropet[:, rb, i, :], L.ropet[:, rb, :, :]) for i in range(4)]

        def pass1_gen(Lg, grp):
            xts = [load_x(C, D['xs'][grp * 512 + i * 128:grp * 512 + (i + 1) * 128, :]) for i in range(4)]
            prenorm(C, 0, 0, xts, Lg.hT, 'hT')
            yield
            ropes = load_rope(grp)
            l0_kv(C, Lg, 4, 512 + grp * 512, 4 + grp * 4, ('kv', 'all'), ropes, None)
            yield
            if grp < 4:
                fm_proj(C, L.w0, 'w0', 1280, 4, Lg.hT, 'hT', 512,
                        lambda j, ka, pa: fw.op('act', lambda: A.copy(out=L.ubT[:, j, 8 + grp * 512:8 + (grp + 1) * 512], in_=pa),
                                                r=[ka], w=['ubT']))
            elif grp == 4:
                def ev(j, ka, pa):
                    fw.op('dve', lambda: V.tensor_scalar(out=L.ubT[:, j, 0:8], in0=pa[:, 120:128], scalar1=L.hmask[:, 0:1], scalar2=None,
                                                         op0=ALU.mult), r=[ka, 'hmask'], w=['ubT'])
                    fw.op('dve', lambda: V.tensor_scalar(out=L.ubT[:, j, 2056:2064], in0=pa[:, 0:8], scalar1=L.hmask[:, 1:2], scalar2=None,
                                                         op0=ALU.mult), r=[ka, 'hmask'], w=['ubT'])
                fm_proj(C, L.w0, 'w0', 1280, 4, Lg.hT, 'hT', 128, ev)

        Lalt = Ctx()
        Lalt.__dict__.update(L.__dict__)
        Lalt.hT = L.AO
        fw.nskeys = {'hT'}
        for gp in range(4):
            alive = [(b_, pass1_gen(L if b_ == 0 else Lalt, 2 * gp + b_)) for b_ in range(2)]
            while alive:
                for item in list(alive):
                    fw.ns = 'g%d' % item[0]
                    try:
                        next(item[1])
                    except StopIteration:
                        alive.remove(item)
        fw.ns = None
        fw.nskeys = set()
        fw.barrier()
        prs = D['prc_s'].rearrange("p (g t) -> p g t", g=4)
        for t in range(4):
            xts = [load_x(C, D['xs'][t * 512 + i * 128:t * 512 + (i + 1) * 128, :]) for i in range(4)]
            prenorm(C, 0, 0, xts, L.hT, 'hT')
            ropes = load_rope(t)
            l0_q(C, L, 4, ropes)
            fm_proj(C, L.w0, 'w0', 768, 4, L.hT, 'hT', 512,
                    lambda j, ka, pa: fw.op('act', lambda: A.activation(out=L.gaT[:, j, :], in_=pa, func=AF.Silu), r=[ka], w=['gaT']))
            fm_proj(C, L.w0, 'w0', 1792, 4, L.hT, 'hT', 512,
                    lambda j, ka, pa: fw.op('act', lambda: A.activation(out=L.gbT[:, j, :], in_=pa, func=AF.Silu), r=[ka], w=['gbT']))
            fw.dma('sp', L.prc[:], prs[:, :, t * 512:(t + 1) * 512], w=['prc'])
            l0_attn(C, L, 512, list(range(36)), ('kv', 'all'), extra=pool_b_steps(C, L, 8 + t * 512, 512, L.prc))
            outproj_residual(C, 0, 0, L.AO, [('AO', c) for c in range(8)], L.wo, 4, xts,
                             lambda i: D['x1s'][t * 512 + i * 128:t * 512 + (i + 1) * 128, :])
        fw.barrier()
        for o_ in (0, 264, 512, 776):
            fw.op('dve', lambda: V.memset(L.ubT[:, :, o_:o_ + 8], 0.0), w=['ubT'])
        fw.dma('sp', L.prc[:, :, 0:256], D['prc_p'].rearrange("p (g t) -> p g t", g=4), w=['prc'])
        fw.barrier()
        fw.nskeys = {'hT', 'qT', 'gaT', 'gbT', 'AO', 'ubT'}

        def half_ctx(b):
            Lb = Ctx()
            Lb.__dict__.update(L.__dict__)
            for nm in ('hT', 'qT', 'gaT', 'gbT', 'AO'):
                setattr(Lb, nm, getattr(L, nm)[:, :, b * 256:(b + 1) * 256])
            Lb.ubT = L.ubT[:, :, b * 512:b * 512 + 272]
            return Lb

        def prompt_gen(Lb, s):
            xts = [load_x(C, D['xp'][s * 256 + i * 128:s * 256 + (i + 1) * 128, :]) for i in range(2)]
            prenorm(C, 0, 1, xts, Lb.hT, 'hT')
            yield
            reg = s % 2
            l0_kv(C, Lb, 2, reg * 256, reg * 2, ('kv', reg), None, s * 256)
            yield
            fm_proj(C, L.w0, 'w0', 1280, 4, Lb.hT, 'hT', 256,
                    lambda j, ka, pa: fw.op('act', lambda: A.copy(out=Lb.ubT[:, j, 8:264], in_=pa), r=[ka], w=['ubT']))
            yield
            l0_q(C, Lb, 2, None)
            yield
            fm_proj(C, L.w0, 'w0', 768, 4, Lb.hT, 'hT', 256,
                    lambda j, ka, pa: fw.op('act', lambda: A.activation(out=Lb.gaT[:, j, 0:256], in_=pa, func=AF.Silu), r=[ka], w=['gaT']))
            yield
            fm_proj(C, L.w0, 'w0', 1792, 4, Lb.hT, 'hT', 256,
                    lambda j, ka, pa: fw.op('act', lambda: A.activation(out=Lb.gbT[:, j, 0:256], in_=pa, func=AF.Silu), r=[ka], w=['gbT']))
            yield
            l0_attn(C, Lb, 256, [reg * 2, reg * 2 + 1], ('kv', reg), extra=pool_b_steps(C, Lb, 8, 256, L.prc))
            yield
            outproj_residual(C, 0, 1, Lb.AO, [('AO', c) for c in range(8)], L.wo, 2, xts,
                             lambda i: D['x1p'][s * 256 + i * 128:s * 256 + (i + 1) * 128, :])

        for pair in range(2):
            alive = [(b, prompt_gen(half_ctx(b), 2 * pair + b)) for b in range(2)]
            while alive:
                for item in list(alive):
                    fw.ns = 'p%d' % item[0]
                    try:
                        next(item[1])
                    except StopIteration:
                        alive.remove(item)
            fw.ns = None
        fw.nskeys = set()
        fw.barrier()


def _rope_tables(tok, half_dims):
    R = half_dims * 2
    half = R // 2
    inv = (10000.0 ** (-np.arange(0, half, 2, dtype=np.float32) / np.float32(half))).astype(np.float32)
    row = (tok // 64).astype(np.float32)[:, None] * inv[None, :]
    col = (tok % 64).astype(np.float32)[:, None] * inv[None, :]
    cr, sr, cc, sc = np.cos(row), np.sin(row), np.cos(col), np.sin(col)
    cos = np.concatenate([cr, cr, cc, cc], axis=1).astype(np.float32)
    sin = np.concatenate([-sr, sr, -sc, sc], axis=1).astype(np.float32)
    return cos, sin


def _pool_rc(tok, S):
    out = []
    for w in POOL_W:
        lo = np.clip(tok - w // 2, 0, S)
        hi = np.clip(tok + w - w // 2, 0, S)
        out.append((1.0 / (hi - lo).astype(np.float32)).astype(np.float32))
    return np.concatenate(out)


def _rep(v, n=128):
    return np.ascontiguousarray(np.broadcast_to(np.asarray(v, np.float32)[None, :], (n, len(v))))


def _na_bias_tables(rpb, half):
    H = 8
    kc = np.arange(64)
    qc = np.arange(64)
    c0 = np.clip(qc - 8, 0, 48)
    colvalid = (kc[:, None] >= c0[None, :]) & (kc[:, None] < c0[None, :] + 16)
    dc = kc[:, None] - qc[None, :] + 15
    dcc = np.clip(dc, 0, 30)
    out = np.full((H, 2, 64, 4480), NEG, np.float32)

    def fill(dst, delta_ok, dr):
        if not delta_ok:
            return
        vals = rpb[:, dr, :][:, dcc]
        dst[...] = np.where(colvalid[None], vals, NEG)

    for kr2 in range(2):
        for e in range(22):
            delta = kr2 + 10 - e
            fill(out[:, kr2, :, e * 64:(e + 1) * 64], -4 <= delta <= 3, delta + 7 if -4 <= delta <= 3 else 0)
    base = 32 * half

    def exact(dst, qpos, kpos):
        qr = base + qpos
        kr = base + kpos
        if kr < 0 or kr > 63:
            return
        r0 = min(max(qr - 4, 0), 56)
        ok = r0 <= kr < r0 + 8
        fill(dst, ok, kr - qr + 7 if ok else 0)

    for m in range(6):
        for kr2 in range(2):
            for q in range(4):
                exact(out[:, kr2, :, 1408 + m * 256 + q * 64:1408 + m * 256 + (q + 1) * 64], q, -4 + 2 * m + kr2)
    for mi, m in enumerate(range(2, 8)):
        for kr2 in range(2):
            for q in range(4):
                exact(out[:, kr2, :, 2944 + mi * 256 + q * 64:2944 + mi * 256 + (q + 1) * 64], 28 + q, 20 + 2 * m + kr2)
    return np.ascontiguousarray(out.reshape(H, 128, 4480))


def prepare_inputs(inp):
    f = lambda a: np.ascontiguousarray(np.asarray(a), dtype=np.float32)
    xp_, xs_ = f(inp['x_prompt']), f(inp['x_sample'])
    c, c_ctx = f(inp['c']), f(inp['c_ctx'])
    g_pre, g_post = f(inp['g_pre']), f(inp['g_post'])
    com = {}
    com['w_mod'] = f(inp['w_mod']); com['b_mod'] = f(inp['b_mod']); com['g_post'] = g_post
    gpT = np.zeros((128, 32), np.float32)
    for l in range(2):
        for ch in range(8):
            for g in range(2):
                gpT[:, l * 16 + ch * 2 + g] = g_pre[l, ch * 128:(ch + 1) * 128]
    com['g_preT'] = gpT
    sel = np.zeros((2, 256), np.float32); sel[0, 0:128] = 1; sel[1, 128:256] = 1
    com['sel'] = sel
    We = f(inp['w_in_e'])[0]
    com['w_in_e'] = np.ascontiguousarray(np.concatenate(
        [We[:, 0:512][:, PERM_A], We[:, 512:768], We[:, 768:1280][:, PERM_A], We[:, 1280:2304]], axis=1))
    com['gain_a'] = _rep(np.concatenate([np.tile(f(inp['a_q_norm'])[0], 8), np.tile(f(inp['a_k_norm'])[0], 2)]))
    bmap = f(inp['b_map'])[0]
    com['b_mapT'] = np.ascontiguousarray(bmap.transpose(1, 0, 2).reshape(128, 512))
    com['b_scaleT'] = np.ascontiguousarray(f(inp['b_scale'])[0].reshape(4, 128).T)
    Woe = f(inp['w_out_e'])[0]
    com['w_out_e'] = np.ascontiguousarray(np.concatenate([Woe[0:512][PERM_A], Woe[512:1024]], axis=0))
    tokp = np.arange(256)
    com['prc_p'] = _rep(_pool_rc(tokp, 256))
    Wo = f(inp['w_in_o'])[0]
    com['w1_fm'] = np.ascontiguousarray(np.concatenate([Wo[:, 0:512], Wo[:, 512:1024], Wo[:, 1536:2048], Wo[:, 2720:3232]], axis=1))
    com['w1_tm'] = np.ascontiguousarray(np.concatenate([Wo[:, 512:1024], Wo[:, 1024:1536], Wo[:, 2048:2432], Wo[:, 2432:2688],
                                                        Wo[:, 2688:2720]], axis=1))
    kpe_w = Wo[:, 2688:2720]
    swp = np.r_[8:16, 0:8, 24:32, 16:24]
    z64 = np.zeros((1024, 64), np.float32)
    com['w_kpe2'] = np.ascontiguousarray(np.concatenate([z64, kpe_w, z64, kpe_w[:, swp]], axis=1))
    com['gain_q'] = _rep(f(inp['d_q_norm'])[0]); com['gain_kv'] = _rep(f(inp['d_kv_norm'])[0])
    Wuq = f(inp['d_w_uq'])[0].reshape(384, 8, 96)
    Wuq_b = np.concatenate([Wuq[:, :, 0:64], Wuq[:, :, 64:96][:, :, swp]], axis=2)
    com['w_uq_ab'] = np.ascontiguousarray(np.concatenate([Wuq.reshape(384, 768), Wuq_b.reshape(384, 768)], axis=1))
    Wukv = f(inp['d_w_ukv'])[0].reshape(256, 8, 128)
    com['w_uk'] = np.ascontiguousarray(Wukv[:, :, 0:64].reshape(256, 512))
    com['w_uv'] = np.ascontiguousarray(Wukv[:, :, 64:128].reshape(256, 512))
    com['w_out_o'] = f(inp['w_out_o'])[0]
    rpb = f(inp['c_rpb'])[0]
    na = [_na_bias_tables(rpb, 0), _na_bias_tables(rpb, 1)]
    maps = []
    for r in range(8):
        b, half = r // 2, r % 2
        m = dict(com)
        own = np.arange(half * 2048, (half + 1) * 2048)
        other = np.arange(2048, 4096) if half == 0 else np.concatenate([np.arange(1920, 2048), np.arange(0, 1920)])
        tok = np.concatenate([own, other])
        m['xs'] = np.ascontiguousarray(xs_[b][tok])
        m['xp'] = np.ascontiguousarray(xp_[4 * r:4 * r + 4].reshape(1024, 1024))
        cond = np.stack([c[b], c_ctx], axis=0)
        m['condT'] = np.ascontiguousarray(cond.reshape(2, 8, 128).transpose(2, 1, 0).reshape(128, 16))
        cos, sin = _rope_tables(tok, 32)
        m['rope_a'] = np.ascontiguousarray(np.concatenate([cos, sin], axis=1))
        m['prc_s'] = _rep(_pool_rc(own, 4096))
        m['hmask'] = _rep(np.array([0.0, 1.0] if half == 0 else [1.0, 0.0], np.float32))
        m['cak_T'] = np.ascontiguousarray(f(inp['cache_a_k'])[b, 0].reshape(512, 128).T)
        m['cav'] = np.ascontiguousarray(f(inp['cache_a_v'])[b, 0].reshape(512, 128))
        cd, sd = _rope_tables(own, 16)
        rd = np.zeros((128, 2, 2048), np.float32)
        rd[64:96, 0, :] = cd.T; rd[64:96, 1, :] = sd.T
        m['rope_d'] = np.ascontiguousarray(rd.reshape(128, 4096))
        m['cck_T'] = np.ascontiguousarray(f(inp['cache_c_k'])[b, 0].reshape(512, 512).T)
        m['ccv'] = np.ascontiguousarray(f(inp['cache_c_v'])[b, 0].reshape(512, 512))
        m['cckv_T'] = np.ascontiguousarray(f(inp['cache_d_ckv'])[b, 0].T)
        kp = np.zeros((128, 512), np.float32); kp[64:96] = f(inp['cache_d_kpe'])[b, 0].T
        m['ckpe_T'] = kp
        m['nabias'] = na[half]
        maps.append(m)
    return maps


_NC_CACHE = {}


def kernel(**inputs):
    maps = prepare_inputs(inputs)
    if 'nc' not in _NC_CACHE:
        _NC_CACHE['nc'] = build_program()
    res = run_bass_kernel_spmd(_NC_CACHE['nc'], maps, core_ids=list(range(8)))
    R = res.results
    y_p = np.stack([R[r]['y_p'].reshape(4, 256, 1024) for r in range(8)]).reshape(32, 256, 1024)
    y_s = np.stack([R[r]['y_s'] for r in range(8)]).reshape(4, 4096, 1024)
    cat = lambda k, shp: np.ascontiguousarray(np.stack([R[r][k] for r in range(8)]).reshape(shp)).astype(np.float32)
    return (y_p.astype(np.float32), y_s.astype(np.float32),
            cat('nak', (32, 1, 256, 2, 64)), cat('nav', (32, 1, 256, 2, 64)),
            cat('nck', (32, 1, 256, 8, 64)), cat('ncv', (32, 1, 256, 8, 64)),
            cat('nckv', (32, 1, 256, 256)), cat('nkpe', (32, 1, 256, 32)))


def rms_tok(C, ps_ap, pkey, W, gain, gkey, out_ap, okeys):
    nc, fw = C.nc, C.fw
    ks, ss = stat_col(C)
    fw.op('act', lambda: nc.scalar.activation(out=C.junk[:, 0:W], in_=ps_ap, func=AF.Square, accum_out=ss), r=[pkey], w=[ks])
    kr, rs = rstd_col(C, [ks], ss, 1, float(W))
    fw.op('dve', lambda: nc.vector.scalar_tensor_tensor(out=out_ap, in0=ps_ap, scalar=rs, in1=gain, op0=ALU.mult, op1=ALU.mult),
          r=[pkey, gkey] + kr, w=okeys)


def transpose_to(C, src_bf, skey, nblk, dst_fn, dkeys):
    fw, PE, V = C.fw, C.nc.tensor, C.nc.vector
    for k in range(nblk):
        f = lambda: PE.transpose(out=C.psT[:, k * 128:(k + 1) * 128], in_=src_bf[:, k * 128:(k + 1) * 128], identity=C.ident_b[:])
        if k < nblk - 1:
            fw.op_noinc('pe', f, r=[skey, 'ident_b'], w=['psT'])
        else:
            fw.op('pe', f, r=[skey, 'ident_b'], w=['psT'])
    fw.op('dve', lambda: V.tensor_copy(out=dst_fn(), in_=C.psT[:, 0:nblk * 128].rearrange("p (k t) -> p k t", k=nblk)),
          r=['psT'], w=dkeys)


def vpair_copy(C, dst_tile_ap, src_ps, skey, dkeys, npair):
    fw, V = C.fw, C.nc.vector
    d4 = dst_tile_ap.rearrange("p (j a d) -> p j a d", j=npair, a=3, d=64)
    s4 = src_ps.rearrange("p (j a d) -> p j a d", j=npair, a=2, d=64)
    fw.op('dve', lambda: V.tensor_copy(out=d4[:, :, 0, :], in_=s4[:, :, 0, :]), r=[skey], w=dkeys)
    fw.op('dve', lambda: V.tensor_copy(out=d4[:, :, 2, :], in_=s4[:, :, 1, :]), r=[skey], w=dkeys)


def phase_l1(C, stop_after):
    nc, fw, D = C.nc, C.fw, C.D
    V, A, PE, G = nc.vector, nc.scalar, nc.tensor, nc.gpsimd
    L = Ctx()
    xinB_flat = D['xinB'].rearrange("r c -> (r c)")
    KCv = xinB_flat[OFF_KC:OFF_KC + 512 * 512].rearrange("(j p t) -> p j t", j=4, p=128, t=512)
    VCv = xinB_flat[OFF_VC:OFF_VC + 512 * 512].rearrange("(n p c) -> p n c", n=4, p=128, c=512)
    xout_flat = D['xout'].rearrange("r c -> (r c)")
    RK = XIN_ROWS * 2048
    with ExitStack() as sa:
        L.w1 = sbt(nc, sa, 'w1', [128, 8, 2048], BF); L.w1t = sbt(nc, sa, 'w1t', [128, 8, 1696], BF)
        L.wk2 = sbt(nc, sa, 'wk2', [128, 8, 192], BF); L.gq = sbt(nc, sa, 'gq', [128, 384], F32); L.gkv = sbt(nc, sa, 'gkv', [128, 256], F32)
        L.hT = sbt(nc, sa, 'hT1', [128, 8, 512], BF)
        L.cqb = sbt(nc, sa, 'cqb', [128, 384], BF); L.ckvb = sbt(nc, sa, 'ckvb', [128, 256], BF)
        sa1 = sa.enter_context(ExitStack())
        L.stg = sbt(nc, sa1, 'stg', [128, 2, 4, 512], BF); L.stgv = sbt(nc, sa1, 'stgv', [128, 2, 512], BF)
        L.stgq = sbt(nc, sa1, 'stgq', [128, 3, 512], BF); L.stgc = sbt(nc, sa1, 'stgc', [128, 2, 512], BF)
        L.stgk = sbt(nc, sa1, 'stgk', [128, 512], BF); L.rd = sbt(nc, sa1, 'rd', [128, 2, 512], F32)
        L.r1 = sbt(nc, sa1, 'r1', [128, 512], F32); L.r2 = sbt(nc, sa1, 'r2', [128, 512], F32)
        L.f32a = C.ty; f32b = C.rcb[:, 0, 0:288]
        print('sbuf remaining (l1 phase A):', nc.sbuf_bytes_remaining)
        fw.dma('sp', L.gq[:], D['gain_q'], w=['gq'])
        fw.dma('sp', L.gkv[:], D['gain_kv'], w=['gkv'])
        phase_setup(C, (1,), stack=sa1, after_issue=lambda: (load_w_cast(C, L.w1, D['w1_fm'], 8, 2048, 'w1'),
                                                            load_w_cast(C, L.w1t, D['w1_tm'], 8, 1696, 'w1t'),
                                                            load_w_cast(C, L.wk2, D['w_kpe2'], 8, 192, 'wk2')))
        rdv = D['rope_d'].rearrange("p (a t) -> p a t", a=2)
        sg = 0
        for t in range(4):
            tc0, tc1 = t * 512, (t + 1) * 512
            xts = [load_x(C, D['x1s'][tc0 + i * 128:tc0 + (i + 1) * 128, :]) for i in range(4)]
            prenorm(C, 1, 0, xts, L.hT, 'hT')
            fw.dma('sp', L.rd[:], rdv[:, :, tc0:tc1], w=['rd'])
            for grp, (dname, silu) in enumerate((('s_qc', False), ('s_kc', False), ('s_gc', True), ('s_gd', True))):
                sg = 1 - sg
                sgk = ('stg', sg)

                def ev(j, ka, pa, sg=sg, silu=silu, sgk=sgk):
                    if silu:
                        fw.op('act', lambda: A.activation(out=L.stg[:, sg, j, :], in_=pa, func=AF.Silu), r=[ka], w=[sgk])
                    else:
                        fw.op('act', lambda: A.copy(out=L.stg[:, sg, j, :], in_=pa), r=[ka], w=[sgk])
                fm_proj(C, L.w1, 'w1', grp * 512, 4, L.hT, 'hT', 512, ev)
                fw.dma('sp', D[dname].rearrange("(j p) t -> p j t", p=128)[:, :, tc0:tc1], L.stg[:, sg, :, :], r=[sgk], w=[dname])
                if dname == 's_kc' and t == 0:
                    fw.dma('sp', KCv[:, :, 0:256], L.stg[:, sg, :, 0:256], r=[sgk], w=['xinB'])
                if dname == 's_kc' and t == 3:
                    fw.dma('sp', KCv[:, :, 256:512], L.stg[:, sg, :, 256:512], r=[sgk], w=['xinB'])
            k0, p0 = next_A(C)
            mm_acc(C, p0[0:96, :], k0, [(L.wk2[:, c, 0:96], L.hT[:, c, :]) for c in range(8)], hkeys('hT') + ['wk2'])
            k1, p1 = next_A(C)
            mm_acc(C, p1[0:96, :], k1, [(L.wk2[:, c, 96:192], L.hT[:, c, :]) for c in range(8)], hkeys('hT') + ['wk2'])
            fw.op('dve', lambda: V.tensor_tensor(out=L.r1[64:96, :], in0=p0[64:96, :], in1=L.rd[64:96, 0, :], op=ALU.mult), r=[k0, 'rd'], w=['r1'])
            fw.op('dve', lambda: V.tensor_tensor(out=L.r2[64:96, :], in0=p1[64:96, :], in1=L.rd[64:96, 1, :], op=ALU.mult), r=[k1, 'rd'], w=['r2'])
            fw.op('dve', lambda: V.tensor_tensor(out=L.stgk[64:96, :], in0=L.r1[64:96, :], in1=L.r2[64:96, :], op=ALU.add), r=['r1', 'r2'], w=['stgk'])
            fw.dma('sp', D['xin'][256:288, tc0:tc1], L.stgk[64:96, :], r=['stgk'], w=['xin'])
            for i in range(4):
                sv = i % 2
                ka, pa = next_A(C)
                mm_acc(C, pa, ka, [(L.hT[:, c, i * 128:(i + 1) * 128], L.w1t[:, c, 512:1024]) for c in range(8)], hkeys('hT') + ['w1t'])
                fw.op('act', lambda: A.copy(out=L.stgv[:, sv, :], in_=pa), r=[ka], w=[('stgv', sv)])
                fw.dma('sp', D['s_vc'][tc0 + i * 128:tc0 + (i + 1) * 128, :], L.stgv[:, sv, :], r=[('stgv', sv)], w=['s_vc'])
                if (t == 0 and i < 2) or (t == 3 and i >= 2):
                    fw.dma('sp', VCv[:, i, :], L.stgv[:, sv, :], r=[('stgv', sv)], w=['xinB'])
                ka, pa = next_A(C)
                mm_acc(C, pa[:, 0:384], ka, [(L.hT[:, c, i * 128:(i + 1) * 128], L.w1t[:, c, 1024:1408]) for c in range(8)], hkeys('hT') + ['w1t'])
                rms_tok(C, pa[:, 0:384], ka, 384, L.gq[:], 'gq', L.cqb[:], ['cqb'])
                transpose_to(C, L.cqb, 'cqb', 3, lambda: L.stgq[:, :, i * 128:(i + 1) * 128], ['stgq'])
                ka, pa = next_A(C)
                mm_acc(C, pa[:, 0:256], ka, [(L.hT[:, c, i * 128:(i + 1) * 128], L.w1t[:, c, 1408:1664]) for c in range(8)], hkeys('hT') + ['w1t'])
                rms_tok(C, pa[:, 0:256], ka, 256, L.gkv[:], 'gkv', L.ckvb[:], ['ckvb'])
                transpose_to(C, L.ckvb, 'ckvb', 2, lambda: L.stgc[:, :, i * 128:(i + 1) * 128], ['stgc'])
            fw.dma('sp', D['s_cq'].rearrange("(k p) t -> p k t", p=128)[:, :, tc0:tc1], L.stgq[:], r=['stgq'], w=['s_cq'])
            fw.dma('sp', D['xin'][0:256, :].rearrange("(k p) t -> p k t", p=128)[:, :, tc0:tc1], L.stgc[:], r=['stgc'], w=['xin'])
        if stop_after == 'l1a0':
            return
        fw.custom('pool', lambda: G.collective_compute("AllGather", ALU.bypass, replica_groups=[[0, 1], [2, 3], [4, 5], [6, 7]],
                                                       ins=[D['xin']], outs=[D['xout']]), C.cc_sem, 1, r=['xin'], w=['xout'])
        fw.custom('pool', lambda: G.collective_compute("AllGather", ALU.bypass, replica_groups=[[0, 1], [2, 3], [4, 5], [6, 7]],
                                                       ins=[D['xinB']], outs=[D['xoutB']]), C.cc_sem2, 1, r=['xinB'], w=['xoutB'])
        xout_tok = (fw.state['xout'], fw.state['xoutB'])
        if stop_after == 'l1a':
            return
        fw.barrier()
        fw.state['xout'], fw.state['xoutB'] = xout_tok
        sa1.close()
        sa2 = sa.enter_context(ExitStack())
        L.wuq = sbt(nc, sa2, 'wuq', [128, 3, 1536], BF); L.wuk = sbt(nc, sa2, 'wuk', [128, 2, 512], BF); L.wuv = sbt(nc, sa2, 'wuv', [128, 2, 512], BF)
        L.wo = sbt(nc, sa2, 'wo1', [128, 8, 1024], BF)
        L.qcp = sbt(nc, sa2, 'qcp', [128, 4, 256], BF); L.kcp = sbt(nc, sa2, 'kcp', [128, 4, 256], BF)
        L.gcp = sbt(nc, sa2, 'gcp', [128, 4, 256], BF); L.gdp = sbt(nc, sa2, 'gdp', [128, 4, 256], BF)
        L.kpep = sbt(nc, sa2, 'kpep', [128, 256], BF); L.Vcp = sbt(nc, sa2, 'Vcp', [128, 2, 768], BF)
        L.cqTp = sbt(nc, sa2, 'cqTp', [128, 3, 256], BF); L.ckvTp = sbt(nc, sa2, 'ckvTp', [128, 2, 256], BF)
        L.kThp = sbt(nc, sa2, 'kThp', [128, 2, 256], BF); L.Vdp = sbt(nc, sa2, 'Vdp', [128, 2, 192], BF)
        L.qdT = sbt(nc, sa2, 'qdT', [128, 2, 512], BF); L.AOp = sbt(nc, sa2, 'AOp', [128, 8, 256], BF)
        L.f32a = sbt(nc, sa2, 'f32a', [128, 1024], F32)
        print('sbuf remaining (l1 prompts):', nc.sbuf_bytes_remaining)
        load_w_cast(C, L.wuq, D['w_uq_ab'], 3, 1536, 'wuq')
        load_w_cast(C, L.wuk, D['w_uk'], 2, 512, 'wuk')
        load_w_cast(C, L.wuv, D['w_uv'], 2, 512, 'wuv')
        load_w_cast(C, L.wo, D['w_out_o'], 8, 1024, 'wo')
        fw.op('dve', lambda: V.memset(L.Vcp[:], 1.0), w=['Vcp'])
        fw.op('dve', lambda: V.memset(L.Vdp[:], 1.0), w=['Vdp'])
        fw.op('dve', lambda: V.memset(L.kThp[:], 0.0), w=[('kThp', 0), ('kThp', 1)])
        fw.op('dve', lambda: V.memset(L.qdT[:], 0.0), w=[('qdT', 0), ('qdT', 1)])
        for s in range(4):
            r0 = s * 256
            xts = [load_x(C, D['x1p'][r0 + i * 128:r0 + (i + 1) * 128, :]) for i in range(2)]
            prenorm(C, 1, 1, xts, L.hT, 'hT')
            for grp, (dst, silu, dk) in enumerate(((L.qcp, False, 'qcp'), (L.kcp, False, 'kcp'), (L.gcp, True, 'gcp'), (L.gdp, True, 'gdp'))):
                def ev(j, ka, pa, dst=dst, silu=silu, dk=dk):
                    if silu:
                        fw.op('act', lambda: A.activation(out=dst[:, j, :], in_=pa, func=AF.Silu), r=[ka], w=[dk])
                    else:
                        fw.op('act', lambda: A.copy(out=dst[:, j, :], in_=pa), r=[ka], w=[dk])
                fm_proj(C, L.w1, 'w1', grp * 512, 4, L.hT, 'hT', 256, ev)
            k0, p0 = next_A(C)
            mm_acc(C, p0[0:96, 0:256], k0, [(L.wk2[:, c, 0:96], L.hT[:, c, 0:256]) for c in range(8)], hkeys('hT') + ['wk2'])
            fw.op('dve', lambda: V.tensor_copy(out=L.kpep[64:96, :], in_=p0[64:96, 0:256]), r=[k0], w=['kpep'])
            if stop_after == 'l1p1':
                fw.barrier(); return
            for i in range(2):
                tk = slice(i * 128, (i + 1) * 128)
                for half in range(2):
                    mm_acc(C, C.psA[:, half * 512:(half + 1) * 512], ('A', half),
                           [(L.hT[:, c, tk], L.w1t[:, c, half * 512:(half + 1) * 512]) for c in range(8)], hkeys('hT') + ['w1t'])
                fw.op('act', lambda: A.copy(out=L.f32a[:, 0:512], in_=C.psA[:, 0:512]), r=[('A', 0)], w=['f32a'])
                fw.op('act', lambda: A.copy(out=L.f32a[:, 512:1024], in_=C.psA[:, 512:1024]), r=[('A', 1)], w=['f32a'])
                fw.dma('sp', D['nck'][r0 + i * 128:r0 + (i + 1) * 128, :], L.f32a[:, 0:512], r=['f32a'])
                fw.dma('sp', D['ncv'][r0 + i * 128:r0 + (i + 1) * 128, :], L.f32a[:, 512:1024], r=['f32a'])
                vpair_copy(C, L.Vcp[:, i, :], C.psA[:, 512:1024], ('A', 1), ['Vcp'], 4)
                if stop_after == 'l1p1a':
                    fw.barrier(); return
                mm_acc(C, C.psA[:, 0:384], ('A', 0), [(L.hT[:, c, tk], L.w1t[:, c, 1024:1408]) for c in range(8)], hkeys('hT') + ['w1t'])
                rms_tok(C, C.psA[:, 0:384], ('A', 0), 384, L.gq[:], 'gq', L.cqb[:], ['cqb'])
                transpose_to(C, L.cqb, 'cqb', 3, lambda: L.cqTp[:, :, tk], ['cqTp'])
                if stop_after == 'l1p1b':
                    fw.barrier(); return
                mm_acc(C, C.psA[:, 512:800], ('A', 1), [(L.hT[:, c, tk], L.w1t[:, c, 1408:1696]) for c in range(8)], hkeys('hT') + ['w1t'])
                rms_tok(C, C.psA[:, 512:768], ('A', 1), 256, L.gkv[:], 'gkv', f32b[:, 0:256], [('rcb', 0)])
                fw.op('act', lambda: A.copy(out=f32b[:, 256:288], in_=C.psA[:, 768:800]), r=[('A', 1)], w=[('rcb', 0)])
                fw.dma('sp', D['nckv'][r0 + i * 128:r0 + (i + 1) * 128, :], f32b[:, 0:256], r=[('rcb', 0)])
                fw.dma('sp', D['nkpe'][r0 + i * 128:r0 + (i + 1) * 128, :], f32b[:, 256:288], r=[('rcb', 0)])
                fw.op('act', lambda: A.copy(out=L.ckvb[:], in_=f32b[:, 0:256]), r=[('rcb', 0)], w=['ckvb'])
                transpose_to(C, L.ckvb, 'ckvb', 2, lambda: L.ckvTp[:, :, tk], ['ckvTp'])
            if stop_after == 'l1p2':
                fw.barrier(); return
            calls = []
            for h in range(8):
                j, side = h // 2, h % 2
                rows = slice(side * 64, (side + 1) * 64)
                kts = [(L.kcp[:, j, kt * 128:(kt + 1) * 128], L.Vcp[:, kt, j * 192 + side * 64:j * 192 + side * 64 + 128], None,
                        ['kcp', 'Vcp']) for kt in range(2)]
                calls.append(dict(q64=L.qcp[rows, j, :], qkeys=['qcp'], nq=256, kts=kts, scale=0.125, side=side,
                                  dst=L.AOp[rows, j, :], gate=L.gcp[rows, j, :], dkeys=[('AOp', j)], gkeys=['gcp']))
            run_attn_calls(C, calls)
            if stop_after == 'l1p3':
                fw.barrier(); return
            for j in range(4):
                for kt in range(2):
                    ka, pa = next_A(C)
                    mm_acc(C, pa[:, 0:128], ka, [(L.ckvTp[:, c, kt * 128:(kt + 1) * 128], L.wuv[:, c, j * 128:(j + 1) * 128]) for c in range(2)],
                           ['ckvTp', 'wuv'])
                    vpair_copy(C, L.Vdp[:, kt, :], pa[:, 0:128], ka, ['Vdp'], 1)
                for hh in range(2):
                    h = 2 * j + hh
                    ka, pa = next_A(C)
                    mm_acc(C, pa[0:64, 0:256], ka, [(L.wuk[:, c, h * 64:(h + 1) * 64], L.ckvTp[:, c, :]) for c in range(2)], ['ckvTp', 'wuk'])
                    fw.op('dve', lambda: V.tensor_copy(out=L.kThp[0:64, hh, :], in_=pa[0:64, 0:256]), r=[ka], w=[('kThp', hh)])
                    fw.op('pool', lambda: G.tensor_copy(out=L.kThp[64:96, hh, :], in_=L.kpep[64:96, :]), r=['kpep'], w=[('kThp', hh)])
                    ka, pa = next_A(C)
                    mm_acc(C, pa[0:96, 0:256], ka, [(L.wuq[:, c, h * 96:(h + 1) * 96], L.cqTp[:, c, :]) for c in range(3)], ['cqTp', 'wuq'])
                    fw.op('dve', lambda: V.tensor_copy(out=L.qdT[0:64, hh, 0:256], in_=pa[0:64, 0:256]), r=[ka], w=[('qdT', hh)])
                    fw.op('dve', lambda: V.tensor_copy(out=L.qdT[64:96, hh, 0:256], in_=pa[64:96, 0:256]), r=[ka], w=[('qdT', hh)])
                    rows = slice(hh * 64, (hh + 1) * 64)
                    kts = [(L.kThp[:, hh, kt * 128:(kt + 1) * 128], L.Vdp[:, kt, hh * 64:hh * 64 + 128], None, [('kThp', hh), 'Vdp'])
                           for kt in range(2)]
                    attention(C, L.qdT[:, hh, 0:256], 256, [('qdT', hh)], kts, 96.0 ** -0.5, hh, L.AOp[rows, 4 + j, :], L.gdp[rows, j, :],
                              [('AOp', 4 + j)], ['gdp'])
            if stop_after == 'l1p4':
                fw.barrier(); return
            outproj_residual(C, 1, 1, L.AOp, [('AOp', c) for c in range(8)], L.wo, 2, xts,
                             lambda i: D['y_p'][r0 + i * 128:r0 + (i + 1) * 128, :])
        fw.barrier()
        fw.state['xout'], fw.state['xoutB'] = xout_tok
    if stop_after == 'l1p':
        return
    phase_l1_sample(C, xout_flat, RK)


def phase_l1_sample(C, xout_flat, RK):
    nc, fw, D = C.nc, C.fw, C.D
    V, A, PE, G = nc.vector, nc.scalar, nc.tensor, nc.gpsimd
    with ExitStack() as so:
        AOc = sbt(nc, so, 'AOc', [128, 4, 2048], BF)
        AOd = sbt(nc, so, 'AOd', [128, 4, 2048], BF)
        fw.dma('sp', AOc[:], D['s_gc'].rearrange("(j p) t -> p j t", p=128), w=[('AOc', j, t) for j in range(4) for t in range(4)])
        fw.dma('sp', AOd[:], D['s_gd'].rearrange("(j p) t -> p j t", p=128), w=[('AOd', j, t) for j in range(4) for t in range(4)])
        with ExitStack() as sc:
            kc = sbt(nc, sc, 'kcA', [128, 4, 3072], BF)
            Vc = sbt(nc, sc, 'VcA', [128, 24, 768], BF)
            qc = sbt(nc, sc, 'qcA', [128, 4, 2048], BF)
            EB = sbt(nc, sc, 'EB', [128, 2, 4480], BF)
            ebs = sbt(nc, sc, 'ebs', [128, 2, 2240], F32)
            print('sbuf remaining (l1 C):', nc.sbuf_bytes_remaining)
            fw.op('dve', lambda: V.memset(Vc[:], 1.0), w=['Vc'])
            fw.dma('pool', kc[:, :, 0:512], D['cck_T'].rearrange("(j p) k -> p j k", p=128), w=['kc'])
            fw.dma('sp', kc[:, :, 768:2816], D['s_kc'].rearrange("(j p) t -> p j t", p=128), w=['kc'])
            xoB = D['xoutB'].rearrange("r c -> (r c)")
            RKB = XINB_ROWS * 2048
            kc0 = xoB[OFF_KC:OFF_KC + 512 * 512].rearrange("(j p t) -> p j t", j=4, p=128, t=512)
            kc1 = xoB[RKB + OFF_KC:RKB + OFF_KC + 512 * 512].rearrange("(j p t) -> p j t", j=4, p=128, t=512)
            fw.dma('sp', kc[:, :, 512:768], kc0[:, :, 256:512], r=['xoutB'], w=['kc'])
            fw.dma('sp', kc[:, :, 2816:3072], kc1[:, :, 0:256], r=['xoutB'], w=['kc'])
            qq = D['s_qc'].rearrange("(j p) t -> p j t", p=128)
            fw.dma('sp', qc[:], qq, w=['qc'])
            ccv = D['ccv'].rearrange("(n p) c -> p n c", p=128)
            svc = D['s_vc'].rearrange("(n p) c -> p n c", p=128)
            vc0 = xoB[OFF_VC:OFF_VC + 512 * 512].rearrange("(n p c) -> p n c", n=4, p=128, c=512)
            vc1 = xoB[RKB + OFF_VC:RKB + OFF_VC + 512 * 512].rearrange("(n p c) -> p n c", n=4, p=128, c=512)
            for h in range(8):
                dcol = (h // 2) * 192 + (h % 2) * 128
                fw.dma('pool', Vc[:, 0:4, dcol:dcol + 64], ccv[:, :, h * 64:(h + 1) * 64], w=['Vc'])
                fw.dma('sp', Vc[:, 6:22, dcol:dcol + 64], svc[:, :, h * 64:(h + 1) * 64], w=['Vc'])
                fw.dma('sp', Vc[:, 4:6, dcol:dcol + 64], vc0[:, 2:4, h * 64:(h + 1) * 64], r=['xoutB'], w=['Vc'])
                fw.dma('sp', Vc[:, 22:24, dcol:dcol + 64], vc1[:, 0:2, h * 64:(h + 1) * 64], r=['xoutB'], w=['Vc'])
            eng_rr = 0
            for h in range(8):
                j, side = h // 2, h % 2
                rows = slice(side * 64, (side + 1) * 64)
                eb = h % 2
                for part in range(2):
                    fw.dma('sp', ebs[:, part, :], D['nabias'][h, :, part * 2240:(part + 1) * 2240], w=[('ebs', part)])
                    fw.op('act', lambda: A.activation(out=EB[:, eb, part * 2240:(part + 1) * 2240], in_=ebs[:, part, :], func=AF.Exp),
                          r=[('ebs', part)], w=[('EB', eb)])
                calls = []
                for t in range(4):
                    kts = [(kc[:, j, kt * 128:(kt + 1) * 128], Vc[:, kt, j * 192 + side * 64:j * 192 + side * 64 + 128], None, ['kc', 'Vc'])
                           for kt in range(4)]
                    for m in range(8):
                        kt = 4 * t + m
                        c0 = (14 - 2 * m) * 64
                        cr = (0, 512)
                        if t == 0 and m <= 5:
                            spec = [(EB[:, eb, 1408 + m * 256:1408 + (m + 1) * 256], 0, 256), (EB[:, eb, c0 + 256:c0 + 512], 256, 512)]
                        elif t == 3 and m >= 2:
                            spec = [(EB[:, eb, c0:c0 + 256], 0, 256), (EB[:, eb, 2944 + (m - 2) * 256:2944 + (m - 1) * 256], 256, 512)]
                        else:
                            lo, hi = max(0, 2 * m - 7), min(7, 2 * m + 1)
                            cr = (lo * 64, (hi + 1) * 64)
                            spec = [(EB[:, eb, c0 + cr[0]:c0 + cr[1]], cr[0], cr[1])]
                        eng_rr += 1
                        ebl = [(ap_, a0, a1, [('EB', eb)], 'pool' if eng_rr % 2 else 'dve') for (ap_, a0, a1) in spec]
                        kts.append((kc[:, j, 512 + kt * 128:512 + (kt + 1) * 128],
                                    Vc[:, 4 + kt, j * 192 + side * 64:j * 192 + side * 64 + 128], ebl, ['kc', 'Vc'], cr))
                    ao = AOc[rows, j, t * 512:(t + 1) * 512]
                    calls.append(dict(q64=qc[rows, j, t * 512:(t + 1) * 512], qkeys=['qc'], nq=512, kts=kts, scale=0.125, side=side,
                                      dst=ao, gate=ao, dkeys=[('AOc', j, t)], gkeys=[('AOc', j, t)]))
                run_attn_calls(C, calls, la=4, banks=[(C.psS[i][:, :], ('S', i)) for i in range(3)] +
                               [(C.psA[:, i * 512:(i + 1) * 512], ('A', i)) for i in range(2)])
            fw.barrier()
        with ExitStack() as sd:
            ckvT = sbt(nc, sd, 'ckvTA', [128, 2, 4608], BF)
            kpeT = sbt(nc, sd, 'kpeTA', [128, 4608], BF)
            cqT = sbt(nc, sd, 'cqTA', [128, 3, 2048], BF)
            rd = sbt(nc, sd, 'rdA', [128, 2, 2048], F32)
            wuq = sbt(nc, sd, 'wuqA', [128, 3, 1536], BF); wuk = sbt(nc, sd, 'wukA', [128, 2, 512], BF); wuv = sbt(nc, sd, 'wuvA', [128, 2, 512], BF)
            kTh = sbt(nc, sd, 'kThA', [128, 2, 4608], BF)
            Vd = sbt(nc, sd, 'VdA', [128, 36, 192], BF)
            qdT = sbt(nc, sd, 'qdTA', [128, 2, 512], BF)
            r1 = sbt(nc, sd, 'r1A', [128, 512], F32); r2 = sbt(nc, sd, 'r2A', [128, 512], F32)
            print('sbuf remaining (l1 D):', nc.sbuf_bytes_remaining)
            load_w_cast(C, wuq, D['w_uq_ab'], 3, 1536, 'wuq')
            load_w_cast(C, wuk, D['w_uk'], 2, 512, 'wuk')
            load_w_cast(C, wuv, D['w_uv'], 2, 512, 'wuv')
            fw.op('dve', lambda: V.memset(Vd[:], 1.0), w=['Vd'])
            fw.op('dve', lambda: V.memset(kTh[:], 0.0), w=[('kTh', 0), ('kTh', 1)])
            fw.op('dve', lambda: V.memset(qdT[:], 0.0), w=[('qdT', 0), ('qdT', 1)])
            fw.dma('pool', ckvT[:, :, 0:512], D['cckv_T'].rearrange("(k p) t -> p k t", p=128), w=['ckvT'])
            fw.dma('pool', kpeT[64:96, 0:512], D['ckpe_T'][64:96, :], w=['kpeT'])
            for rk in range(2):
                base = rk * RK
                src = xout_flat[base:base + 256 * 2048].rearrange("(k p t) -> p k t", k=2, p=128, t=2048)
                fw.dma('sp', ckvT[:, :, 512 + rk * 2048:512 + (rk + 1) * 2048], src, r=['xout'], w=['ckvT'])
                srck = xout_flat[base + OFF_KPE:base + OFF_KPE + 32 * 2048].rearrange("(p t) -> p t", p=32, t=2048)
                fw.dma('sp', kpeT[64:96, 512 + rk * 2048:512 + (rk + 1) * 2048], srck, r=['xout'], w=['kpeT'])
            fw.dma('sp', cqT[:], D['s_cq'].rearrange("(k p) t -> p k t", p=128), w=['cqT'])
            fw.dma('sp', rd[:], D['rope_d'].rearrange("p (a t) -> p a t", a=2), w=['rd'])
            sc = 96.0 ** -0.5
            for j in range(4):
                for g4 in range(9):
                    ka, pa = next_A(C)
                    for k4 in range(4):
                        kt = g4 * 4 + k4
                        mm_acc(C, pa[:, k4 * 128:(k4 + 1) * 128], ka,
                               [(ckvT[:, c, kt * 128:(kt + 1) * 128], wuv[:, c, j * 128:(j + 1) * 128]) for c in range(2)], ['ckvT', 'wuv'])
                    d4 = Vd[:, g4 * 4:(g4 + 1) * 4, :].rearrange("p k (a d) -> p k a d", a=3, d=64)
                    s4 = pa.rearrange("p (k a d) -> p k a d", k=4, a=2, d=64)
                    fw.op('dve', lambda: V.tensor_copy(out=d4[:, :, 0, :], in_=s4[:, :, 0, :]), r=[ka], w=['Vd'])
                    fw.op('dve', lambda: V.tensor_copy(out=d4[:, :, 2, :], in_=s4[:, :, 1, :]), r=[ka], w=['Vd'])
                for hh in range(2):
                    h = 2 * j + hh
                    for blk in range(9):
                        ka, pa = next_A(C)
                        mm_acc(C, pa[0:64, :], ka, [(wuk[:, c, h * 64:(h + 1) * 64], ckvT[:, c, blk * 512:(blk + 1) * 512]) for c in range(2)],
                               ['ckvT', 'wuk'])
                        fw.op('dve', lambda: V.tensor_copy(out=kTh[0:64, hh, blk * 512:(blk + 1) * 512], in_=pa[0:64, :]), r=[ka], w=[('kTh', hh)])
                    fw.op('pool', lambda: G.tensor_copy(out=kTh[64:96, hh, :], in_=kpeT[64:96, :]), r=['kpeT'], w=[('kTh', hh)])
                def emit_q(t, hh, j=j):
                    h = 2 * j + hh
                    tq = slice(t * 512, (t + 1) * 512)
                    k0, p0 = next_A(C)
                    mm_acc(C, p0[0:96, :], k0, [(wuq[:, c, h * 96:(h + 1) * 96], cqT[:, c, tq]) for c in range(3)], ['cqT', 'wuq'])
                    k1, p1 = next_A(C)
                    mm_acc(C, p1[0:96, :], k1, [(wuq[:, c, 768 + h * 96:768 + (h + 1) * 96], cqT[:, c, tq]) for c in range(3)], ['cqT', 'wuq'])
                    fw.op('dve', lambda: V.tensor_copy(out=qdT[0:64, hh, :], in_=p0[0:64, :]), r=[k0], w=[('qdT', hh)])
                    fw.op('dve', lambda: V.tensor_tensor(out=r1[64:96, :], in0=p0[64:96, :], in1=rd[64:96, 0, tq], op=ALU.mult), r=[k0, 'rd'], w=['r1'])
                    fw.op('dve', lambda: V.tensor_tensor(out=r2[64:96, :], in0=p1[64:96, :], in1=rd[64:96, 1, tq], op=ALU.mult), r=[k1, 'rd'], w=['r2'])
                    fw.op('dve', lambda: V.tensor_tensor(out=qdT[64:96, hh, :], in0=r1[64:96, :], in1=r2[64:96, :], op=ALU.add),
                          r=['r1', 'r2'], w=[('qdT', hh)])

                seq = [(t, hh) for t in range(4) for hh in range(2)]
                emit_q(*seq[0])
                for i, (t, hh) in enumerate(seq):
                    tq = slice(t * 512, (t + 1) * 512)
                    rows = slice(hh * 64, (hh + 1) * 64)
                    kts = [(kTh[:, hh, kt * 128:(kt + 1) * 128], Vd[:, kt, hh * 64:hh * 64 + 128], None, [('kTh', hh), 'Vd'])
                           for kt in range(36)]
                    ao = AOd[rows, j, tq]
                    fl = (lambda nx=seq[i + 1]: emit_q(*nx)) if i + 1 < len(seq) else None
                    attention(C, qdT[:, hh, :], 512, [('qdT', hh)], kts, sc, hh, ao, ao, [('AOd', j, t)], [('AOd', j, t)], filler=fl)
            fw.barrier()
        with ExitStack() as sp_:
            wo = sbt(nc, sp_, 'wo1A', [128, 8, 1024], BF)
            load_w_cast(C, wo, D['w_out_o'], 8, 1024, 'wo')
            for t in range(4):
                xts = [load_x(C, D['x1s'][t * 512 + i * 128:t * 512 + (i + 1) * 128, :]) for i in range(4)]
                keys = [('AOc', j, t) for j in range(4)] + [('AOd', j, t) for j in range(4)]
                outproj_residual(C, 1, 0, None, keys, wo, 4, xts,
                                 lambda i: D['y_s'][t * 512 + i * 128:t * 512 + (i + 1) * 128, :],
                                 ao_fn=lambda c, i: (AOc[:, c, t * 512 + i * 128:t * 512 + (i + 1) * 128] if c < 4
                                                     else AOd[:, c - 4, t * 512 + i * 128:t * 512 + (i + 1) * 128]))
            fw.barrier()
```

```python
import numpy as np
from contextlib import ExitStack
import concourse.bass as bass
import concourse.mybir as mybir
from concourse.bass_utils import run_bass_kernel_spmd

F32 = mybir.dt.float32
BF = mybir.dt.bfloat16
AF = mybir.ActivationFunctionType
ALU = mybir.AluOpType
AX = mybir.AxisListType
P = 128


class FW:
    ND = 24
    NHW = 16

    def __init__(self, nc, es):
        self.nc = nc
        self.eng = {'pe': nc.tensor, 'act': nc.scalar, 'dve': nc.vector, 'pool': nc.gpsimd, 'sp': nc.sync}
        self.sem = {e: es.enter_context(nc.semaphore('s_' + e)) for e in ('pe', 'act', 'dve', 'pool')}
        self.n = {e: 0 for e in self.sem}
        self.dsem = [es.enter_context(nc.semaphore('d%d' % i)) for i in range(self.ND)]
        self.dn = [0] * self.ND
        self.rr = 0
        self.rr_sw = 0
        self.waited = {e: {} for e in self.eng}
        self.state = {}
        self.ns = None
        self.nskeys = set()

    def _k(self, keys):
        if self.ns is None:
            return list(keys)
        out = []
        for k in keys:
            base = k if isinstance(k, str) else k[0]
            out.append(('ns', self.ns, k) if base in self.nskeys else k)
        return out

    def _wait(self, eng, tid, seq):
        if self.waited[eng].get(tid, 0) >= seq:
            return
        self.waited[eng][tid] = seq
        if tid[0] == 'e':
            assert seq <= self.n[tid[1]], (eng, tid, seq, self.n[tid[1]])
            self.eng[eng].wait_ge(self.sem[tid[1]], seq)
        elif tid[0] == 'c':
            sem, amount = self.csem[tid[1]]
            self.eng[eng].wait_ge(sem, amount * seq)
        else:
            self.eng[eng].wait_ge(self.dsem[tid[1]], 16 * seq)

    def _dep(self, eng, tok, raw, strict):
        tid, seq = tok
        if tid == ('e', eng) and not strict:
            if eng == 'pe':
                return
        self._wait(eng, tid, seq)

    def _sync(self, eng, r, w, strict=False):
        for k in r:
            st = self.state.get(k)
            if st and st[0]:
                self._dep(eng, st[0], True, strict)
        for k in w:
            st = self.state.get(k)
            if st:
                if st[0]:
                    self._dep(eng, st[0], True, strict)
                for tid, seq in st[1].items():
                    self._dep(eng, (tid, seq), False, strict)

    def _mark(self, tid, seq, r, w):
        for k in r:
            st = self.state.setdefault(k, [None, {}])
            st[1][tid] = seq
        for k in w:
            self.state[k] = [(tid, seq), {}]

    @staticmethod
    def _excl(r, w):
        px = [k for k in r if k == 'psT' or (isinstance(k, tuple) and k[0] in ('A', 'S', 'O'))]
        if not px:
            return r, w
        return [k for k in r if k not in px], list(w) + px

    def op(self, eng, fn, r=(), w=()):
        r, w = self._excl(self._k(r), self._k(w))
        self._sync(eng, r, w)
        ins = fn()
        self.n[eng] += 1
        ins.then_inc(self.sem[eng], 1)
        self._mark(('e', eng), self.n[eng], r, w)
        return ins

    def op_noinc(self, eng, fn, r=(), w=()):
        r, w = self._excl(self._k(r), self._k(w))
        self._sync(eng, r, w)
        fn()
        self._mark(('e', eng), self.n[eng] + 1, r, w)

    def dma(self, q, out, in_, r=(), w=(), **kw):
        r, w = self._k(r), self._k(w)
        self._sync(q, r, w, strict=True)
        if q == 'pool':
            i = self.NHW + self.rr_sw
            self.rr_sw = (self.rr_sw + 1) % (self.ND - self.NHW)
        else:
            i = self.rr
            self.rr = (i + 1) % self.NHW
        if self.dn[i] > 0:
            self._wait(q, ('d', i), self.dn[i])
        ins = self.eng[q].dma_start(out=out, in_=in_, **kw)
        self.dn[i] += 1
        ins.then_inc(self.dsem[i], 16)
        self._mark(('d', i), self.dn[i], r, w)
        return ins

    def custom(self, eng, fn, sem, amount, r=(), w=()):
        self._sync(eng, r, w, strict=True)
        ins = fn()
        ins.then_inc(sem, amount)
        self.csem = getattr(self, 'csem', {})
        cid = len(self.csem)
        self.csem[cid] = (sem, amount)
        self._mark(('c', cid), 1, r, w)
        return ins

    def barrier(self):
        for e in self.eng:
            for e2 in self.sem:
                if e2 != e and self.n[e2] > 0:
                    self._wait(e, ('e', e2), self.n[e2])
            for i in range(self.ND):
                if self.dn[i] > 0:
                    self._wait(e, ('d', i), self.dn[i])
        self.state.clear()

    def finish(self):
        for i in range(self.ND):
            if self.dn[i] > 0:
                self._wait('sp', ('d', i), self.dn[i])
        for cid in getattr(self, 'csem', {}):
            self._wait('sp', ('c', cid), 1)


D_MODEL = 1024
NEG = -30000.0
EPS = 1e-6
PERM_A = np.concatenate([np.r_[j * 64:(j + 1) * 64, (4 + j) * 64:(5 + j) * 64] for j in range(4)])
POOL_W = (2, 4, 8, 16)
XIN_ROWS = 288
XINB_ROWS = 256
OFF_KPE = 256 * 2048
OFF_KC = 0
OFF_VC = 512 * 512


class Ctx:
    pass


_SBT_N = [0]


def sbt(nc, es, name, shape, dt):
    _SBT_N[0] += 1
    return es.enter_context(nc.sbuf_tensor('sb%d_%s' % (_SBT_N[0], name), list(shape), dt))


def build_program(stop_after=None):
    nc = bass.Bass("TRN2", target_bir_lowering=False)
    D = {}

    def din(name, shape, dt=F32):
        D[name] = nc.dram_tensor(name, list(shape), dt, kind="ExternalInput").ap()

    def dout(name, shape):
        D[name] = nc.dram_tensor(name, list(shape), F32, kind="ExternalOutput").ap()

    def dscr(name, shape, dt):
        if stop_after is not None and name in ('x1s', 'x1p'):
            D[name] = nc.dram_tensor(name, list(shape), dt, kind="ExternalOutput").ap()
        else:
            D[name] = nc.dram_tensor(name, list(shape), dt).ap()

    din('xs', [4096, 1024]); din('xp', [1024, 1024])
    din('condT', [128, 16]); din('w_mod', [2, 1024, 3072]); din('b_mod', [2, 3072])
    din('g_preT', [128, 32]); din('g_post', [2, 1024]); din('sel', [2, 256])
    din('w_in_e', [1024, 2304]); din('gain_a', [128, 640]); din('b_mapT', [128, 512]); din('b_scaleT', [128, 4])
    din('w_out_e', [1024, 1024]); din('rope_a', [4096, 128]); din('prc_s', [128, 4 * 2048]); din('prc_p', [128, 4 * 256])
    din('hmask', [128, 2]); din('cak_T', [128, 512]); din('cav', [512, 128])
    din('w1_fm', [1024, 2048]); din('w1_tm', [1024, 1696]); din('w_kpe2', [1024, 192])
    din('gain_q', [128, 384]); din('gain_kv', [128, 256])
    din('w_uq_ab', [384, 1536]); din('w_uk', [256, 512]); din('w_uv', [256, 512]); din('w_out_o', [1024, 1024])
    din('rope_d', [128, 2 * 2048]); din('cck_T', [512, 512]); din('ccv', [512, 512]); din('cckv_T', [256, 512])
    din('ckpe_T', [128, 512]); din('nabias', [8, 128, 4480])
    dout('y_s', [2048, 1024]); dout('y_p', [1024, 1024]); dout('nak', [1024, 128]); dout('nav', [1024, 128])
    dout('nck', [1024, 512]); dout('ncv', [1024, 512]); dout('nckv', [1024, 256]); dout('nkpe', [1024, 32])
    dscr('x1s', [2048, 1024], F32); dscr('x1p', [1024, 1024], F32)
    dscr('s_qc', [512, 2048], BF); dscr('s_kc', [512, 2048], BF); dscr('s_gc', [512, 2048], BF); dscr('s_gd', [512, 2048], BF)
    dscr('s_cq', [384, 2048], BF); dscr('s_vc', [2048, 512], BF)
    dscr('xin', [XIN_ROWS, 2048], BF); dscr('xout', [2 * XIN_ROWS, 2048], BF)
    dscr('xinB', [XINB_ROWS, 2048], BF); dscr('xoutB', [2 * XINB_ROWS, 2048], BF)

    C = Ctx()
    C.nc = nc; C.D = D
    with ExitStack() as gs:
        fw = FW(nc, gs)
        C.fw = fw
        C.cc_sem = gs.enter_context(nc.semaphore('cc_sem'))
        C.cc_sem2 = gs.enter_context(nc.semaphore('cc_sem2'))
        C.ident_b = sbt(nc, gs, 'ident_b', [128, 128], BF)
        C.ident_f = sbt(nc, gs, 'ident_f', [128, 128], F32)
        C.modS = sbt(nc, gs, 'modS', [128, 32], F32)
        C.modH = sbt(nc, gs, 'modH', [128, 32], F32)
        C.gg = sbt(nc, gs, 'gg', [128, 4, 1024], F32)
        C.stat = sbt(nc, gs, 'stat', [128, 64], F32)
        C.stat_i = 0
        C.xbuf = sbt(nc, gs, 'xbuf', [128, 4, 1024], F32)
        C.xb_i = 0
        C.xn = sbt(nc, gs, 'xn', [128, 4, 1024], BF)
        C.junk = sbt(nc, gs, 'junk', [128, 1024], BF)
        C.ty = sbt(nc, gs, 'ty', [128, 1024], F32)
        C.pbuf = sbt(nc, gs, 'pbuf', [128, 6, 512], BF)
        C.p_i = 0
        C.rcb = sbt(nc, gs, 'rcb', [128, 2, 512], F32)
        C.qz = sbt(nc, gs, 'qz', [128, 2, 2, 512], BF)
        C.qz_i = [0, 0]
        C.psS = [gs.enter_context(nc.psum_tensor('psS%d' % i, [128, 512], F32)) for i in range(3)]
        C.s_i = 0
        C.psO = [gs.enter_context(nc.psum_tensor('psO%d' % i, [128, 512], F32)) for i in range(2)]
        C.o_i = 0
        C.psA = gs.enter_context(nc.psum_tensor('psA', [128, 1024], F32))
        C.a_i = 0
        C.a_wide = True
        C.psT = gs.enter_context(nc.psum_tensor('psT', [128, 1024], BF))
        fw.op('pool', lambda: nc.gpsimd.memset(C.qz[:], 0.0), w=[('qz', a, b) for a in range(2) for b in range(2)])
        for t, k in ((C.ident_b, 'ident_b'), (C.ident_f, 'ident_f')):
            fw.op('pool', lambda: nc.gpsimd.memset(t[:], 1.0), w=[k])
            fw.op('pool', lambda: nc.gpsimd.affine_select(out=t[:], in_=t[:], pattern=[[-1, 128]], compare_op=ALU.is_equal,
                                                          fill=0.0, base=0, channel_multiplier=1), r=[k], w=[k])
        import os as _os
        if _os.environ.get('KSKIP_L0'):
            phase_setup(C, (0,))
        if stop_after != 'setup' and not _os.environ.get('KSKIP_L0'):
            phase_l0(C, stop_after)
            fw.barrier()
        if stop_after is None or stop_after.startswith('l1'):
            phase_l1(C, stop_after)
            fw.barrier()
        fw.finish()
    return nc


def stat_col(C):
    i = C.stat_i
    C.stat_i = (i + 1) % 64
    return ('st', i), C.stat[:, i:i + 1]


def stat_cols(C, n):
    if C.stat_i + n > 64:
        C.stat_i = 0
    i = C.stat_i
    C.stat_i = (i + n) % 64
    return [('st', j) for j in range(i, i + n)], C.stat[:, i:i + n]


def next_A(C):
    if getattr(C, 'a_wide', False):
        i = C.a_i % 5
        C.a_i = i + 1
        if i < 2:
            return ('A', i), C.psA[:, i * 512:(i + 1) * 512]
        return ('S', i - 2), C.psS[i - 2][:, :]
    i = C.a_i % 2
    C.a_i = i + 1
    return ('A', i), C.psA[:, i * 512:(i + 1) * 512]


def phase_setup(C, layers=(0, 1), after_issue=None, stack=None):
    nc, fw, D = C.nc, C.fw, C.D
    V, A, PE = nc.vector, nc.scalar, nc.tensor
    own = ExitStack() if stack is None else None
    with (own if own is not None else ExitStack()) as _tmp:
        s0 = own if own is not None else stack
        condT = sbt(nc, s0, 'condT', [128, 16], F32); scT = sbt(nc, s0, 'scT', [128, 16], BF)
        wm = sbt(nc, s0, 'wm', [128, 8, 1024], BF)
        mrow = sbt(nc, s0, 'mrow', [2, 3072], F32); brow = sbt(nc, s0, 'brow', [2, 3072], F32)
        gpT = sbt(nc, s0, 'gpT', [128, 32], F32); sels = sbt(nc, s0, 'sels', [2, 256], F32)
        gpost = sbt(nc, s0, 'gpost', [128, 1024], F32)
        fw.dma('sp', condT[:], D['condT'], w=['condT'])
        fw.dma('sp', gpT[:], D['g_preT'], w=['gpT'])
        fw.dma('sp', sels[:], D['sel'], w=['sels'])
        fw.op('act', lambda: A.activation(out=scT[:], in_=condT[:], func=AF.Silu), r=['condT'], w=['scT'])
        for l in layers:
            fw.dma('sp', brow[:], D['b_mod'][l:l + 1, :].partition_broadcast(2), w=['brow'])
            fw.dma('sp', gpost[:], D['g_post'][l:l + 1, :].partition_broadcast(128), w=['gpost'])
            wsrc = D['w_mod'][l].rearrange("(c p) n -> p c n", p=128)
            for blk in range(3):
                for c in range(8):
                    fw.dma('pool', wm[:, c, :], wsrc[:, c, blk * 1024:(blk + 1) * 1024], w=[('wm', c)])
                for sub in range(2):
                    col0 = blk * 1024 + sub * 512
                    ka, pa = next_A(C)
                    for c in range(8):
                        f = lambda: PE.matmul(pa[0:2, :], lhsT=scT[:, 2 * c:2 * c + 2], rhs=wm[:, c, sub * 512:(sub + 1) * 512],
                                              start=(c == 0), stop=(c == 7))
                        if c < 7:
                            fw.op_noinc('pe', f, r=['scT', ('wm', c)], w=[ka])
                        else:
                            fw.op('pe', f, r=['scT', ('wm', c)], w=[ka])
                    fw.op('dve', lambda: V.tensor_tensor(out=mrow[0:2, col0:col0 + 512], in0=pa[0:2, :], in1=brow[0:2, col0:col0 + 512],
                                                         op=ALU.add), r=[ka, 'brow'], w=['mrow'])
            ka, pa = next_A(C)
            for c in range(16):
                f = lambda: PE.transpose(out=pa[:, 2 * c:2 * c + 2], in_=mrow[0:2, c * 128:(c + 1) * 128], identity=C.ident_f[0:2, 0:2])
                if c < 15:
                    fw.op_noinc('pe', f, r=['mrow', 'ident_f'], w=[ka])
                else:
                    fw.op('pe', f, r=['mrow', 'ident_f'], w=[ka])
            fw.op('dve', lambda: V.tensor_copy(out=C.modH[:, l * 16:(l + 1) * 16], in_=pa[:, 0:16]), r=[ka], w=['modH'])
            fw.op('dve', lambda: V.scalar_tensor_tensor(out=C.modS[:, l * 16:(l + 1) * 16], in0=pa[:, 16:32], scalar=1.0,
                                                        in1=gpT[:, l * 16:(l + 1) * 16], op0=ALU.add, op1=ALU.mult),
                  r=[ka, 'gpT'], w=['modS'])
            for g in range(2):
                for half in range(2):
                    ka, pa = next_A(C)
                    fw.op('pe', lambda: PE.matmul(pa, lhsT=sels[:, g * 128:(g + 1) * 128],
                                                  rhs=mrow[0:2, 2048 + half * 512:2048 + (half + 1) * 512], start=True, stop=True),
                          r=['sels', 'mrow'], w=[ka])
                    fw.op('dve', lambda: V.tensor_tensor(out=C.gg[:, l * 2 + g, half * 512:(half + 1) * 512], in0=pa,
                                                         in1=gpost[:, half * 512:(half + 1) * 512], op=ALU.mult),
                          r=[ka, 'gpost'], w=[('gg', l * 2 + g)])
        if after_issue is not None:
            after_issue()
        if own is not None:
            fw.barrier()


def load_x(C, src_rows):
    i = C.xb_i
    C.xb_i = (i + 1) % 4
    C.fw.dma('sp', C.xbuf[:, i, :], src_rows, w=[('xb', i)])
    return ('xb', i), C.xbuf[:, i, :]


def rstd_col(C, ss_key, ss_ap, n, width):
    nc, fw = C.nc, C.fw
    keys, rs = stat_cols(C, n)
    fw.op('act', lambda: nc.scalar.activation(out=rs, in_=ss_ap, func=AF.Sqrt, scale=1.0 / width, bias=EPS), r=ss_key, w=keys)
    fw.op('dve', lambda: nc.vector.reciprocal(out=rs, in_=rs), r=keys, w=keys)
    return keys, rs


def prenorm(C, l, g, xts, hT, hkey):
    nc, fw = C.nc, C.fw
    V, A, PE = nc.vector, nc.scalar, nc.tensor
    n = len(xts)
    for i, (xk, xt) in enumerate(xts):
        ks, ss = stat_col(C)
        fw.op('act', lambda: A.activation(out=C.junk[:], in_=xt, func=AF.Square, accum_out=ss), r=[xk], w=[ks])
        kr, rs = rstd_col(C, [ks], ss, 1, 1024.0)
        fw.op('dve', lambda: V.tensor_scalar(out=C.xn[:, i, :], in0=xt, scalar1=rs, scalar2=None, op0=ALU.mult),
              r=[xk] + kr, w=[('xn', i)])
    for cp in range(4):
        for c in (2 * cp, 2 * cp + 1):
            for i in range(n):
                f = lambda: PE.transpose(out=C.psT[:, (c % 2) * 512 + i * 128:(c % 2) * 512 + (i + 1) * 128],
                                         in_=C.xn[:, i, c * 128:(c + 1) * 128], identity=C.ident_b[:])
                if c == 2 * cp + 1 and i == n - 1:
                    fw.op('pe', f, r=[('xn', i), 'ident_b'], w=['psT'])
                else:
                    fw.op_noinc('pe', f, r=[('xn', i), 'ident_b'], w=['psT'])
        for c in (2 * cp, 2 * cp + 1):
            j = l * 16 + c * 2 + g
            fw.op('dve', lambda: V.tensor_scalar(out=hT[:, c, 0:n * 128], in0=C.psT[:, (c % 2) * 512:(c % 2) * 512 + n * 128],
                                                 scalar1=C.modS[:, j:j + 1], scalar2=C.modH[:, j:j + 1], op0=ALU.mult, op1=ALU.add),
                  r=['psT', 'modS', 'modH'], w=[(hkey, c)])


def mm_acc(C, out_ap, out_key, pairs, rkeys):
    fw, PE = C.fw, C.nc.tensor
    n = len(pairs)
    for i, (l_, r_) in enumerate(pairs):
        f = lambda: PE.matmul(out_ap, lhsT=l_, rhs=r_, start=(i == 0), stop=(i == n - 1))
        if i < n - 1:
            fw.op_noinc('pe', f, r=rkeys, w=[out_key])
        else:
            fw.op('pe', f, r=rkeys, w=[out_key])


def load_w_cast(C, dst, src, nchunk, ncols, key):
    for c in range(nchunk):
        for c0 in range(0, ncols, 1024):
            c1 = min(ncols, c0 + 1024)
            C.fw.dma('pool', dst[:, c, c0:c1], src[c * 128:(c + 1) * 128, c0:c1], w=[key])


def pad_q(C, qT, qkeys, side, nq):
    fw, G = C.fw, C.nc.gpsimd
    rows = slice(0, 64) if side == 0 else slice(64, 128)
    b = C.qz_i[side]
    C.qz_i[side] = 1 - b
    fw.op('pool', lambda: G.tensor_copy(out=C.qz[rows, side, b, 0:nq], in_=qT), r=qkeys, w=[('qz', side, b)])
    return C.qz[:, side, b, 0:nq], [('qz', side, b)]


def run_attn_calls(C, calls, la=2, banks=None, extra=None):
    nxt = pad_q(C, calls[0]['q64'], calls[0]['qkeys'], calls[0]['side'], calls[0]['nq'])
    for i, c in enumerate(calls):
        cur = nxt
        box = {}
        n_ = calls[i + 1] if i + 1 < len(calls) else None
        ex = extra[i] if (extra is not None and i < len(extra)) else None

        def fl(n_=n_, box=box, ex=ex):
            if n_ is not None:
                box['v'] = pad_q(C, n_['q64'], n_['qkeys'], n_['side'], n_['nq'])
            if ex is not None:
                ex()
        attention(C, cur[0], c['nq'], cur[1], c['kts'], c['scale'], c['side'], c['dst'], c['gate'], c['dkeys'], c['gkeys'], filler=fl, la=la, banks=banks)
        nxt = box.get('v')
    if extra is not None:
        for ex in extra[len(calls):]:
            ex()


def attention(C, qT, nq, qkeys, keytiles, scale, side, dst, gate, dkeys, gkeys, pad=False, filler=None, la=2, banks=None):
    nc, fw = C.nc, C.fw
    V, A, PE, G = nc.vector, nc.scalar, nc.tensor, nc.gpsimd
    wide_saved = getattr(C, 'a_wide', False)
    C.a_wide = False
    oi = C.o_i
    C.o_i = 1 - oi
    psO = C.psO[oi]; okey = ('O', oi)
    nk = len(keytiles)
    orows = slice(0, 64) if side == 0 else slice(64, 128)
    srows = slice(64, 128) if side == 0 else slice(0, 64)
    pend = []
    if pad:
        qT, qkeys = pad_q(C, qT, qkeys, side, nq)

    def qk(idx):
        kT, va, eb, rk = keytiles[idx][:4]
        a, b_ = keytiles[idx][4] if len(keytiles[idx]) > 4 else (0, nq)
        bk = banks if banks is not None else [(C.psS[i][:, :], ('S', i)) for i in range(3)]
        si = C.s_i % len(bk)
        C.s_i = si + 1
        sap, skey = bk[si]
        fw.op('pe', lambda: PE.matmul(sap[:, a:b_], lhsT=kT, rhs=qT[:, a:b_], start=True, stop=True), r=rk + qkeys, w=[skey])
        pi = C.p_i
        C.p_i = (pi + 1) % 6
        fw.op('act', lambda: A.activation(out=C.pbuf[:, pi, a:b_], in_=sap[:, a:b_], func=AF.Exp, scale=scale),
              r=[skey], w=[('p', pi)])
        if eb is not None:
            for (ebap, c0, c1, ekeys, eng) in eb:
                e = V if eng == 'dve' else G
                fw.op(eng, lambda: e.tensor_tensor(out=C.pbuf[:, pi, c0:c1], in0=C.pbuf[:, pi, c0:c1], in1=ebap, op=ALU.mult),
                      r=[('p', pi)] + ekeys, w=[('p', pi)])
        pend.append((idx, pi))

    def pv():
        idx, pi = pend.pop(0)
        kT, va, eb, rk = keytiles[idx][:4]
        a, b_ = keytiles[idx][4] if len(keytiles[idx]) > 4 else (0, nq)
        assert idx > 0 or (a, b_) == (0, nq)
        f = lambda: PE.matmul(psO[:, a:b_], lhsT=va, rhs=C.pbuf[:, pi, a:b_], start=(idx == 0), stop=(idx == nk - 1))
        if idx == nk - 1:
            fw.op('pe', f, r=rk + [('p', pi)], w=[okey])
        else:
            fw.op_noinc('pe', f, r=rk + [('p', pi)], w=[okey])

    for idx in range(nk):
        qk(idx)
        if idx >= la:
            pv()
        if filler is not None and idx == min(3, nk - 1):
            filler()
    while pend:
        pv()
    fw.op('dve', lambda: V.reciprocal(out=C.rcb[srows, oi, 0:nq], in_=psO[srows, 0:nq]), r=[okey], w=[('rcb', oi)])
    fw.op('dve', lambda: V.tensor_tensor(out=C.rcb[orows, oi, 0:nq], in0=psO[orows, 0:nq], in1=C.rcb[srows, oi, 0:nq], op=ALU.mult),
          r=[okey, ('rcb', oi)], w=[('rcb', oi)])
    fw.op('dve', lambda: V.tensor_tensor(out=dst, in0=C.rcb[orows, oi, 0:nq], in1=gate, op=ALU.mult),
          r=[('rcb', oi)] + gkeys, w=dkeys)
    C.a_wide = wide_saved


def outproj_residual(C, l, g, AO, ao_keys, wo, n, xts, dst_rows_fn, ao_fn=None):
    nc, fw = C.nc, C.fw
    V, A, PE = nc.vector, nc.scalar, nc.tensor
    sets = [[(C.psA[:, 0:512], ('A', 0)), (C.psA[:, 512:1024], ('A', 1))],
            [(C.psS[0][:, :], ('S', 0)), (C.psS[1][:, :], ('S', 1))]]
    for i, (xk, xt) in enumerate(xts):
        hs = sets[i % 2]
        for half in range(2):
            mm_acc(C, hs[half][0], hs[half][1],
                   [((ao_fn(c, i) if ao_fn else AO[:, c, i * 128:(i + 1) * 128]), wo[:, c, half * 512:(half + 1) * 512]) for c in range(8)],
                   ao_keys + ['wo'])
        ks, ss2 = stat_cols(C, 2)
        for half in range(2):
            fw.op('act', lambda: A.activation(out=C.junk[:, 0:512], in_=hs[half][0], func=AF.Square, accum_out=ss2[:, half:half + 1]),
                  r=[hs[half][1]], w=[ks[half]])
        k1, ss = stat_col(C)
        fw.op('dve', lambda: V.tensor_tensor(out=ss, in0=ss2[:, 0:1], in1=ss2[:, 1:2], op=ALU.add), r=ks, w=[k1])
        kr, rs = rstd_col(C, [k1], ss, 1, 1024.0)
        for half in range(2):
            fw.op('dve', lambda: V.tensor_tensor(out=C.ty[:, half * 512:(half + 1) * 512], in0=hs[half][0],
                                                 in1=C.gg[:, l * 2 + g, half * 512:(half + 1) * 512], op=ALU.mult),
                  r=[hs[half][1], ('gg', l * 2 + g)], w=[('ty', half)])
        fw.op('dve', lambda: V.scalar_tensor_tensor(out=xt, in0=C.ty[:], scalar=rs, in1=xt, op0=ALU.mult, op1=ALU.add),
              r=[('ty', 0), ('ty', 1), xk] + kr, w=[xk])
        fw.dma('sp', dst_rows_fn(i), xt, r=[xk])


def normrope(C, L, z_ps, zkey, H, gain, rope, out_bf, okeys):
    nc, fw = C.nc, C.fw
    V, A = nc.vector, nc.scalar
    W = H * 64
    v3 = lambda ap: ap.rearrange("p (h d) -> p h d", d=64)
    fw.op('act', lambda: A.activation(out=L.sq[:, 0:W], in_=z_ps, func=AF.Square), r=[zkey], w=['sq'])
    ks, ssq = stat_cols(C, H)
    fw.op('dve', lambda: V.tensor_reduce(out=ssq, in_=v3(L.sq[:, 0:W]), axis=AX.X, op=ALU.add), r=['sq'], w=ks)
    kr, rs = rstd_col(C, ks, ssq, H, 64.0)
    fw.op('dve', lambda: V.tensor_tensor(out=v3(L.zg[:, 0:W]), in0=v3(z_ps), in1=rs.unsqueeze(2).broadcast_to([128, H, 64]), op=ALU.mult),
          r=[zkey] + kr, w=['zg'])
    fw.op('dve', lambda: V.tensor_tensor(out=L.zg[:, 0:W], in0=L.zg[:, 0:W], in1=gain, op=ALU.mult), r=['zg', 'gainA'], w=['zg'])
    if rope is None:
        fw.op('act', lambda: A.copy(out=out_bf, in_=L.zg[:, 0:W]), r=['zg'], w=okeys)
        return
    rk, rt = rope[0], rope[1]
    fw.op('dve', lambda: V.tensor_tensor(out=v3(L.t1[:, 0:W]), in0=v3(L.zg[:, 0:W]),
                                         in1=rt[:, 0:64].unsqueeze(1).broadcast_to([128, H, 64]), op=ALU.mult),
          r=['zg', rk], w=['t1'])
    z4 = L.zg[:, 0:W].rearrange("p (h b s d) -> p h b s d", b=2, s=2, d=16)
    t4 = L.t2[:, 0:W].rearrange("p (h b s d) -> p h b s d", b=2, s=2, d=16)
    s4 = rt[:, 64:128].rearrange("p (b s d) -> p b s d", b=2, s=2, d=16)
    for so, si in ((0, 1), (1, 0)):
        fw.op('pool', lambda: nc.gpsimd.tensor_tensor(out=t4[:, :, :, so, :], in0=z4[:, :, :, si, :],
                                                      in1=s4[:, :, so, :].unsqueeze(1).broadcast_to([128, H, 2, 16]), op=ALU.mult),
              r=['zg', rk], w=[('t2', so)])
    fw.op('dve', lambda: V.tensor_tensor(out=out_bf, in0=L.t1[:, 0:W], in1=L.t2[:, 0:W], op=ALU.add),
          r=['t1', ('t2', 0), ('t2', 1)], w=okeys)


def hkeys(name):
    return [(name, c) for c in range(8)]


def l0_kv(C, L, n, kcol0, kt0, kkey, ropes, out_row0):
    nc, fw, D = C.nc, C.fw, C.D
    V, A, PE, G = nc.vector, nc.scalar, nc.tensor, nc.gpsimd
    akeys = [('A', 0)] + ([('A', 1)] if n > 2 else [])
    for i in range(n):
        mm_acc(C, C.psA[:, i * 256:(i + 1) * 256], ('A', i // 2),
               [(L.hT[:, c, i * 128:(i + 1) * 128], L.w0[:, c, 512:768]) for c in range(8)], hkeys('hT') + ['w0'])
    W = n * 128
    pk = C.psA[:, 0:n * 256].rearrange("p (i s h d) -> p i s h d", s=2, h=2, d=64)
    zk = pk[:, :, 0, :, :]
    v4 = lambda t: t[:, 0:W].rearrange("p (i h d) -> p i h d", h=2, d=64)
    fw.op('act', lambda: A.activation(out=v4(L.sq), in_=zk, func=AF.Square), r=akeys, w=['sq'])
    ks, ssq = stat_cols(C, 2 * n)
    fw.op('dve', lambda: V.tensor_reduce(out=ssq, in_=L.sq[:, 0:W].rearrange("p (g d) -> p g d", d=64), axis=AX.X, op=ALU.add), r=['sq'], w=ks)
    kr, rs = rstd_col(C, ks, ssq, 2 * n, 64.0)
    fw.op('dve', lambda: V.tensor_tensor(out=v4(L.zg), in0=zk, in1=rs.rearrange("p (i h) -> p i h", h=2).unsqueeze(3).broadcast_to([128, n, 2, 64]),
                                         op=ALU.mult), r=akeys + kr, w=['zg'])
    z3 = L.zg[:, 0:W].rearrange("p (i c) -> p i c", c=128)
    fw.op('dve', lambda: V.tensor_tensor(out=z3, in0=z3, in1=L.gainA[:, 512:640].unsqueeze(1).broadcast_to([128, n, 128]), op=ALU.mult),
          r=['zg', 'gainA'], w=['zg'])
    if ropes is None:
        fw.op('act', lambda: A.copy(out=L.zb[:, 0:W], in_=L.zg[:, 0:W]), r=['zg'], w=['zb'])
    else:
        rk = ropes[0][0]
        rt = ropes[0][2]
        fw.op('dve', lambda: V.tensor_tensor(out=v4(L.t1), in0=v4(L.zg), in1=rt[:, 0:n, 0:64].unsqueeze(2).broadcast_to([128, n, 2, 64]),
                                             op=ALU.mult), r=['zg', rk], w=['t1'])
        for i in range(n):
            z4 = L.zg[:, i * 128:(i + 1) * 128].rearrange("p (h b s d) -> p h b s d", b=2, s=2, d=16)
            t4 = L.t2[:, i * 128:(i + 1) * 128].rearrange("p (h b s d) -> p h b s d", b=2, s=2, d=16)
            s4 = rt[:, i, 64:128].rearrange("p (b s d) -> p b s d", b=2, s=2, d=16)
            for so, si in ((0, 1), (1, 0)):
                fw.op('pool', lambda: G.tensor_tensor(out=t4[:, :, :, so, :], in0=z4[:, :, :, si, :],
                                                      in1=s4[:, :, so, :].unsqueeze(1).broadcast_to([128, 2, 2, 16]), op=ALU.mult),
                      r=['zg', rk], w=[('t2', so)])
        fw.op('dve', lambda: V.tensor_tensor(out=L.zb[:, 0:W], in0=L.t1[:, 0:W], in1=L.t2[:, 0:W], op=ALU.add),
              r=['t1', ('t2', 0), ('t2', 1)], w=['zb'])
    if out_row0 is not None:
        for i in range(n):
            fw.dma('sp', D['nak'][out_row0 + i * 128:out_row0 + (i + 1) * 128, :], L.zg[:, i * 128:(i + 1) * 128], r=['zg'])
        fw.op('act', lambda: A.copy(out=L.vt32[:, 0:W].rearrange("p (i c) -> p i c", c=128), in_=pk[:, :, 1, :, :].rearrange("p i h d -> p i (h d)")),
              r=akeys, w=['vt32'])
        for i in range(n):
            fw.dma('sp', D['nav'][out_row0 + i * 128:out_row0 + (i + 1) * 128, :], L.vt32[:, i * 128:(i + 1) * 128], r=['vt32'])
    for i in range(n):
        f = lambda: PE.transpose(out=C.psT[:, i * 128:(i + 1) * 128], in_=L.zb[:, i * 128:(i + 1) * 128], identity=C.ident_b[:])
        if i < n - 1:
            fw.op_noinc('pe', f, r=['zb', 'ident_b'], w=['psT'])
        else:
            fw.op('pe', f, r=['zb', 'ident_b'], w=['psT'])
    fw.op('act', lambda: A.copy(out=L.kT[:, kcol0:kcol0 + W], in_=C.psT[:, 0:W]), r=['psT'], w=[kkey])
    fw.op('dve', lambda: V.tensor_copy(out=L.Vt[:, kt0:kt0 + n, 0:64], in_=pk[:, :, 1, 0, :]), r=akeys, w=[kkey])
    fw.op('dve', lambda: V.tensor_copy(out=L.Vt[:, kt0:kt0 + n, 128:192], in_=pk[:, :, 1, 1, :]), r=akeys, w=[kkey])


def l0_q(C, L, n, ropes):
    nc, fw = C.nc, C.fw
    V, PE = nc.vector, nc.tensor
    for i in range(n):
        ka, pa = next_A(C)
        mm_acc(C, pa, ka, [(L.hT[:, c, i * 128:(i + 1) * 128], L.w0[:, c, 0:512]) for c in range(8)], hkeys('hT') + ['w0'])
        normrope(C, L, pa, ka, 8, L.gainA[:, 0:512], None if ropes is None else ropes[i], L.zb[:, 0:512], ['zb'])
        for j in range(4):
            f = lambda: PE.transpose(out=C.psT[:, j * 128:(j + 1) * 128], in_=L.zb[:, j * 128:(j + 1) * 128], identity=C.ident_b[:])
            if j < 3:
                fw.op_noinc('pe', f, r=['zb', 'ident_b'], w=['psT'])
            else:
                fw.op('pe', f, r=['zb', 'ident_b'], w=['psT'])
        fw.op('dve', lambda: V.tensor_copy(out=L.qT[:, :, i * 128:(i + 1) * 128], in_=C.psT[:, 0:512].rearrange("p (j t) -> p j t", j=4)),
              r=['psT'], w=['qT'])


def fm_proj(C, w, wkey, col0, nch, hT, hname, ntok, evac):
    for j in range(nch):
        ka, pa = next_A(C)
        mm_acc(C, pa[:, 0:ntok], ka, [(w[:, c, col0 + j * 128:col0 + (j + 1) * 128], hT[:, c, 0:ntok]) for c in range(8)],
               hkeys(hname) + [wkey])
        evac(j, ka, pa[:, 0:ntok])


def pool_b_steps(C, L, off, ntok, prc):
    nc, fw = C.nc, C.fw
    V, G, PE = nc.vector, nc.gpsimd, nc.tensor
    add = lambda o, a, b, r, w: fw.op('pool', lambda: G.tensor_tensor(out=o, in0=a, in1=b, op=ALU.add), r=r, w=w)
    a, b, s = L.t1, L.t2, L.sq

    def chain(g):
        u = L.ubT[:, g, :]
        w_ = POOL_W[g]
        if w_ == 2:
            add(s[:, 0:ntok], u[:, off - 1:off - 1 + ntok], u[:, off:off + ntok], ['ubT'], ['sq'])
        else:
            add(a[:, 0:ntok + 15], u[:, off - 8:off + ntok + 7], u[:, off - 7:off + ntok + 8], ['ubT'], ['t1'])
            if w_ == 4:
                add(s[:, 0:ntok], a[:, 6:6 + ntok], a[:, 8:8 + ntok], ['t1'], ['sq'])
            else:
                add(b[:, 0:ntok + 13], a[:, 0:ntok + 13], a[:, 2:ntok + 15], ['t1'], [('t2', 0), ('t2', 1)])
                if w_ == 8:
                    add(s[:, 0:ntok], b[:, 4:4 + ntok], b[:, 8:8 + ntok], [('t2', 0), ('t2', 1)], ['sq'])
                else:
                    add(a[:, 0:ntok + 9], b[:, 0:ntok + 9], b[:, 4:ntok + 13], [('t2', 0), ('t2', 1)], ['t1'])
                    add(s[:, 0:ntok], a[:, 0:ntok], a[:, 8:8 + ntok], ['t1'], ['sq'])
        fw.op('dve', lambda: V.tensor_tensor(out=s[:, 0:ntok], in0=s[:, 0:ntok], in1=prc[:, g, 0:ntok], op=ALU.mult), r=['sq', 'prc'], w=['sq'])
        fw.op('dve', lambda: V.tensor_tensor(out=L.zb[:, 0:ntok], in0=s[:, 0:ntok], in1=u[:, off:off + ntok], op=ALU.subtract),
              r=['sq', 'ubT'], w=['zb'])

    def mm(g):
        ka, pa = next_A(C)
        fw.op('pe', lambda: PE.matmul(pa[:, 0:ntok], lhsT=L.bm[:, g * 128:(g + 1) * 128], rhs=L.zb[:, 0:ntok], start=True, stop=True),
              r=['bm', 'zb'], w=[ka])
        fw.op('dve', lambda: V.scalar_tensor_tensor(out=L.AO[:, 4 + g, 0:ntok], in0=pa[:, 0:ntok], scalar=L.bsc[:, g:g + 1],
                                                    in1=L.gbT[:, g, 0:ntok], op0=ALU.mult, op1=ALU.mult),
              r=[ka, 'bsc', 'gbT'], w=[('AO', 4 + g)])

    return [lambda: chain(0)] + [(lambda g=g: (mm(g), chain(g + 1))) for g in range(3)] + [lambda: mm(3)]


def l0_attn(C, L, nq, ktlist, kkey, extra=None):
    calls = []
    for j in range(4):
        for side in (0, 1):
            rows = slice(side * 64, (side + 1) * 64)
            kts = [(L.kT[:, kt * 128:(kt + 1) * 128], L.Vt[:, kt, side * 64:side * 64 + 128], None, [kkey]) for kt in ktlist]
            calls.append(dict(q64=L.qT[rows, j, 0:nq], qkeys=['qT'], nq=nq, kts=kts, scale=0.125, side=side,
                              dst=L.AO[rows, j, 0:nq], gate=L.gaT[rows, j, 0:nq], dkeys=[('AO', j)], gkeys=['gaT']))
    run_attn_calls(C, calls, extra=extra)


def phase_l0(C, stop_after):
    nc, fw, D = C.nc, C.fw, C.D
    V, A, PE, G = nc.vector, nc.scalar, nc.tensor, nc.gpsimd
    L = Ctx()
    with ExitStack() as s1:
        L.w0 = sbt(nc, s1, 'w0', [128, 8, 2304], BF); L.wo = sbt(nc, s1, 'wo0', [128, 8, 1024], BF)
        phase_setup(C, (0,), after_issue=lambda: (load_w_cast(C, L.w0, D['w_in_e'], 8, 2304, 'w0'),
                                                   load_w_cast(C, L.wo, D['w_out_e'], 8, 1024, 'wo')))
        L.bm = sbt(nc, s1, 'bm', [128, 512], BF); L.bsc = sbt(nc, s1, 'bsc', [128, 4], F32)
        L.gainA = sbt(nc, s1, 'gainA', [128, 640], F32); L.hmask = sbt(nc, s1, 'hmask', [128, 2], F32)
        L.kT = sbt(nc, s1, 'kT', [128, 4608], BF); L.Vt = sbt(nc, s1, 'Vt', [128, 36, 192], BF)
        L.ubT = sbt(nc, s1, 'ubT', [128, 4, 2064], BF); L.hT = sbt(nc, s1, 'hT', [128, 8, 512], BF)
        L.qT = sbt(nc, s1, 'qT', [128, 4, 512], BF); L.gaT = sbt(nc, s1, 'gaT', [128, 4, 512], BF)
        L.gbT = sbt(nc, s1, 'gbT', [128, 4, 512], BF); L.AO = sbt(nc, s1, 'AO', [128, 8, 512], BF)
        L.sq = sbt(nc, s1, 'sq', [128, 640], F32); L.zg = sbt(nc, s1, 'zg', [128, 640], F32)
        L.t1 = sbt(nc, s1, 't1', [128, 640], F32); L.t2 = sbt(nc, s1, 't2', [128, 640], F32)
        L.zb = sbt(nc, s1, 'zb', [128, 640], BF); L.vt32 = sbt(nc, s1, 'vt32', [128, 256], F32)
        L.ropet = sbt(nc, s1, 'ropet', [128, 2, 4, 128], F32); L.prc = sbt(nc, s1, 'prc', [128, 4, 512], F32)
        print('sbuf remaining (l0):', nc.sbuf_bytes_remaining)
        fw.dma('pool', L.bm[:], D['b_mapT'], w=['bm'])
        fw.dma('sp', L.bsc[:], D['b_scaleT'], w=['bsc'])
        fw.dma('sp', L.gainA[:], D['gain_a'], w=['gainA'])
        fw.dma('sp', L.hmask[:], D['hmask'], w=['hmask'])
        fw.op('dve', lambda: V.memset(L.Vt[:, :, 64:128], 1.0), w=[('kv', 'all')])
        fw.dma('pool', L.kT[:, 0:512], D['cak_T'], w=[('kv', 'all')])
        cav = D['cav'].rearrange("(t p) d -> p t d", p=128)
        fw.dma('pool', L.Vt[:, 0:4, 0:64], cav[:, :, 0:64], w=[('kv', 'all')])
        fw.dma('pool', L.Vt[:, 0:4, 128:192], cav[:, :, 64:128], w=[('kv', 'all')])
        rope_src = D['rope_a'].rearrange("(g t p) d -> g p t d", t=4, p=128)
        rb = 0

        def load_rope(grp):
            nonlocal rb
            rb = 1 - rb
            fw.dma('sp', L.ropet[:, rb, :, :], rope_src[grp], w=[('rope', rb)])
            return [(('rope', rb), L.ropet[:, rb, i, :], L.ropet[:, rb, :, :]) for i in range(4)]

        for grp in range(8):
            xts = [load_x(C, D['xs'][grp * 512 + i * 128:grp * 512 + (i + 1) * 128, :]) for i in range(4)]
            prenorm(C, 0, 0, xts, L.hT, 'hT')
            ropes = load_rope(grp)
            l0_kv(C, L, 4, 512 + grp * 512, 4 + grp * 4, ('kv', 'all'), ropes, None)
            if grp < 4:
                fm_proj(C, L.w0, 'w0', 1280, 4, L.hT, 'hT', 512,
                        lambda j, ka, pa: fw.op('act', lambda: A.copy(out=L.ubT[:, j, 8 + grp * 512:8 + (grp + 1) * 512], in_=pa),
                                                r=[ka], w=['ubT']))
            elif grp == 4:
                def ev(j, ka, pa):
                    fw.op('dve', lambda: V.tensor_scalar(out=L.ubT[:, j, 0:8], in0=pa[:, 120:128], scalar1=L.hmask[:, 0:1], scalar2=None,
                                                         op0=ALU.mult), r=[ka, 'hmask'], w=['ubT'])
                    fw.op('dve', lambda: V.tensor_scalar(out=L.ubT[:, j, 2056:2064], in0=pa[:, 0:8], scalar1=L.hmask[:, 1:2], scalar2=None,
                                                         op0=ALU.mult), r=[ka, 'hmask'], w=['ubT'])
                fm_proj(C, L.w0, 'w0', 1280, 4, L.hT, 'hT', 128, ev)
        prs = D['prc_s'].rearrange("p (g t) -> p g t", g=4)
        for t in range(4):
            xts = [load_x(C, D['xs'][t * 512 + i * 128:t * 512 + (i + 1) * 128, :]) for i in range(4)]
            prenorm(C, 0, 0, xts, L.hT, 'hT')
            ropes = load_rope(t)
            l0_q(C, L, 4, ropes)
            fm_proj(C, L.w0, 'w0', 768, 4, L.hT, 'hT', 512,
                    lambda j, ka, pa: fw.op('act', lambda: A.activation(out=L.gaT[:, j, :], in_=pa, func=AF.Silu), r=[ka], w=['gaT']))
            fm_proj(C, L.w0, 'w0', 1792, 4, L.hT, 'hT', 512,
                    lambda j, ka, pa: fw.op('act', lambda: A.activation(out=L.gbT[:, j, :], in_=pa, func=AF.Silu), r=[ka], w=['gbT']))
            fw.dma('sp', L.prc[:], prs[:, :, t * 512:(t + 1) * 512], w=['prc'])
            l0_attn(C, L, 512, list(range(36)), ('kv', 'all'), extra=pool_b_steps(C, L, 8 + t * 512, 512, L.prc))
            outproj_residual(C, 0, 0, L.AO, [('AO', c) for c in range(8)], L.wo, 4, xts,
                             lambda i: D['x1s'][t * 512 + i * 128:t * 512 + (i + 1) * 128, :])
        fw.barrier()
        for o_ in (0, 264, 512, 776):
            fw.op('dve', lambda: V.memset(L.ubT[:, :, o_:o_ + 8], 0.0), w=['ubT'])
        fw.dma('sp', L.prc[:, :, 0:256], D['prc_p'].rearrange("p (g t) -> p g t", g=4), w=['prc'])
        fw.barrier()
        fw.nskeys = {'hT', 'qT', 'gaT', 'gbT', 'AO', 'ubT'}

        def half_ctx(b):
            Lb = Ctx()
            Lb.__dict__.update(L.__dict__)
            for nm in ('hT', 'qT', 'gaT', 'gbT', 'AO'):
                setattr(Lb, nm, getattr(L, nm)[:, :, b * 256:(b + 1) * 256])
            Lb.ubT = L.ubT[:, :, b * 512:b * 512 + 272]
            return Lb

        def prompt_gen(Lb, s):
            xts = [load_x(C, D['xp'][s * 256 + i * 128:s * 256 + (i + 1) * 128, :]) for i in range(2)]
            prenorm(C, 0, 1, xts, Lb.hT, 'hT')
            yield
            reg = s % 2
            l0_kv(C, Lb, 2, reg * 256, reg * 2, ('kv', reg), None, s * 256)
            yield
            fm_proj(C, L.w0, 'w0', 1280, 4, Lb.hT, 'hT', 256,
                    lambda j, ka, pa: fw.op('act', lambda: A.copy(out=Lb.ubT[:, j, 8:264], in_=pa), r=[ka], w=['ubT']))
            yield
            l0_q(C, Lb, 2, None)
            yield
            fm_proj(C, L.w0, 'w0', 768, 4, Lb.hT, 'hT', 256,
                    lambda j, ka, pa: fw.op('act', lambda: A.activation(out=Lb.gaT[:, j, 0:256], in_=pa, func=AF.Silu), r=[ka], w=['gaT']))
            yield
            fm_proj(C, L.w0, 'w0', 1792, 4, Lb.hT, 'hT', 256,
                    lambda j, ka, pa: fw.op('act', lambda: A.activation(out=Lb.gbT[:, j, 0:256], in_=pa, func=AF.Silu), r=[ka], w=['gbT']))
            yield
            l0_attn(C, Lb, 256, [reg * 2, reg * 2 + 1], ('kv', reg), extra=pool_b_steps(C, Lb, 8, 256, L.prc))
            yield
            outproj_residual(C, 0, 1, Lb.AO, [('AO', c) for c in range(8)], L.wo, 2, xts,
                             lambda i: D['x1p'][s * 256 + i * 128:s * 256 + (i + 1) * 128, :])

        for pair in range(2):
            alive = [(b, prompt_gen(half_ctx(b), 2 * pair + b)) for b in range(2)]
            while alive:
                for item in list(alive):
                    fw.ns = 'p%d' % item[0]
                    try:
                        next(item[1])
                    except StopIteration:
                        alive.remove(item)
            fw.ns = None
        fw.nskeys = set()
        fw.barrier()


def _rope_tables(tok, half_dims):
    R = half_dims * 2
    half = R // 2
    inv = (10000.0 ** (-np.arange(0, half, 2, dtype=np.float32) / np.float32(half))).astype(np.float32)
    row = (tok // 64).astype(np.float32)[:, None] * inv[None, :]
    col = (tok % 64).astype(np.float32)[:, None] * inv[None, :]
    cr, sr, cc, sc = np.cos(row), np.sin(row), np.cos(col), np.sin(col)
    cos = np.concatenate([cr, cr, cc, cc], axis=1).astype(np.float32)
    sin = np.concatenate([-sr, sr, -sc, sc], axis=1).astype(np.float32)
    return cos, sin


def _pool_rc(tok, S):
    out = []
    for w in POOL_W:
        lo = np.clip(tok - w // 2, 0, S)
        hi = np.clip(tok + w - w // 2, 0, S)
        out.append((1.0 / (hi - lo).astype(np.float32)).astype(np.float32))
    return np.concatenate(out)


def _rep(v, n=128):
    return np.ascontiguousarray(np.broadcast_to(np.asarray(v, np.float32)[None, :], (n, len(v))))


def _na_bias_tables(rpb, half):
    H = 8
    kc = np.arange(64)
    qc = np.arange(64)
    c0 = np.clip(qc - 8, 0, 48)
    colvalid = (kc[:, None] >= c0[None, :]) & (kc[:, None] < c0[None, :] + 16)
    dc = kc[:, None] - qc[None, :] + 15
    dcc = np.clip(dc, 0, 30)
    out = np.full((H, 2, 64, 4480), NEG, np.float32)

    def fill(dst, delta_ok, dr):
        if not delta_ok:
            return
        vals = rpb[:, dr, :][:, dcc]
        dst[...] = np.where(colvalid[None], vals, NEG)

    for kr2 in range(2):
        for e in range(22):
            delta = kr2 + 10 - e
            fill(out[:, kr2, :, e * 64:(e + 1) * 64], -4 <= delta <= 3, delta + 7 if -4 <= delta <= 3 else 0)
    base = 32 * half

    def exact(dst, qpos, kpos):
        qr = base + qpos
        kr = base + kpos
        if kr < 0 or kr > 63:
            return
        r0 = min(max(qr - 4, 0), 56)
        ok = r0 <= kr < r0 + 8
        fill(dst, ok, kr - qr + 7 if ok else 0)

    for m in range(6):
        for kr2 in range(2):
            for q in range(4):
                exact(out[:, kr2, :, 1408 + m * 256 + q * 64:1408 + m * 256 + (q + 1) * 64], q, -4 + 2 * m + kr2)
    for mi, m in enumerate(range(2, 8)):
        for kr2 in range(2):
            for q in range(4):
                exact(out[:, kr2, :, 2944 + mi * 256 + q * 64:2944 + mi * 256 + (q + 1) * 64], 28 + q, 20 + 2 * m + kr2)
    return np.ascontiguousarray(out.reshape(H, 128, 4480))


def prepare_inputs(inp):
    f = lambda a: np.ascontiguousarray(np.asarray(a), dtype=np.float32)
    xp_, xs_ = f(inp['x_prompt']), f(inp['x_sample'])
    c, c_ctx = f(inp['c']), f(inp['c_ctx'])
    g_pre, g_post = f(inp['g_pre']), f(inp['g_post'])
    com = {}
    com['w_mod'] = f(inp['w_mod']); com['b_mod'] = f(inp['b_mod']); com['g_post'] = g_post
    gpT = np.zeros((128, 32), np.float32)
    for l in range(2):
        for ch in range(8):
            for g in range(2):
                gpT[:, l * 16 + ch * 2 + g] = g_pre[l, ch * 128:(ch + 1) * 128]
    com['g_preT'] = gpT
    sel = np.zeros((2, 256), np.float32); sel[0, 0:128] = 1; sel[1, 128:256] = 1
    com['sel'] = sel
    We = f(inp['w_in_e'])[0]
    com['w_in_e'] = np.ascontiguousarray(np.concatenate(
        [We[:, 0:512][:, PERM_A], We[:, 512:768], We[:, 768:1280][:, PERM_A], We[:, 1280:2304]], axis=1))
    com['gain_a'] = _rep(np.concatenate([np.tile(f(inp['a_q_norm'])[0], 8), np.tile(f(inp['a_k_norm'])[0], 2)]))
    bmap = f(inp['b_map'])[0]
    com['b_mapT'] = np.ascontiguousarray(bmap.transpose(1, 0, 2).reshape(128, 512))
    com['b_scaleT'] = np.ascontiguousarray(f(inp['b_scale'])[0].reshape(4, 128).T)
    Woe = f(inp['w_out_e'])[0]
    com['w_out_e'] = np.ascontiguousarray(np.concatenate([Woe[0:512][PERM_A], Woe[512:1024]], axis=0))
    tokp = np.arange(256)
    com['prc_p'] = _rep(_pool_rc(tokp, 256))
    Wo = f(inp['w_in_o'])[0]
    com['w1_fm'] = np.ascontiguousarray(np.concatenate([Wo[:, 0:512], Wo[:, 512:1024], Wo[:, 1536:2048], Wo[:, 2720:3232]], axis=1))
    com['w1_tm'] = np.ascontiguousarray(np.concatenate([Wo[:, 512:1024], Wo[:, 1024:1536], Wo[:, 2048:2432], Wo[:, 2432:2688],
                                                        Wo[:, 2688:2720]], axis=1))
    kpe_w = Wo[:, 2688:2720]
    swp = np.r_[8:16, 0:8, 24:32, 16:24]
    z64 = np.zeros((1024, 64), np.float32)
    com['w_kpe2'] = np.ascontiguousarray(np.concatenate([z64, kpe_w, z64, kpe_w[:, swp]], axis=1))
    com['gain_q'] = _rep(f(inp['d_q_norm'])[0]); com['gain_kv'] = _rep(f(inp['d_kv_norm'])[0])
    Wuq = f(inp['d_w_uq'])[0].reshape(384, 8, 96)
    Wuq_b = np.concatenate([Wuq[:, :, 0:64], Wuq[:, :, 64:96][:, :, swp]], axis=2)
    com['w_uq_ab'] = np.ascontiguousarray(np.concatenate([Wuq.reshape(384, 768), Wuq_b.reshape(384, 768)], axis=1))
    Wukv = f(inp['d_w_ukv'])[0].reshape(256, 8, 128)
    com['w_uk'] = np.ascontiguousarray(Wukv[:, :, 0:64].reshape(256, 512))
    com['w_uv'] = np.ascontiguousarray(Wukv[:, :, 64:128].reshape(256, 512))
    com['w_out_o'] = f(inp['w_out_o'])[0]
    rpb = f(inp['c_rpb'])[0]
    na = [_na_bias_tables(rpb, 0), _na_bias_tables(rpb, 1)]
    maps = []
    for r in range(8):
        b, half = r // 2, r % 2
        m = dict(com)
        own = np.arange(half * 2048, (half + 1) * 2048)
        other = np.arange(2048, 4096) if half == 0 else np.concatenate([np.arange(1920, 2048), np.arange(0, 1920)])
        tok = np.concatenate([own, other])
        m['xs'] = np.ascontiguousarray(xs_[b][tok])
        m['xp'] = np.ascontiguousarray(xp_[4 * r:4 * r + 4].reshape(1024, 1024))
        cond = np.stack([c[b], c_ctx], axis=0)
        m['condT'] = np.ascontiguousarray(cond.reshape(2, 8, 128).transpose(2, 1, 0).reshape(128, 16))
        cos, sin = _rope_tables(tok, 32)
        m['rope_a'] = np.ascontiguousarray(np.concatenate([cos, sin], axis=1))
        m['prc_s'] = _rep(_pool_rc(own, 4096))
        m['hmask'] = _rep(np.array([0.0, 1.0] if half == 0 else [1.0, 0.0], np.float32))
        m['cak_T'] = np.ascontiguousarray(f(inp['cache_a_k'])[b, 0].reshape(512, 128).T)
        m['cav'] = np.ascontiguousarray(f(inp['cache_a_v'])[b, 0].reshape(512, 128))
        cd, sd = _rope_tables(own, 16)
        rd = np.zeros((128, 2, 2048), np.float32)
        rd[64:96, 0, :] = cd.T; rd[64:96, 1, :] = sd.T
        m['rope_d'] = np.ascontiguousarray(rd.reshape(128, 4096))
        m['cck_T'] = np.ascontiguousarray(f(inp['cache_c_k'])[b, 0].reshape(512, 512).T)
        m['ccv'] = np.ascontiguousarray(f(inp['cache_c_v'])[b, 0].reshape(512, 512))
        m['cckv_T'] = np.ascontiguousarray(f(inp['cache_d_ckv'])[b, 0].T)
        kp = np.zeros((128, 512), np.float32); kp[64:96] = f(inp['cache_d_kpe'])[b, 0].T
        m['ckpe_T'] = kp
        m['nabias'] = na[half]
        maps.append(m)
    return maps


_NC_CACHE = {}


def kernel(**inputs):
    maps = prepare_inputs(inputs)
    if 'nc' not in _NC_CACHE:
        _NC_CACHE['nc'] = build_program()
    res = run_bass_kernel_spmd(_NC_CACHE['nc'], maps, core_ids=list(range(8)))
    R = res.results
    y_p = np.stack([R[r]['y_p'].reshape(4, 256, 1024) for r in range(8)]).reshape(32, 256, 1024)
    y_s = np.stack([R[r]['y_s'] for r in range(8)]).reshape(4, 4096, 1024)
    cat = lambda k, shp: np.ascontiguousarray(np.stack([R[r][k] for r in range(8)]).reshape(shp)).astype(np.float32)
    return (y_p.astype(np.float32), y_s.astype(np.float32),
            cat('nak', (32, 1, 256, 2, 64)), cat('nav', (32, 1, 256, 2, 64)),
            cat('nck', (32, 1, 256, 8, 64)), cat('ncv', (32, 1, 256, 8, 64)),
            cat('nckv', (32, 1, 256, 256)), cat('nkpe', (32, 1, 256, 32)))


def rms_tok(C, ps_ap, pkey, W, gain, gkey, out_ap, okeys):
    nc, fw = C.nc, C.fw
    ks, ss = stat_col(C)
    fw.op('act', lambda: nc.scalar.activation(out=C.junk[:, 0:W], in_=ps_ap, func=AF.Square, accum_out=ss), r=[pkey], w=[ks])
    kr, rs = rstd_col(C, [ks], ss, 1, float(W))
    fw.op('dve', lambda: nc.vector.scalar_tensor_tensor(out=out_ap, in0=ps_ap, scalar=rs, in1=gain, op0=ALU.mult, op1=ALU.mult),
          r=[pkey, gkey] + kr, w=okeys)


def transpose_to(C, src_bf, skey, nblk, dst_fn, dkeys):
    fw, PE, V = C.fw, C.nc.tensor, C.nc.vector
    for k in range(nblk):
        f = lambda: PE.transpose(out=C.psT[:, k * 128:(k + 1) * 128], in_=src_bf[:, k * 128:(k + 1) * 128], identity=C.ident_b[:])
        if k < nblk - 1:
            fw.op_noinc('pe', f, r=[skey, 'ident_b'], w=['psT'])
        else:
            fw.op('pe', f, r=[skey, 'ident_b'], w=['psT'])
    fw.op('dve', lambda: V.tensor_copy(out=dst_fn(), in_=C.psT[:, 0:nblk * 128].rearrange("p (k t) -> p k t", k=nblk)),
          r=['psT'], w=dkeys)


def vpair_copy(C, dst_tile_ap, src_ps, skey, dkeys, npair):
    fw, V = C.fw, C.nc.vector
    d4 = dst_tile_ap.rearrange("p (j a d) -> p j a d", j=npair, a=3, d=64)
    s4 = src_ps.rearrange("p (j a d) -> p j a d", j=npair, a=2, d=64)
    fw.op('dve', lambda: V.tensor_copy(out=d4[:, :, 0, :], in_=s4[:, :, 0, :]), r=[skey], w=dkeys)
    fw.op('dve', lambda: V.tensor_copy(out=d4[:, :, 2, :], in_=s4[:, :, 1, :]), r=[skey], w=dkeys)


def phase_l1(C, stop_after):
    nc, fw, D = C.nc, C.fw, C.D
    V, A, PE, G = nc.vector, nc.scalar, nc.tensor, nc.gpsimd
    L = Ctx()
    xinB_flat = D['xinB'].rearrange("r c -> (r c)")
    KCv = xinB_flat[OFF_KC:OFF_KC + 512 * 512].rearrange("(j p t) -> p j t", j=4, p=128, t=512)
    VCv = xinB_flat[OFF_VC:OFF_VC + 512 * 512].rearrange("(n p c) -> p n c", n=4, p=128, c=512)
    xout_flat = D['xout'].rearrange("r c -> (r c)")
    RK = XIN_ROWS * 2048
    with ExitStack() as sa:
        L.w1 = sbt(nc, sa, 'w1', [128, 8, 2048], BF); L.w1t = sbt(nc, sa, 'w1t', [128, 8, 1696], BF)
        L.wk2 = sbt(nc, sa, 'wk2', [128, 8, 192], BF); L.gq = sbt(nc, sa, 'gq', [128, 384], F32); L.gkv = sbt(nc, sa, 'gkv', [128, 256], F32)
        L.hT = sbt(nc, sa, 'hT1', [128, 8, 512], BF)
        L.cqb = sbt(nc, sa, 'cqb', [128, 384], BF); L.ckvb = sbt(nc, sa, 'ckvb', [128, 256], BF)
        sa1 = sa.enter_context(ExitStack())
        L.stg = sbt(nc, sa1, 'stg', [128, 2, 4, 512], BF); L.stgv = sbt(nc, sa1, 'stgv', [128, 2, 512], BF)
        L.stgq = sbt(nc, sa1, 'stgq', [128, 3, 512], BF); L.stgc = sbt(nc, sa1, 'stgc', [128, 2, 512], BF)
        L.stgk = sbt(nc, sa1, 'stgk', [128, 512], BF); L.rd = sbt(nc, sa1, 'rd', [128, 2, 512], F32)
        L.r1 = sbt(nc, sa1, 'r1', [128, 512], F32); L.r2 = sbt(nc, sa1, 'r2', [128, 512], F32)
        L.f32a = C.ty; f32b = C.rcb[:, 0, 0:288]
        print('sbuf remaining (l1 phase A):', nc.sbuf_bytes_remaining)
        fw.dma('sp', L.gq[:], D['gain_q'], w=['gq'])
        fw.dma('sp', L.gkv[:], D['gain_kv'], w=['gkv'])
        phase_setup(C, (1,), stack=sa1, after_issue=lambda: (load_w_cast(C, L.w1, D['w1_fm'], 8, 2048, 'w1'),
                                                            load_w_cast(C, L.w1t, D['w1_tm'], 8, 1696, 'w1t'),
                                                            load_w_cast(C, L.wk2, D['w_kpe2'], 8, 192, 'wk2')))
        rdv = D['rope_d'].rearrange("p (a t) -> p a t", a=2)
        sg = 0
        for t in range(4):
            tc0, tc1 = t * 512, (t + 1) * 512
            xts = [load_x(C, D['x1s'][tc0 + i * 128:tc0 + (i + 1) * 128, :]) for i in range(4)]
            prenorm(C, 1, 0, xts, L.hT, 'hT')
            fw.dma('sp', L.rd[:], rdv[:, :, tc0:tc1], w=['rd'])
            for grp, (dname, silu) in enumerate((('s_qc', False), ('s_kc', False), ('s_gc', True), ('s_gd', True))):
                sg = 1 - sg
                sgk = ('stg', sg)

                def ev(j, ka, pa, sg=sg, silu=silu, sgk=sgk):
                    if silu:
                        fw.op('act', lambda: A.activation(out=L.stg[:, sg, j, :], in_=pa, func=AF.Silu), r=[ka], w=[sgk])
                    else:
                        fw.op('act', lambda: A.copy(out=L.stg[:, sg, j, :], in_=pa), r=[ka], w=[sgk])
                fm_proj(C, L.w1, 'w1', grp * 512, 4, L.hT, 'hT', 512, ev)
                fw.dma('sp', D[dname].rearrange("(j p) t -> p j t", p=128)[:, :, tc0:tc1], L.stg[:, sg, :, :], r=[sgk], w=[dname])
                if dname == 's_kc' and t == 0:
                    fw.dma('sp', KCv[:, :, 0:256], L.stg[:, sg, :, 0:256], r=[sgk], w=['xinB'])
                if dname == 's_kc' and t == 3:
                    fw.dma('sp', KCv[:, :, 256:512], L.stg[:, sg, :, 256:512], r=[sgk], w=['xinB'])
            k0, p0 = next_A(C)
            mm_acc(C, p0[0:96, :], k0, [(L.wk2[:, c, 0:96], L.hT[:, c, :]) for c in range(8)], hkeys('hT') + ['wk2'])
            k1, p1 = next_A(C)
            mm_acc(C, p1[0:96, :], k1, [(L.wk2[:, c, 96:192], L.hT[:, c, :]) for c in range(8)], hkeys('hT') + ['wk2'])
            fw.op('dve', lambda: V.tensor_tensor(out=L.r1[64:96, :], in0=p0[64:96, :], in1=L.rd[64:96, 0, :], op=ALU.mult), r=[k0, 'rd'], w=['r1'])
            fw.op('dve', lambda: V.tensor_tensor(out=L.r2[64:96, :], in0=p1[64:96, :], in1=L.rd[64:96, 1, :], op=ALU.mult), r=[k1, 'rd'], w=['r2'])
            fw.op('dve', lambda: V.tensor_tensor(out=L.stgk[64:96, :], in0=L.r1[64:96, :], in1=L.r2[64:96, :], op=ALU.add), r=['r1', 'r2'], w=['stgk'])
            fw.dma('sp', D['xin'][256:288, tc0:tc1], L.stgk[64:96, :], r=['stgk'], w=['xin'])
            for i in range(4):
                sv = i % 2
                ka, pa = next_A(C)
                mm_acc(C, pa, ka, [(L.hT[:, c, i * 128:(i + 1) * 128], L.w1t[:, c, 512:1024]) for c in range(8)], hkeys('hT') + ['w1t'])
                fw.op('act', lambda: A.copy(out=L.stgv[:, sv, :], in_=pa), r=[ka], w=[('stgv', sv)])
                fw.dma('sp', D['s_vc'][tc0 + i * 128:tc0 + (i + 1) * 128, :], L.stgv[:, sv, :], r=[('stgv', sv)], w=['s_vc'])
                if (t == 0 and i < 2) or (t == 3 and i >= 2):
                    fw.dma('sp', VCv[:, i, :], L.stgv[:, sv, :], r=[('stgv', sv)], w=['xinB'])
                ka, pa = next_A(C)
                mm_acc(C, pa[:, 0:384], ka, [(L.hT[:, c, i * 128:(i + 1) * 128], L.w1t[:, c, 1024:1408]) for c in range(8)], hkeys('hT') + ['w1t'])
                rms_tok(C, pa[:, 0:384], ka, 384, L.gq[:], 'gq', L.cqb[:], ['cqb'])
                transpose_to(C, L.cqb, 'cqb', 3, lambda: L.stgq[:, :, i * 128:(i + 1) * 128], ['stgq'])
                ka, pa = next_A(C)
                mm_acc(C, pa[:, 0:256], ka, [(L.hT[:, c, i * 128:(i + 1) * 128], L.w1t[:, c, 1408:1664]) for c in range(8)], hkeys('hT') + ['w1t'])
                rms_tok(C, pa[:, 0:256], ka, 256, L.gkv[:], 'gkv', L.ckvb[:], ['ckvb'])
                transpose_to(C, L.ckvb, 'ckvb', 2, lambda: L.stgc[:, :, i * 128:(i + 1) * 128], ['stgc'])
            fw.dma('sp', D['s_cq'].rearrange("(k p) t -> p k t", p=128)[:, :, tc0:tc1], L.stgq[:], r=['stgq'], w=['s_cq'])
            fw.dma('sp', D['xin'][0:256, :].rearrange("(k p) t -> p k t", p=128)[:, :, tc0:tc1], L.stgc[:], r=['stgc'], w=['xin'])
        if stop_after == 'l1a0':
            return
        fw.custom('pool', lambda: G.collective_compute("AllGather", ALU.bypass, replica_groups=[[0, 1], [2, 3], [4, 5], [6, 7]],
                                                       ins=[D['xin']], outs=[D['xout']]), C.cc_sem, 1, r=['xin'], w=['xout'])
        fw.custom('pool', lambda: G.collective_compute("AllGather", ALU.bypass, replica_groups=[[0, 1], [2, 3], [4, 5], [6, 7]],
                                                       ins=[D['xinB']], outs=[D['xoutB']]), C.cc_sem2, 1, r=['xinB'], w=['xoutB'])
        xout_tok = (fw.state['xout'], fw.state['xoutB'])
        if stop_after == 'l1a':
            return
        fw.barrier()
        fw.state['xout'], fw.state['xoutB'] = xout_tok
        sa1.close()
        sa2 = sa.enter_context(ExitStack())
        L.wuq = sbt(nc, sa2, 'wuq', [128, 3, 1536], BF); L.wuk = sbt(nc, sa2, 'wuk', [128, 2, 512], BF); L.wuv = sbt(nc, sa2, 'wuv', [128, 2, 512], BF)
        L.wo = sbt(nc, sa2, 'wo1', [128, 8, 1024], BF)
        L.qcp = sbt(nc, sa2, 'qcp', [128, 4, 256], BF); L.kcp = sbt(nc, sa2, 'kcp', [128, 4, 256], BF)
        L.gcp = sbt(nc, sa2, 'gcp', [128, 4, 256], BF); L.gdp = sbt(nc, sa2, 'gdp', [128, 4, 256], BF)
        L.kpep = sbt(nc, sa2, 'kpep', [128, 256], BF); L.Vcp = sbt(nc, sa2, 'Vcp', [128, 2, 768], BF)
        L.cqTp = sbt(nc, sa2, 'cqTp', [128, 3, 256], BF); L.ckvTp = sbt(nc, sa2, 'ckvTp', [128, 2, 256], BF)
        L.kThp = sbt(nc, sa2, 'kThp', [128, 2, 256], BF); L.Vdp = sbt(nc, sa2, 'Vdp', [128, 2, 192], BF)
        L.qdT = sbt(nc, sa2, 'qdT', [128, 2, 512], BF); L.AOp = sbt(nc, sa2, 'AOp', [128, 8, 256], BF)
        L.f32a = sbt(nc, sa2, 'f32a', [128, 1024], F32)
        print('sbuf remaining (l1 prompts):', nc.sbuf_bytes_remaining)
        load_w_cast(C, L.wuq, D['w_uq_ab'], 3, 1536, 'wuq')
        load_w_cast(C, L.wuk, D['w_uk'], 2, 512, 'wuk')
        load_w_cast(C, L.wuv, D['w_uv'], 2, 512, 'wuv')
        load_w_cast(C, L.wo, D['w_out_o'], 8, 1024, 'wo')
        fw.op('dve', lambda: V.memset(L.Vcp[:], 1.0), w=['Vcp'])
        fw.op('dve', lambda: V.memset(L.Vdp[:], 1.0), w=['Vdp'])
        fw.op('dve', lambda: V.memset(L.kThp[:], 0.0), w=[('kThp', 0), ('kThp', 1)])
        fw.op('dve', lambda: V.memset(L.qdT[:], 0.0), w=[('qdT', 0), ('qdT', 1)])
        for s in range(4):
            r0 = s * 256
            xts = [load_x(C, D['x1p'][r0 + i * 128:r0 + (i + 1) * 128, :]) for i in range(2)]
            prenorm(C, 1, 1, xts, L.hT, 'hT')
            for grp, (dst, silu, dk) in enumerate(((L.qcp, False, 'qcp'), (L.kcp, False, 'kcp'), (L.gcp, True, 'gcp'), (L.gdp, True, 'gdp'))):
                def ev(j, ka, pa, dst=dst, silu=silu, dk=dk):
                    if silu:
                        fw.op('act', lambda: A.activation(out=dst[:, j, :], in_=pa, func=AF.Silu), r=[ka], w=[dk])
                    else:
                        fw.op('act', lambda: A.copy(out=dst[:, j, :], in_=pa), r=[ka], w=[dk])
                fm_proj(C, L.w1, 'w1', grp * 512, 4, L.hT, 'hT', 256, ev)
            k0, p0 = next_A(C)
            mm_acc(C, p0[0:96, 0:256], k0, [(L.wk2[:, c, 0:96], L.hT[:, c, 0:256]) for c in range(8)], hkeys('hT') + ['wk2'])
            fw.op('dve', lambda: V.tensor_copy(out=L.kpep[64:96, :], in_=p0[64:96, 0:256]), r=[k0], w=['kpep'])
            if stop_after == 'l1p1':
                fw.barrier(); return
            for i in range(2):
                tk = slice(i * 128, (i + 1) * 128)
                for half in range(2):
                    mm_acc(C, C.psA[:, half * 512:(half + 1) * 512], ('A', half),
                           [(L.hT[:, c, tk], L.w1t[:, c, half * 512:(half + 1) * 512]) for c in range(8)], hkeys('hT') + ['w1t'])
                fw.op('act', lambda: A.copy(out=L.f32a[:, 0:512], in_=C.psA[:, 0:512]), r=[('A', 0)], w=['f32a'])
                fw.op('act', lambda: A.copy(out=L.f32a[:, 512:1024], in_=C.psA[:, 512:1024]), r=[('A', 1)], w=['f32a'])
                fw.dma('sp', D['nck'][r0 + i * 128:r0 + (i + 1) * 128, :], L.f32a[:, 0:512], r=['f32a'])
                fw.dma('sp', D['ncv'][r0 + i * 128:r0 + (i + 1) * 128, :], L.f32a[:, 512:1024], r=['f32a'])
                vpair_copy(C, L.Vcp[:, i, :], C.psA[:, 512:1024], ('A', 1), ['Vcp'], 4)
                if stop_after == 'l1p1a':
                    fw.barrier(); return
                mm_acc(C, C.psA[:, 0:384], ('A', 0), [(L.hT[:, c, tk], L.w1t[:, c, 1024:1408]) for c in range(8)], hkeys('hT') + ['w1t'])
                rms_tok(C, C.psA[:, 0:384], ('A', 0), 384, L.gq[:], 'gq', L.cqb[:], ['cqb'])
                transpose_to(C, L.cqb, 'cqb', 3, lambda: L.cqTp[:, :, tk], ['cqTp'])
                if stop_after == 'l1p1b':
                    fw.barrier(); return
                mm_acc(C, C.psA[:, 512:800], ('A', 1), [(L.hT[:, c, tk], L.w1t[:, c, 1408:1696]) for c in range(8)], hkeys('hT') + ['w1t'])
                rms_tok(C, C.psA[:, 512:768], ('A', 1), 256, L.gkv[:], 'gkv', f32b[:, 0:256], [('rcb', 0)])
                fw.op('act', lambda: A.copy(out=f32b[:, 256:288], in_=C.psA[:, 768:800]), r=[('A', 1)], w=[('rcb', 0)])
                fw.dma('sp', D['nckv'][r0 + i * 128:r0 + (i + 1) * 128, :], f32b[:, 0:256], r=[('rcb', 0)])
                fw.dma('sp', D['nkpe'][r0 + i * 128:r0 + (i + 1) * 128, :], f32b[:, 256:288], r=[('rcb', 0)])
                fw.op('act', lambda: A.copy(out=L.ckvb[:], in_=f32b[:, 0:256]), r=[('rcb', 0)], w=['ckvb'])
                transpose_to(C, L.ckvb, 'ckvb', 2, lambda: L.ckvTp[:, :, tk], ['ckvTp'])
            if stop_after == 'l1p2':
                fw.barrier(); return
            calls = []
            for h in range(8):
                j, side = h // 2, h % 2
                rows = slice(side * 64, (side + 1) * 64)
                kts = [(L.kcp[:, j, kt * 128:(kt + 1) * 128], L.Vcp[:, kt, j * 192 + side * 64:j * 192 + side * 64 + 128], None,
                        ['kcp', 'Vcp']) for kt in range(2)]
                calls.append(dict(q64=L.qcp[rows, j, :], qkeys=['qcp'], nq=256, kts=kts, scale=0.125, side=side,
                                  dst=L.AOp[rows, j, :], gate=L.gcp[rows, j, :], dkeys=[('AOp', j)], gkeys=['gcp']))
            run_attn_calls(C, calls)
            if stop_after == 'l1p3':
                fw.barrier(); return
            for j in range(4):
                for kt in range(2):
                    ka, pa = next_A(C)
                    mm_acc(C, pa[:, 0:128], ka, [(L.ckvTp[:, c, kt * 128:(kt + 1) * 128], L.wuv[:, c, j * 128:(j + 1) * 128]) for c in range(2)],
                           ['ckvTp', 'wuv'])
                    vpair_copy(C, L.Vdp[:, kt, :], pa[:, 0:128], ka, ['Vdp'], 1)
                for hh in range(2):
                    h = 2 * j + hh
                    ka, pa = next_A(C)
                    mm_acc(C, pa[0:64, 0:256], ka, [(L.wuk[:, c, h * 64:(h + 1) * 64], L.ckvTp[:, c, :]) for c in range(2)], ['ckvTp', 'wuk'])
                    fw.op('act', lambda: A.copy(out=L.kThp[0:64, hh, :], in_=pa[0:64, 0:256]), r=[ka], w=[('kThp', hh)])
                    fw.op('pool', lambda: G.tensor_copy(out=L.kThp[64:96, hh, :], in_=L.kpep[64:96, :]), r=['kpep'], w=[('kThp', hh)])
                    ka, pa = next_A(C)
                    mm_acc(C, pa[0:96, 0:256], ka, [(L.wuq[:, c, h * 96:(h + 1) * 96], L.cqTp[:, c, :]) for c in range(3)], ['cqTp', 'wuq'])
                    fw.op('act', lambda: A.copy(out=L.qdT[0:64, hh, 0:256], in_=pa[0:64, 0:256]), r=[ka], w=[('qdT', hh)])
                    fw.op('act', lambda: A.copy(out=L.qdT[64:96, hh, 0:256], in_=pa[64:96, 0:256]), r=[ka], w=[('qdT', hh)])
                    rows = slice(hh * 64, (hh + 1) * 64)
                    kts = [(L.kThp[:, hh, kt * 128:(kt + 1) * 128], L.Vdp[:, kt, hh * 64:hh * 64 + 128], None, [('kThp', hh), 'Vdp'])
                           for kt in range(2)]
                    attention(C, L.qdT[:, hh, 0:256], 256, [('qdT', hh)], kts, 96.0 ** -0.5, hh, L.AOp[rows, 4 + j, :], L.gdp[rows, j, :],
                              [('AOp', 4 + j)], ['gdp'])
            if stop_after == 'l1p4':
                fw.barrier(); return
            outproj_residual(C, 1, 1, L.AOp, [('AOp', c) for c in range(8)], L.wo, 2, xts,
                             lambda i: D['y_p'][r0 + i * 128:r0 + (i + 1) * 128, :])
        fw.barrier()
        fw.state['xout'], fw.state['xoutB'] = xout_tok
    if stop_after == 'l1p':
        return
    phase_l1_sample(C, xout_flat, RK)


def phase_l1_sample(C, xout_flat, RK):
    nc, fw, D = C.nc, C.fw, C.D
    V, A, PE, G = nc.vector, nc.scalar, nc.tensor, nc.gpsimd
    with ExitStack() as so:
        AOc = sbt(nc, so, 'AOc', [128, 4, 2048], BF)
        AOd = sbt(nc, so, 'AOd', [128, 4, 2048], BF)
        fw.dma('sp', AOc[:], D['s_gc'].rearrange("(j p) t -> p j t", p=128), w=[('AOc', j, t) for j in range(4) for t in range(4)])
        fw.dma('sp', AOd[:], D['s_gd'].rearrange("(j p) t -> p j t", p=128), w=[('AOd', j, t) for j in range(4) for t in range(4)])
        with ExitStack() as sc:
            kc = sbt(nc, sc, 'kcA', [128, 4, 3072], BF)
            Vc = sbt(nc, sc, 'VcA', [128, 24, 768], BF)
            qc = sbt(nc, sc, 'qcA', [128, 4, 2048], BF)
            EB = sbt(nc, sc, 'EB', [128, 2, 4480], BF)
            ebs = sbt(nc, sc, 'ebs', [128, 2, 2240], F32)
            print('sbuf remaining (l1 C):', nc.sbuf_bytes_remaining)
            fw.op('dve', lambda: V.memset(Vc[:], 1.0), w=['Vc'])
            fw.dma('pool', kc[:, :, 0:512], D['cck_T'].rearrange("(j p) k -> p j k", p=128), w=['kc'])
            fw.dma('sp', kc[:, :, 768:2816], D['s_kc'].rearrange("(j p) t -> p j t", p=128), w=['kc'])
            xoB = D['xoutB'].rearrange("r c -> (r c)")
            RKB = XINB_ROWS * 2048
            kc0 = xoB[OFF_KC:OFF_KC + 512 * 512].rearrange("(j p t) -> p j t", j=4, p=128, t=512)
            kc1 = xoB[RKB + OFF_KC:RKB + OFF_KC + 512 * 512].rearrange("(j p t) -> p j t", j=4, p=128, t=512)
            fw.dma('sp', kc[:, :, 512:768], kc0[:, :, 256:512], r=['xoutB'], w=['kc'])
            fw.dma('sp', kc[:, :, 2816:3072], kc1[:, :, 0:256], r=['xoutB'], w=['kc'])
            qq = D['s_qc'].rearrange("(j p) t -> p j t", p=128)
            fw.dma('sp', qc[:], qq, w=['qc'])
            ccv = D['ccv'].rearrange("(n p) c -> p n c", p=128)
            svc = D['s_vc'].rearrange("(n p) c -> p n c", p=128)
            vc0 = xoB[OFF_VC:OFF_VC + 512 * 512].rearrange("(n p c) -> p n c", n=4, p=128, c=512)
            vc1 = xoB[RKB + OFF_VC:RKB + OFF_VC + 512 * 512].rearrange("(n p c) -> p n c", n=4, p=128, c=512)
            for h in range(8):
                dcol = (h // 2) * 192 + (h % 2) * 128
                fw.dma('pool', Vc[:, 0:4, dcol:dcol + 64], ccv[:, :, h * 64:(h + 1) * 64], w=['Vc'])
                fw.dma('sp', Vc[:, 6:22, dcol:dcol + 64], svc[:, :, h * 64:(h + 1) * 64], w=['Vc'])
                fw.dma('sp', Vc[:, 4:6, dcol:dcol + 64], vc0[:, 2:4, h * 64:(h + 1) * 64], r=['xoutB'], w=['Vc'])
                fw.dma('sp', Vc[:, 22:24, dcol:dcol + 64], vc1[:, 0:2, h * 64:(h + 1) * 64], r=['xoutB'], w=['Vc'])
            eng_rr = 0
            for h in range(8):
                j, side = h // 2, h % 2
                rows = slice(side * 64, (side + 1) * 64)
                eb = h % 2
                for part in range(2):
                    fw.dma('sp', ebs[:, part, :], D['nabias'][h, :, part * 2240:(part + 1) * 2240], w=[('ebs', part)])
                    fw.op('act', lambda: A.activation(out=EB[:, eb, part * 2240:(part + 1) * 2240], in_=ebs[:, part, :], func=AF.Exp),
                          r=[('ebs', part)], w=[('EB', eb)])
                calls = []
                for t in range(4):
                    kts = [(kc[:, j, kt * 128:(kt + 1) * 128], Vc[:, kt, j * 192 + side * 64:j * 192 + side * 64 + 128], None, ['kc', 'Vc'])
                           for kt in range(4)]
                    for m in range(8):
                        kt = 4 * t + m
                        c0 = (14 - 2 * m) * 64
                        cr = (0, 512)
                        if t == 0 and m <= 5:
                            spec = [(EB[:, eb, 1408 + m * 256:1408 + (m + 1) * 256], 0, 256), (EB[:, eb, c0 + 256:c0 + 512], 256, 512)]
                        elif t == 3 and m >= 2:
                            spec = [(EB[:, eb, c0:c0 + 256], 0, 256), (EB[:, eb, 2944 + (m - 2) * 256:2944 + (m - 1) * 256], 256, 512)]
                        else:
                            lo, hi = max(0, 2 * m - 7), min(7, 2 * m + 1)
                            cr = (lo * 64, (hi + 1) * 64)
                            spec = [(EB[:, eb, c0 + cr[0]:c0 + cr[1]], cr[0], cr[1])]
                        eng_rr += 1
                        ebl = [(ap_, a0, a1, [('EB', eb)], 'pool' if eng_rr % 2 else 'dve') for (ap_, a0, a1) in spec]
                        kts.append((kc[:, j, 512 + kt * 128:512 + (kt + 1) * 128],
                                    Vc[:, 4 + kt, j * 192 + side * 64:j * 192 + side * 64 + 128], ebl, ['kc', 'Vc'], cr))
                    ao = AOc[rows, j, t * 512:(t + 1) * 512]
                    calls.append(dict(q64=qc[rows, j, t * 512:(t + 1) * 512], qkeys=['qc'], nq=512, kts=kts, scale=0.125, side=side,
                                      dst=ao, gate=ao, dkeys=[('AOc', j, t)], gkeys=[('AOc', j, t)]))
                run_attn_calls(C, calls, la=4, banks=[(C.psS[i][:, :], ('S', i)) for i in range(3)] +
                               [(C.psA[:, i * 512:(i + 1) * 512], ('A', i)) for i in range(2)])
            fw.barrier()
        with ExitStack() as sd:
            ckvT = sbt(nc, sd, 'ckvTA', [128, 2, 4608], BF)
            kpeT = sbt(nc, sd, 'kpeTA', [128, 4608], BF)
            cqT = sbt(nc, sd, 'cqTA', [128, 3, 2048], BF)
            rd = sbt(nc, sd, 'rdA', [128, 2, 2048], F32)
            wuq = sbt(nc, sd, 'wuqA', [128, 3, 1536], BF); wuk = sbt(nc, sd, 'wukA', [128, 2, 512], BF); wuv = sbt(nc, sd, 'wuvA', [128, 2, 512], BF)
            kTh = sbt(nc, sd, 'kThA', [128, 2, 4608], BF)
            Vd = sbt(nc, sd, 'VdA', [128, 36, 192], BF)
            qdT = sbt(nc, sd, 'qdTA', [128, 2, 512], BF)
            r1 = sbt(nc, sd, 'r1A', [128, 512], F32); r2 = sbt(nc, sd, 'r2A', [128, 512], F32)
            print('sbuf remaining (l1 D):', nc.sbuf_bytes_remaining)
            load_w_cast(C, wuq, D['w_uq_ab'], 3, 1536, 'wuq')
            load_w_cast(C, wuk, D['w_uk'], 2, 512, 'wuk')
            load_w_cast(C, wuv, D['w_uv'], 2, 512, 'wuv')
            fw.op('dve', lambda: V.memset(Vd[:], 1.0), w=['Vd'])
            fw.op('dve', lambda: V.memset(kTh[:], 0.0), w=[('kTh', 0), ('kTh', 1)])
            fw.op('dve', lambda: V.memset(qdT[:], 0.0), w=[('qdT', 0), ('qdT', 1)])
            fw.dma('pool', ckvT[:, :, 0:512], D['cckv_T'].rearrange("(k p) t -> p k t", p=128), w=['ckvT'])
            fw.dma('pool', kpeT[64:96, 0:512], D['ckpe_T'][64:96, :], w=['kpeT'])
            for rk in range(2):
                base = rk * RK
                src = xout_flat[base:base + 256 * 2048].rearrange("(k p t) -> p k t", k=2, p=128, t=2048)
                fw.dma('sp', ckvT[:, :, 512 + rk * 2048:512 + (rk + 1) * 2048], src, r=['xout'], w=['ckvT'])
                srck = xout_flat[base + OFF_KPE:base + OFF_KPE + 32 * 2048].rearrange("(p t) -> p t", p=32, t=2048)
                fw.dma('sp', kpeT[64:96, 512 + rk * 2048:512 + (rk + 1) * 2048], srck, r=['xout'], w=['kpeT'])
            fw.dma('sp', cqT[:], D['s_cq'].rearrange("(k p) t -> p k t", p=128), w=['cqT'])
            fw.dma('sp', rd[:], D['rope_d'].rearrange("p (a t) -> p a t", a=2), w=['rd'])
            sc = 96.0 ** -0.5
            for j in range(4):
                for g4 in range(9):
                    ka, pa = next_A(C)
                    for k4 in range(4):
                        kt = g4 * 4 + k4
                        mm_acc(C, pa[:, k4 * 128:(k4 + 1) * 128], ka,
                               [(ckvT[:, c, kt * 128:(kt + 1) * 128], wuv[:, c, j * 128:(j + 1) * 128]) for c in range(2)], ['ckvT', 'wuv'])
                    d4 = Vd[:, g4 * 4:(g4 + 1) * 4, :].rearrange("p k (a d) -> p k a d", a=3, d=64)
                    s4 = pa.rearrange("p (k a d) -> p k a d", k=4, a=2, d=64)
                    fw.op('dve', lambda: V.tensor_copy(out=d4[:, :, 0, :], in_=s4[:, :, 0, :]), r=[ka], w=['Vd'])
                    fw.op('dve', lambda: V.tensor_copy(out=d4[:, :, 2, :], in_=s4[:, :, 1, :]), r=[ka], w=['Vd'])
                for hh in range(2):
                    h = 2 * j + hh
                    for blk in range(9):
                        ka, pa = next_A(C)
                        mm_acc(C, pa[0:64, :], ka, [(wuk[:, c, h * 64:(h + 1) * 64], ckvT[:, c, blk * 512:(blk + 1) * 512]) for c in range(2)],
                               ['ckvT', 'wuk'])
                        fw.op('dve', lambda: V.tensor_copy(out=kTh[0:64, hh, blk * 512:(blk + 1) * 512], in_=pa[0:64, :]), r=[ka], w=[('kTh', hh)])
                    fw.op('pool', lambda: G.tensor_copy(out=kTh[64:96, hh, :], in_=kpeT[64:96, :]), r=['kpeT'], w=[('kTh', hh)])
                def emit_q(t, hh, j=j):
                    h = 2 * j + hh
                    tq = slice(t * 512, (t + 1) * 512)
                    k0, p0 = next_A(C)
                    mm_acc(C, p0[0:96, :], k0, [(wuq[:, c, h * 96:(h + 1) * 96], cqT[:, c, tq]) for c in range(3)], ['cqT', 'wuq'])
                    k1, p1 = next_A(C)
                    mm_acc(C, p1[0:96, :], k1, [(wuq[:, c, 768 + h * 96:768 + (h + 1) * 96], cqT[:, c, tq]) for c in range(3)], ['cqT', 'wuq'])
                    fw.op('dve', lambda: V.tensor_copy(out=qdT[0:64, hh, :], in_=p0[0:64, :]), r=[k0], w=[('qdT', hh)])
                    fw.op('dve', lambda: V.tensor_tensor(out=r1[64:96, :], in0=p0[64:96, :], in1=rd[64:96, 0, tq], op=ALU.mult), r=[k0, 'rd'], w=['r1'])
                    fw.op('dve', lambda: V.tensor_tensor(out=r2[64:96, :], in0=p1[64:96, :], in1=rd[64:96, 1, tq], op=ALU.mult), r=[k1, 'rd'], w=['r2'])
                    fw.op('dve', lambda: V.tensor_tensor(out=qdT[64:96, hh, :], in0=r1[64:96, :], in1=r2[64:96, :], op=ALU.add),
                          r=['r1', 'r2'], w=[('qdT', hh)])

                seq = [(t, hh) for t in range(4) for hh in range(2)]
                emit_q(*seq[0])
                for i, (t, hh) in enumerate(seq):
                    tq = slice(t * 512, (t + 1) * 512)
                    rows = slice(hh * 64, (hh + 1) * 64)
                    kts = [(kTh[:, hh, kt * 128:(kt + 1) * 128], Vd[:, kt, hh * 64:hh * 64 + 128], None, [('kTh', hh), 'Vd'])
                           for kt in range(36)]
                    ao = AOd[rows, j, tq]
                    fl = (lambda nx=seq[i + 1]: emit_q(*nx)) if i + 1 < len(seq) else None
                    attention(C, qdT[:, hh, :], 512, [('qdT', hh)], kts, sc, hh, ao, ao, [('AOd', j, t)], [('AOd', j, t)], filler=fl)
            fw.barrier()
        with ExitStack() as sp_:
            wo = sbt(nc, sp_, 'wo1A', [128, 8, 1024], BF)
            load_w_cast(C, wo, D['w_out_o'], 8, 1024, 'wo')
            for t in range(4):
                xts = [load_x(C, D['x1s'][t * 512 + i * 128:t * 512 + (i + 1) * 128, :]) for i in range(4)]
                keys = [('AOc', j, t) for j in range(4)] + [('AOd', j, t) for j in range(4)]
                outproj_residual(C, 1, 0, None, keys, wo, 4, xts,
                                 lambda i: D['y_s'][t * 512 + i * 128:t * 512 + (i + 1) * 128, :],
                                 ao_fn=lambda c, i: (AOc[:, c, t * 512 + i * 128:t * 512 + (i + 1) * 128] if c < 4
                                                     else AOd[:, c - 4, t * 512 + i * 128:t * 512 + (i + 1) * 128]))
            fw.barrier()
```

```python
import numpy as np
from contextlib import ExitStack
import concourse.bass as bass
import concourse.mybir as mybir
from concourse.bass_utils import run_bass_kernel_spmd

F32 = mybir.dt.float32
BF = mybir.dt.bfloat16
AF = mybir.ActivationFunctionType
ALU = mybir.AluOpType
AX = mybir.AxisListType
P = 128


class FW:
    ND = 24
    NHW = 16

    def __init__(self, nc, es):
        self.nc = nc
        self.eng = {'pe': nc.tensor, 'act': nc.scalar, 'dve': nc.vector, 'pool': nc.gpsimd, 'sp': nc.sync}
        self.sem = {e: es.enter_context(nc.semaphore('s_' + e)) for e in ('pe', 'act', 'dve', 'pool')}
        self.n = {e: 0 for e in self.sem}
        self.dsem = [es.enter_context(nc.semaphore('d%d' % i)) for i in range(self.ND)]
        self.dn = [0] * self.ND
        self.rr = 0
        self.rr_sw = 0
        self.waited = {e: {} for e in self.eng}
        self.state = {}
        self.ns = None
        self.nskeys = set()

    def _k(self, keys):
        if self.ns is None:
            return list(keys)
        out = []
        for k in keys:
            base = k if isinstance(k, str) else k[0]
            out.append(('ns', self.ns, k) if base in self.nskeys else k)
        return out

    def _wait(self, eng, tid, seq):
        if self.waited[eng].get(tid, 0) >= seq:
            return
        self.waited[eng][tid] = seq
        if tid[0] == 'e':
            assert seq <= self.n[tid[1]], (eng, tid, seq, self.n[tid[1]])
            self.eng[eng].wait_ge(self.sem[tid[1]], seq)
        elif tid[0] == 'c':
            sem, amount = self.csem[tid[1]]
            self.eng[eng].wait_ge(sem, amount * seq)
        else:
            self.eng[eng].wait_ge(self.dsem[tid[1]], 16 * seq)

    def _dep(self, eng, tok, raw, strict):
        tid, seq = tok
        if tid == ('e', eng) and not strict:
            if eng == 'pe':
                return
        self._wait(eng, tid, seq)

    def _sync(self, eng, r, w, strict=False):
        for k in r:
            st = self.state.get(k)
            if st and st[0]:
                self._dep(eng, st[0], True, strict)
        for k in w:
            st = self.state.get(k)
            if st:
                if st[0]:
                    self._dep(eng, st[0], True, strict)
                for tid, seq in st[1].items():
                    self._dep(eng, (tid, seq), False, strict)

    def _mark(self, tid, seq, r, w):
        for k in r:
            st = self.state.setdefault(k, [None, {}])
            st[1][tid] = seq
        for k in w:
            self.state[k] = [(tid, seq), {}]

    @staticmethod
    def _excl(r, w):
        px = [k for k in r if k == 'psT' or (isinstance(k, tuple) and k[0] in ('A', 'S', 'O'))]
        if not px:
            return r, w
        return [k for k in r if k not in px], list(w) + px

    def op(self, eng, fn, r=(), w=()):
        r, w = self._excl(self._k(r), self._k(w))
        self._sync(eng, r, w)
        ins = fn()
        self.n[eng] += 1
        ins.then_inc(self.sem[eng], 1)
        self._mark(('e', eng), self.n[eng], r, w)
        return ins

    def op_noinc(self, eng, fn, r=(), w=()):
        r, w = self._excl(self._k(r), self._k(w))
        self._sync(eng, r, w)
        fn()
        self._mark(('e', eng), self.n[eng] + 1, r, w)

    def dma(self, q, out, in_, r=(), w=(), **kw):
        r, w = self._k(r), self._k(w)
        self._sync(q, r, w, strict=True)
        if q == 'pool':
            i = self.NHW + self.rr_sw
            self.rr_sw = (self.rr_sw + 1) % (self.ND - self.NHW)
        else:
            i = self.rr
            self.rr = (i + 1) % self.NHW
        if self.dn[i] > 0:
            self._wait(q, ('d', i), self.dn[i])
        ins = self.eng[q].dma_start(out=out, in_=in_, **kw)
        self.dn[i] += 1
        ins.then_inc(self.dsem[i], 16)
        self._mark(('d', i), self.dn[i], r, w)
        return ins

    def custom(self, eng, fn, sem, amount, r=(), w=()):
        self._sync(eng, r, w, strict=True)
        ins = fn()
        ins.then_inc(sem, amount)
        self.csem = getattr(self, 'csem', {})
        cid = len(self.csem)
        self.csem[cid] = (sem, amount)
        self._mark(('c', cid), 1, r, w)
        return ins

    def barrier(self):
        for e in self.eng:
            for e2 in self.sem:
                if e2 != e and self.n[e2] > 0:
                    self._wait(e, ('e', e2), self.n[e2])
            for i in range(self.ND):
                if self.dn[i] > 0:
                    self._wait(e, ('d', i), self.dn[i])
        self.state.clear()

    def finish(self):
        for i in range(self.ND):
            if self.dn[i] > 0:
                self._wait('sp', ('d', i), self.dn[i])
        for cid in getattr(self, 'csem', {}):
            self._wait('sp', ('c', cid), 1)


D_MODEL = 1024
NEG = -30000.0
EPS = 1e-6
PERM_A = np.concatenate([np.r_[j * 64:(j + 1) * 64, (4 + j) * 64:(5 + j) * 64] for j in range(4)])
POOL_W = (2, 4, 8, 16)
XIN_ROWS = 288
XINB_ROWS = 256
OFF_KPE = 256 * 2048
OFF_KC = 0
OFF_VC = 512 * 512


class Ctx:
    pass


_SBT_N = [0]


def sbt(nc, es, name, shape, dt):
    _SBT_N[0] += 1
    return es.enter_context(nc.sbuf_tensor('sb%d_%s' % (_SBT_N[0], name), list(shape), dt))


def build_program(stop_after=None):
    nc = bass.Bass("TRN2", target_bir_lowering=False)
    D = {}

    def din(name, shape, dt=F32):
        D[name] = nc.dram_tensor(name, list(shape), dt, kind="ExternalInput").ap()

    def dout(name, shape):
        D[name] = nc.dram_tensor(name, list(shape), F32, kind="ExternalOutput").ap()

    def dscr(name, shape, dt):
        if stop_after is not None and name in ('x1s', 'x1p'):
            D[name] = nc.dram_tensor(name, list(shape), dt, kind="ExternalOutput").ap()
        else:
            D[name] = nc.dram_tensor(name, list(shape), dt).ap()

    din('xs', [4096, 1024]); din('xp', [1024, 1024])
    din('condT', [128, 16]); din('w_mod', [2, 1024, 3072]); din('b_mod', [2, 3072])
    din('g_preT', [128, 32]); din('g_post', [2, 1024]); din('sel', [2, 256])
    din('w_in_e', [1024, 2304]); din('gain_a', [128, 640]); din('b_mapT', [128, 512]); din('b_scaleT', [128, 4])
    din('w_out_e', [1024, 1024]); din('rope_a', [4096, 128]); din('prc_s', [128, 4 * 2048]); din('prc_p', [128, 4 * 256])
    din('hmask', [128, 2]); din('cak_T', [128, 512]); din('cav', [512, 128])
    din('w1_fm', [1024, 2048]); din('w1_tm', [1024, 1696]); din('w_kpe2', [1024, 192])
    din('gain_q', [128, 384]); din('gain_kv', [128, 256])
    din('w_uq_ab', [384, 1536]); din('w_uk', [256, 512]); din('w_uv', [256, 512]); din('w_out_o', [1024, 1024])
    din('rope_d', [128, 2 * 2048]); din('cck_T', [512, 512]); din('ccv', [512, 512]); din('cckv_T', [256, 512])
    din('ckpe_T', [128, 512]); din('nabias', [8, 128, 4480])
    dout('y_s', [2048, 1024]); dout('y_p', [1024, 1024]); dout('nak', [1024, 128]); dout('nav', [1024, 128])
    dout('nck', [1024, 512]); dout('ncv', [1024, 512]); dout('nckv', [1024, 256]); dout('nkpe', [1024, 32])
    dscr('x1s', [2048, 1024], F32); dscr('x1p', [1024, 1024], F32)
    dscr('s_qc', [512, 2048], BF); dscr('s_kc', [512, 2048], BF); dscr('s_gc', [512, 2048], BF); dscr('s_gd', [512, 2048], BF)
    dscr('s_cq', [384, 2048], BF); dscr('s_vc', [2048, 512], BF)
    dscr('xin', [XIN_ROWS, 2048], BF); dscr('xout', [2 * XIN_ROWS, 2048], BF)
    dscr('xinB', [XINB_ROWS, 2048], BF); dscr('xoutB', [2 * XINB_ROWS, 2048], BF)

    C = Ctx()
    C.nc = nc; C.D = D
    with ExitStack() as gs:
        fw = FW(nc, gs)
        C.fw = fw
        C.cc_sem = gs.enter_context(nc.semaphore('cc_sem'))
        C.cc_sem2 = gs.enter_context(nc.semaphore('cc_sem2'))
        C.ident_b = sbt(nc, gs, 'ident_b', [128, 128], BF)
        C.ident_f = sbt(nc, gs, 'ident_f', [128, 128], F32)
        C.modS = sbt(nc, gs, 'modS', [128, 32], F32)
        C.modH = sbt(nc, gs, 'modH', [128, 32], F32)
        C.gg = sbt(nc, gs, 'gg', [128, 4, 1024], F32)
        C.stat = sbt(nc, gs, 'stat', [128, 64], F32)
        C.stat_i = 0
        C.xbuf = sbt(nc, gs, 'xbuf', [128, 4, 1024], F32)
        C.xb_i = 0
        C.xn = sbt(nc, gs, 'xn', [128, 4, 1024], BF)
        C.junk = sbt(nc, gs, 'junk', [128, 1024], BF)
        C.ty = sbt(nc, gs, 'ty', [128, 1024], F32)
        C.pbuf = sbt(nc, gs, 'pbuf', [128, 6, 512], BF)
        C.p_i = 0
        C.rcb = sbt(nc, gs, 'rcb', [128, 2, 512], F32)
        C.qz = sbt(nc, gs, 'qz', [128, 2, 2, 512], BF)
        C.qz_i = [0, 0]
        C.psS = [gs.enter_context(nc.psum_tensor('psS%d' % i, [128, 512], F32)) for i in range(3)]
        C.s_i = 0
        C.psO = [gs.enter_context(nc.psum_tensor('psO%d' % i, [128, 512], F32)) for i in range(2)]
        C.o_i = 0
        C.psA = gs.enter_context(nc.psum_tensor('psA', [128, 1024], F32))
        C.a_i = 0
        C.a_wide = True
        C.psT = gs.enter_context(nc.psum_tensor('psT', [128, 1024], BF))
        fw.op('pool', lambda: nc.gpsimd.memset(C.qz[:], 0.0), w=[('qz', a, b) for a in range(2) for b in range(2)])
        for t, k in ((C.ident_b, 'ident_b'), (C.ident_f, 'ident_f')):
            fw.op('pool', lambda: nc.gpsimd.memset(t[:], 1.0), w=[k])
            fw.op('pool', lambda: nc.gpsimd.affine_select(out=t[:], in_=t[:], pattern=[[-1, 128]], compare_op=ALU.is_equal,
                                                          fill=0.0, base=0, channel_multiplier=1), r=[k], w=[k])
        import os as _os
        if _os.environ.get('KSKIP_L0'):
            phase_setup(C, (0,))
        if stop_after != 'setup' and not _os.environ.get('KSKIP_L0'):
            phase_l0(C, stop_after)
            fw.barrier()
        if stop_after is None or stop_after.startswith('l1'):
            phase_l1(C, stop_after)
            fw.barrier()
        fw.finish()
    return nc


def stat_col(C):
    i = C.stat_i
    C.stat_i = (i + 1) % 64
    return ('st', i), C.stat[:, i:i + 1]


def stat_cols(C, n):
    if C.stat_i + n > 64:
        C.stat_i = 0
    i = C.stat_i
    C.stat_i = (i + n) % 64
    return [('st', j) for j in range(i, i + n)], C.stat[:, i:i + n]


def next_A(C):
    if getattr(C, 'a_wide', False):
        i = C.a_i % 5
        C.a_i = i + 1
        if i < 2:
            return ('A', i), C.psA[:, i * 512:(i + 1) * 512]
        return ('S', i - 2), C.psS[i - 2][:, :]
    i = C.a_i % 2
    C.a_i = i + 1
    return ('A', i), C.psA[:, i * 512:(i + 1) * 512]


def phase_setup(C, layers=(0, 1), after_issue=None, stack=None):
    nc, fw, D = C.nc, C.fw, C.D
    V, A, PE = nc.vector, nc.scalar, nc.tensor
    own = ExitStack() if stack is None else None
    with (own if own is not None else ExitStack()) as _tmp:
        s0 = own if own is not None else stack
        condT = sbt(nc, s0, 'condT', [128, 16], F32); scT = sbt(nc, s0, 'scT', [128, 16], BF)
        wm = sbt(nc, s0, 'wm', [128, 8, 1024], BF)
        mrow = sbt(nc, s0, 'mrow', [2, 3072], F32); brow = sbt(nc, s0, 'brow', [2, 3072], F32)
        gpT = sbt(nc, s0, 'gpT', [128, 32], F32); sels = sbt(nc, s0, 'sels', [2, 256], F32)
        gpost = sbt(nc, s0, 'gpost', [128, 1024], F32)
        fw.dma('sp', condT[:], D['condT'], w=['condT'])
        fw.dma('sp', gpT[:], D['g_preT'], w=['gpT'])
        fw.dma('sp', sels[:], D['sel'], w=['sels'])
        fw.op('act', lambda: A.activation(out=scT[:], in_=condT[:], func=AF.Silu), r=['condT'], w=['scT'])
        for l in layers:
            fw.dma('sp', brow[:], D['b_mod'][l:l + 1, :].partition_broadcast(2), w=['brow'])
            fw.dma('sp', gpost[:], D['g_post'][l:l + 1, :].partition_broadcast(128), w=['gpost'])
            wsrc = D['w_mod'][l].rearrange("(c p) n -> p c n", p=128)
            for blk in range(3):
                for c in range(8):
                    fw.dma('pool', wm[:, c, :], wsrc[:, c, blk * 1024:(blk + 1) * 1024], w=[('wm', c)])
                for sub in range(2):
                    col0 = blk * 1024 + sub * 512
                    ka, pa = next_A(C)
                    for c in range(8):
                        f = lambda: PE.matmul(pa[0:2, :], lhsT=scT[:, 2 * c:2 * c + 2], rhs=wm[:, c, sub * 512:(sub + 1) * 512],
                                              start=(c == 0), stop=(c == 7))
                        if c < 7:
                            fw.op_noinc('pe', f, r=['scT', ('wm', c)], w=[ka])
                        else:
                            fw.op('pe', f, r=['scT', ('wm', c)], w=[ka])
                    fw.op('dve', lambda: V.tensor_tensor(out=mrow[0:2, col0:col0 + 512], in0=pa[0:2, :], in1=brow[0:2, col0:col0 + 512],
                                                         op=ALU.add), r=[ka, 'brow'], w=['mrow'])
            ka, pa = next_A(C)
            for c in range(16):
                f = lambda: PE.transpose(out=pa[:, 2 * c:2 * c + 2], in_=mrow[0:2, c * 128:(c + 1) * 128], identity=C.ident_f[0:2, 0:2])
                if c < 15:
                    fw.op_noinc('pe', f, r=['mrow', 'ident_f'], w=[ka])
                else:
                    fw.op('pe', f, r=['mrow', 'ident_f'], w=[ka])
            fw.op('dve', lambda: V.tensor_copy(out=C.modH[:, l * 16:(l + 1) * 16], in_=pa[:, 0:16]), r=[ka], w=['modH'])
            fw.op('dve', lambda: V.scalar_tensor_tensor(out=C.modS[:, l * 16:(l + 1) * 16], in0=pa[:, 16:32], scalar=1.0,
                                                        in1=gpT[:, l * 16:(l + 1) * 16], op0=ALU.add, op1=ALU.mult),
                  r=[ka, 'gpT'], w=['modS'])
            for g in range(2):
                for half in range(2):
                    ka, pa = next_A(C)
                    fw.op('pe', lambda: PE.matmul(pa, lhsT=sels[:, g * 128:(g + 1) * 128],
                                                  rhs=mrow[0:2, 2048 + half * 512:2048 + (half + 1) * 512], start=True, stop=True),
                          r=['sels', 'mrow'], w=[ka])
                    fw.op('dve', lambda: V.tensor_tensor(out=C.gg[:, l * 2 + g, half * 512:(half + 1) * 512], in0=pa,
                                                         in1=gpost[:, half * 512:(half + 1) * 512], op=ALU.mult),
                          r=[ka, 'gpost'], w=[('gg', l * 2 + g)])
        if after_issue is not None:
            after_issue()
        if own is not None:
            fw.barrier()


def load_x(C, src_rows):
    i = C.xb_i
    C.xb_i = (i + 1) % 4
    C.fw.dma('sp', C.xbuf[:, i, :], src_rows, w=[('xb', i)])
    return ('xb', i), C.xbuf[:, i, :]


def rstd_col(C, ss_key, ss_ap, n, width):
    nc, fw = C.nc, C.fw
    keys, rs = stat_cols(C, n)
    fw.op('act', lambda: nc.scalar.activation(out=rs, in_=ss_ap, func=AF.Sqrt, scale=1.0 / width, bias=EPS), r=ss_key, w=keys)
    fw.op('dve', lambda: nc.vector.reciprocal(out=rs, in_=rs), r=keys, w=keys)
    return keys, rs


def prenorm(C, l, g, xts, hT, hkey):
    nc, fw = C.nc, C.fw
    V, A, PE = nc.vector, nc.scalar, nc.tensor
    n = len(xts)
    for i, (xk, xt) in enumerate(xts):
        ks, ss = stat_col(C)
        fw.op('act', lambda: A.activation(out=C.junk[:], in_=xt, func=AF.Square, accum_out=ss), r=[xk], w=[ks])
        kr, rs = rstd_col(C, [ks], ss, 1, 1024.0)
        fw.op('dve', lambda: V.tensor_scalar(out=C.xn[:, i, :], in0=xt, scalar1=rs, scalar2=None, op0=ALU.mult),
              r=[xk] + kr, w=[('xn', i)])
    for cp in range(4):
        for c in (2 * cp, 2 * cp + 1):
            for i in range(n):
                f = lambda: PE.transpose(out=C.psT[:, (c % 2) * 512 + i * 128:(c % 2) * 512 + (i + 1) * 128],
                                         in_=C.xn[:, i, c * 128:(c + 1) * 128], identity=C.ident_b[:])
                if c == 2 * cp + 1 and i == n - 1:
                    fw.op('pe', f, r=[('xn', i), 'ident_b'], w=['psT'])
                else:
                    fw.op_noinc('pe', f, r=[('xn', i), 'ident_b'], w=['psT'])
        for c in (2 * cp, 2 * cp + 1):
            j = l * 16 + c * 2 + g
            fw.op('dve', lambda: V.tensor_scalar(out=hT[:, c, 0:n * 128], in0=C.psT[:, (c % 2) * 512:(c % 2) * 512 + n * 128],
                                                 scalar1=C.modS[:, j:j + 1], scalar2=C.modH[:, j:j + 1], op0=ALU.mult, op1=ALU.add),
                  r=['psT', 'modS', 'modH'], w=[(hkey, c)])


def mm_acc(C, out_ap, out_key, pairs, rkeys):
    fw, PE = C.fw, C.nc.tensor
    n = len(pairs)
    for i, (l_, r_) in enumerate(pairs):
        f = lambda: PE.matmul(out_ap, lhsT=l_, rhs=r_, start=(i == 0), stop=(i == n - 1))
        if i < n - 1:
            fw.op_noinc('pe', f, r=rkeys, w=[out_key])
        else:
            fw.op('pe', f, r=rkeys, w=[out_key])


def load_w_cast(C, dst, src, nchunk, ncols, key):
    for c in range(nchunk):
        for c0 in range(0, ncols, 1024):
            c1 = min(ncols, c0 + 1024)
            C.fw.dma('pool', dst[:, c, c0:c1], src[c * 128:(c + 1) * 128, c0:c1], w=[key])


def pad_q(C, qT, qkeys, side, nq):
    fw, G = C.fw, C.nc.gpsimd
    rows = slice(0, 64) if side == 0 else slice(64, 128)
    b = C.qz_i[side]
    C.qz_i[side] = 1 - b
    fw.op('pool', lambda: G.tensor_copy(out=C.qz[rows, side, b, 0:nq], in_=qT), r=qkeys, w=[('qz', side, b)])
    return C.qz[:, side, b, 0:nq], [('qz', side, b)]


def run_attn_calls(C, calls, la=2, banks=None, extra=None):
    nxt = pad_q(C, calls[0]['q64'], calls[0]['qkeys'], calls[0]['side'], calls[0]['nq'])
    for i, c in enumerate(calls):
        cur = nxt
        box = {}
        n_ = calls[i + 1] if i + 1 < len(calls) else None
        ex = extra[i] if (extra is not None and i < len(extra)) else None

        def fl(n_=n_, box=box, ex=ex):
            if n_ is not None:
                box['v'] = pad_q(C, n_['q64'], n_['qkeys'], n_['side'], n_['nq'])
            if ex is not None:
                ex()
        attention(C, cur[0], c['nq'], cur[1], c['kts'], c['scale'], c['side'], c['dst'], c['gate'], c['dkeys'], c['gkeys'], filler=fl, la=la, banks=banks)
        nxt = box.get('v')
    if extra is not None:
        for ex in extra[len(calls):]:
            ex()


def attention(C, qT, nq, qkeys, keytiles, scale, side, dst, gate, dkeys, gkeys, pad=False, filler=None, la=2, banks=None):
    nc, fw = C.nc, C.fw
    V, A, PE, G = nc.vector, nc.scalar, nc.tensor, nc.gpsimd
    wide_saved = getattr(C, 'a_wide', False)
    C.a_wide = False
    oi = C.o_i
    C.o_i = 1 - oi
    psO = C.psO[oi]; okey = ('O', oi)
    nk = len(keytiles)
    orows = slice(0, 64) if side == 0 else slice(64, 128)
    srows = slice(64, 128) if side == 0 else slice(0, 64)
    pend = []
    if pad:
        qT, qkeys = pad_q(C, qT, qkeys, side, nq)

    def qk(idx):
        kT, va, eb, rk = keytiles[idx][:4]
        a, b_ = keytiles[idx][4] if len(keytiles[idx]) > 4 else (0, nq)
        bk = banks if banks is not None else [(C.psS[i][:, :], ('S', i)) for i in range(3)]
        si = C.s_i % len(bk)
        C.s_i = si + 1
        sap, skey = bk[si]
        fw.op('pe', lambda: PE.matmul(sap[:, a:b_], lhsT=kT, rhs=qT[:, a:b_], start=True, stop=True), r=rk + qkeys, w=[skey])
        pi = C.p_i
        C.p_i = (pi + 1) % 6
        fw.op('act', lambda: A.activation(out=C.pbuf[:, pi, a:b_], in_=sap[:, a:b_], func=AF.Exp, scale=scale),
              r=[skey], w=[('p', pi)])
        if eb is not None:
            for (ebap, c0, c1, ekeys, eng) in eb:
                e = V if eng == 'dve' else G
                fw.op(eng, lambda: e.tensor_tensor(out=C.pbuf[:, pi, c0:c1], in0=C.pbuf[:, pi, c0:c1], in1=ebap, op=ALU.mult),
                      r=[('p', pi)] + ekeys, w=[('p', pi)])
        pend.append((idx, pi))

    def pv():
        idx, pi = pend.pop(0)
        kT, va, eb, rk = keytiles[idx][:4]
        a, b_ = keytiles[idx][4] if len(keytiles[idx]) > 4 else (0, nq)
        assert idx > 0 or (a, b_) == (0, nq)
        f = lambda: PE.matmul(psO[:, a:b_], lhsT=va, rhs=C.pbuf[:, pi, a:b_], start=(idx == 0), stop=(idx == nk - 1))
        if idx == nk - 1:
            fw.op('pe', f, r=rk + [('p', pi)], w=[okey])
        else:
            fw.op_noinc('pe', f, r=rk + [('p', pi)], w=[okey])

    for idx in range(nk):
        qk(idx)
        if idx >= la:
            pv()
        if filler is not None and idx == min(3, nk - 1):
            filler()
    while pend:
        pv()
    fw.op('dve', lambda: V.reciprocal(out=C.rcb[srows, oi, 0:nq], in_=psO[srows, 0:nq]), r=[okey], w=[('rcb', oi)])
    fw.op('dve', lambda: V.tensor_tensor(out=C.rcb[orows, oi, 0:nq], in0=psO[orows, 0:nq], in1=C.rcb[srows, oi, 0:nq], op=ALU.mult),
          r=[okey, ('rcb', oi)], w=[('rcb', oi)])
    fw.op('dve', lambda: V.tensor_tensor(out=dst, in0=C.rcb[orows, oi, 0:nq], in1=gate, op=ALU.mult),
          r=[('rcb', oi)] + gkeys, w=dkeys)
    C.a_wide = wide_saved


def outproj_residual(C, l, g, AO, ao_keys, wo, n, xts, dst_rows_fn, ao_fn=None):
    nc, fw = C.nc, C.fw
    V, A, PE = nc.vector, nc.scalar, nc.tensor
    sets = [[(C.psA[:, 0:512], ('A', 0)), (C.psA[:, 512:1024], ('A', 1))],
            [(C.psS[0][:, :], ('S', 0)), (C.psS[1][:, :], ('S', 1))]]
    for i, (xk, xt) in enumerate(xts):
        hs = sets[i % 2]
        for half in range(2):
            mm_acc(C, hs[half][0], hs[half][1],
                   [((ao_fn(c, i) if ao_fn else AO[:, c, i * 128:(i + 1) * 128]), wo[:, c, half * 512:(half + 1) * 512]) for c in range(8)],
                   ao_keys + ['wo'])
        ks, ss2 = stat_cols(C, 2)
        for half in range(2):
            fw.op('act', lambda: A.activation(out=C.junk[:, half * 512:(half + 1) * 512], in_=hs[half][0], func=AF.Square, accum_out=ss2[:, half:half + 1]),
                  r=[hs[half][1]], w=[ks[half]])
        k1, ss = stat_col(C)
        fw.op('dve', lambda: V.tensor_tensor(out=ss, in0=ss2[:, 0:1], in1=ss2[:, 1:2], op=ALU.add), r=ks, w=[k1])
        kr, rs = rstd_col(C, [k1], ss, 1, 1024.0)
        for half in range(2):
            fw.op('dve', lambda: V.tensor_tensor(out=C.ty[:, half * 512:(half + 1) * 512], in0=hs[half][0],
                                                 in1=C.gg[:, l * 2 + g, half * 512:(half + 1) * 512], op=ALU.mult),
                  r=[hs[half][1], ('gg', l * 2 + g)], w=[('ty', half)])
        fw.op('dve', lambda: V.scalar_tensor_tensor(out=xt, in0=C.ty[:], scalar=rs, in1=xt, op0=ALU.mult, op1=ALU.add),
              r=[('ty', 0), ('ty', 1), xk] + kr, w=[xk])
        fw.dma('sp', dst_rows_fn(i), xt, r=[xk])


def normrope(C, L, z_ps, zkey, H, gain, rope, out_bf, okeys):
    nc, fw = C.nc, C.fw
    V, A = nc.vector, nc.scalar
    W = H * 64
    v3 = lambda ap: ap.rearrange("p (h d) -> p h d", d=64)
    fw.op('act', lambda: A.activation(out=L.sq[:, 0:W], in_=z_ps, func=AF.Square), r=[zkey], w=['sq'])
    ks, ssq = stat_cols(C, H)
    fw.op('dve', lambda: V.tensor_reduce(out=ssq, in_=v3(L.sq[:, 0:W]), axis=AX.X, op=ALU.add), r=['sq'], w=ks)
    kr, rs = rstd_col(C, ks, ssq, H, 64.0)
    fw.op('dve', lambda: V.tensor_tensor(out=v3(L.zg[:, 0:W]), in0=v3(z_ps), in1=rs.unsqueeze(2).broadcast_to([128, H, 64]), op=ALU.mult),
          r=[zkey] + kr, w=['zg'])
    fw.op('dve', lambda: V.tensor_tensor(out=L.zg[:, 0:W], in0=L.zg[:, 0:W], in1=gain, op=ALU.mult), r=['zg', 'gainA'], w=['zg'])
    if rope is None:
        fw.op('act', lambda: A.copy(out=out_bf, in_=L.zg[:, 0:W]), r=['zg'], w=okeys)
        return
    rk, rt = rope[0], rope[1]
    fw.op('dve', lambda: V.tensor_tensor(out=v3(L.t1[:, 0:W]), in0=v3(L.zg[:, 0:W]),
                                         in1=rt[:, 0:64].unsqueeze(1).broadcast_to([128, H, 64]), op=ALU.mult),
          r=['zg', rk], w=['t1'])
    z4 = L.zg[:, 0:W].rearrange("p (h b s d) -> p h b s d", b=2, s=2, d=16)
    t4 = L.t2[:, 0:W].rearrange("p (h b s d) -> p h b s d", b=2, s=2, d=16)
    s4 = rt[:, 64:128].rearrange("p (b s d) -> p b s d", b=2, s=2, d=16)
    for so, si in ((0, 1), (1, 0)):
        fw.op('pool', lambda: nc.gpsimd.tensor_tensor(out=t4[:, :, :, so, :], in0=z4[:, :, :, si, :],
                                                      in1=s4[:, :, so, :].unsqueeze(1).broadcast_to([128, H, 2, 16]), op=ALU.mult),
              r=['zg', rk], w=[('t2', so)])
    fw.op('dve', lambda: V.tensor_tensor(out=out_bf, in0=L.t1[:, 0:W], in1=L.t2[:, 0:W], op=ALU.add),
          r=['t1', ('t2', 0), ('t2', 1)], w=okeys)


def hkeys(name):
    return [(name, c) for c in range(8)]


def l0_kv(C, L, n, kcol0, kt0, kkey, ropes, out_row0):
    nc, fw, D = C.nc, C.fw, C.D
    V, A, PE, G = nc.vector, nc.scalar, nc.tensor, nc.gpsimd
    akeys = [('A', 0)] + ([('A', 1)] if n > 2 else [])
    for i in range(n):
        mm_acc(C, C.psA[:, i * 256:(i + 1) * 256], ('A', i // 2),
               [(L.hT[:, c, i * 128:(i + 1) * 128], L.w0[:, c, 512:768]) for c in range(8)], hkeys('hT') + ['w0'])
    W = n * 128
    pk = C.psA[:, 0:n * 256].rearrange("p (i s h d) -> p i s h d", s=2, h=2, d=64)
    zk = pk[:, :, 0, :, :]
    v4 = lambda t: t[:, 0:W].rearrange("p (i h d) -> p i h d", h=2, d=64)
    fw.op('act', lambda: A.activation(out=v4(L.sq), in_=zk, func=AF.Square), r=akeys, w=['sq'])
    ks, ssq = stat_cols(C, 2 * n)
    fw.op('dve', lambda: V.tensor_reduce(out=ssq, in_=L.sq[:, 0:W].rearrange("p (g d) -> p g d", d=64), axis=AX.X, op=ALU.add), r=['sq'], w=ks)
    kr, rs = rstd_col(C, ks, ssq, 2 * n, 64.0)
    fw.op('dve', lambda: V.tensor_tensor(out=v4(L.zg), in0=zk, in1=rs.rearrange("p (i h) -> p i h", h=2).unsqueeze(3).broadcast_to([128, n, 2, 64]),
                                         op=ALU.mult), r=akeys + kr, w=['zg'])
    z3 = L.zg[:, 0:W].rearrange("p (i c) -> p i c", c=128)
    fw.op('dve', lambda: V.tensor_tensor(out=z3, in0=z3, in1=L.gainA[:, 512:640].unsqueeze(1).broadcast_to([128, n, 128]), op=ALU.mult),
          r=['zg', 'gainA'], w=['zg'])
    if ropes is None:
        fw.op('act', lambda: A.copy(out=L.zb[:, 0:W], in_=L.zg[:, 0:W]), r=['zg'], w=['zb'])
    else:
        rk = ropes[0][0]
        rt = ropes[0][2]
        fw.op('dve', lambda: V.tensor_tensor(out=v4(L.t1), in0=v4(L.zg), in1=rt[:, 0:n, 0:64].unsqueeze(2).broadcast_to([128, n, 2, 64]),
                                             op=ALU.mult), r=['zg', rk], w=['t1'])
        for i in range(n):
            z4 = L.zg[:, i * 128:(i + 1) * 128].rearrange("p (h b s d) -> p h b s d", b=2, s=2, d=16)
            t4 = L.t2[:, i * 128:(i + 1) * 128].rearrange("p (h b s d) -> p h b s d", b=2, s=2, d=16)
            s4 = rt[:, i, 64:128].rearrange("p (b s d) -> p b s d", b=2, s=2, d=16)
            for so, si in ((0, 1), (1, 0)):
                fw.op('pool', lambda: G.tensor_tensor(out=t4[:, :, :, so, :], in0=z4[:, :, :, si, :],
                                                      in1=s4[:, :, so, :].unsqueeze(1).broadcast_to([128, 2, 2, 16]), op=ALU.mult),
                      r=['zg', rk], w=[('t2', so)])
        fw.op('dve', lambda: V.tensor_tensor(out=L.zb[:, 0:W], in0=L.t1[:, 0:W], in1=L.t2[:, 0:W], op=ALU.add),
              r=['t1', ('t2', 0), ('t2', 1)], w=['zb'])
    if out_row0 is not None:
        for i in range(n):
            fw.dma('sp', D['nak'][out_row0 + i * 128:out_row0 + (i + 1) * 128, :], L.zg[:, i * 128:(i + 1) * 128], r=['zg'])
        fw.op('act', lambda: A.copy(out=L.vt32[:, 0:W].rearrange("p (i c) -> p i c", c=128), in_=pk[:, :, 1, :, :].rearrange("p i h d -> p i (h d)")),
              r=akeys, w=['vt32'])
        for i in range(n):
            fw.dma('sp', D['nav'][out_row0 + i * 128:out_row0 + (i + 1) * 128, :], L.vt32[:, i * 128:(i + 1) * 128], r=['vt32'])
    for i in range(n):
        f = lambda: PE.transpose(out=C.psT[:, i * 128:(i + 1) * 128], in_=L.zb[:, i * 128:(i + 1) * 128], identity=C.ident_b[:])
        if i < n - 1:
            fw.op_noinc('pe', f, r=['zb', 'ident_b'], w=['psT'])
        else:
            fw.op('pe', f, r=['zb', 'ident_b'], w=['psT'])
    fw.op('act', lambda: A.copy(out=L.kT[:, kcol0:kcol0 + W], in_=C.psT[:, 0:W]), r=['psT'], w=[kkey])
    fw.op('dve', lambda: V.tensor_copy(out=L.Vt[:, kt0:kt0 + n, 0:64], in_=pk[:, :, 1, 0, :]), r=akeys, w=[kkey])
    fw.op('dve', lambda: V.tensor_copy(out=L.Vt[:, kt0:kt0 + n, 128:192], in_=pk[:, :, 1, 1, :]), r=akeys, w=[kkey])


def l0_q(C, L, n, ropes):
    nc, fw = C.nc, C.fw
    V, PE = nc.vector, nc.tensor
    for i in range(n):
        ka, pa = next_A(C)
        mm_acc(C, pa, ka, [(L.hT[:, c, i * 128:(i + 1) * 128], L.w0[:, c, 0:512]) for c in range(8)], hkeys('hT') + ['w0'])
        normrope(C, L, pa, ka, 8, L.gainA[:, 0:512], None if ropes is None else ropes[i], L.zb[:, 0:512], ['zb'])
        for j in range(4):
            f = lambda: PE.transpose(out=C.psT[:, j * 128:(j + 1) * 128], in_=L.zb[:, j * 128:(j + 1) * 128], identity=C.ident_b[:])
            if j < 3:
                fw.op_noinc('pe', f, r=['zb', 'ident_b'], w=['psT'])
            else:
                fw.op('pe', f, r=['zb', 'ident_b'], w=['psT'])
        fw.op('dve', lambda: V.tensor_copy(out=L.qT[:, :, i * 128:(i + 1) * 128], in_=C.psT[:, 0:512].rearrange("p (j t) -> p j t", j=4)),
              r=['psT'], w=['qT'])


def fm_proj(C, w, wkey, col0, nch, hT, hname, ntok, evac):
    for j in range(nch):
        ka, pa = next_A(C)
        mm_acc(C, pa[:, 0:ntok], ka, [(w[:, c, col0 + j * 128:col0 + (j + 1) * 128], hT[:, c, 0:ntok]) for c in range(8)],
               hkeys(hname) + [wkey])
        evac(j, ka, pa[:, 0:ntok])


def pool_b_steps(C, L, off, ntok, prc):
    nc, fw = C.nc, C.fw
    V, G, PE = nc.vector, nc.gpsimd, nc.tensor
    add = lambda o, a, b, r, w: fw.op('pool', lambda: G.tensor_tensor(out=o, in0=a, in1=b, op=ALU.add), r=r, w=w)
    a, b, s = L.t1, L.t2, L.sq

    def chain(g):
        u = L.ubT[:, g, :]
        w_ = POOL_W[g]
        if w_ == 2:
            add(s[:, 0:ntok], u[:, off - 1:off - 1 + ntok], u[:, off:off + ntok], ['ubT'], ['sq'])
        else:
            add(a[:, 0:ntok + 15], u[:, off - 8:off + ntok + 7], u[:, off - 7:off + ntok + 8], ['ubT'], ['t1'])
            if w_ == 4:
                add(s[:, 0:ntok], a[:, 6:6 + ntok], a[:, 8:8 + ntok], ['t1'], ['sq'])
            else:
                add(b[:, 0:ntok + 13], a[:, 0:ntok + 13], a[:, 2:ntok + 15], ['t1'], [('t2', 0), ('t2', 1)])
                if w_ == 8:
                    add(s[:, 0:ntok], b[:, 4:4 + ntok], b[:, 8:8 + ntok], [('t2', 0), ('t2', 1)], ['sq'])
                else:
                    add(a[:, 0:ntok + 9], b[:, 0:ntok + 9], b[:, 4:ntok + 13], [('t2', 0), ('t2', 1)], ['t1'])
                    add(s[:, 0:ntok], a[:, 0:ntok], a[:, 8:8 + ntok], ['t1'], ['sq'])
        fw.op('dve', lambda: V.tensor_tensor(out=s[:, 0:ntok], in0=s[:, 0:ntok], in1=prc[:, g, 0:ntok], op=ALU.mult), r=['sq', 'prc'], w=['sq'])
        fw.op('dve', lambda: V.tensor_tensor(out=L.zb[:, 0:ntok], in0=s[:, 0:ntok], in1=u[:, off:off + ntok], op=ALU.subtract),
              r=['sq', 'ubT'], w=['zb'])

    def mm(g):
        ka, pa = next_A(C)
        fw.op('pe', lambda: PE.matmul(pa[:, 0:ntok], lhsT=L.bm[:, g * 128:(g + 1) * 128], rhs=L.zb[:, 0:ntok], start=True, stop=True),
              r=['bm', 'zb'], w=[ka])
        fw.op('dve', lambda: V.scalar_tensor_tensor(out=L.AO[:, 4 + g, 0:ntok], in0=pa[:, 0:ntok], scalar=L.bsc[:, g:g + 1],
                                                    in1=L.gbT[:, g, 0:ntok], op0=ALU.mult, op1=ALU.mult),
              r=[ka, 'bsc', 'gbT'], w=[('AO', 4 + g)])

    return [lambda: chain(0)] + [(lambda g=g: (mm(g), chain(g + 1))) for g in range(3)] + [lambda: mm(3)]


def l0_attn(C, L, nq, ktlist, kkey, extra=None):
    calls = []
    for j in range(4):
        for side in (0, 1):
            rows = slice(side * 64, (side + 1) * 64)
            kts = [(L.kT[:, kt * 128:(kt + 1) * 128], L.Vt[:, kt, side * 64:side * 64 + 128], None, [kkey]) for kt in ktlist]
            calls.append(dict(q64=L.qT[rows, j, 0:nq], qkeys=['qT'], nq=nq, kts=kts, scale=0.125, side=side,
                              dst=L.AO[rows, j, 0:nq], gate=L.gaT[rows, j, 0:nq], dkeys=[('AO', j)], gkeys=['gaT']))
    run_attn_calls(C, calls, extra=extra)


def phase_l0(C, stop_after):
    nc, fw, D = C.nc, C.fw, C.D
    V, A, PE, G = nc.vector, nc.scalar, nc.tensor, nc.gpsimd
    L = Ctx()
    with ExitStack() as s1:
        L.w0 = sbt(nc, s1, 'w0', [128, 8, 2304], BF); L.wo = sbt(nc, s1, 'wo0', [128, 8, 1024], BF)
        phase_setup(C, (0,), after_issue=lambda: (load_w_cast(C, L.w0, D['w_in_e'], 8, 2304, 'w0'),
                                                   load_w_cast(C, L.wo, D['w_out_e'], 8, 1024, 'wo')))
        L.bm = sbt(nc, s1, 'bm', [128, 512], BF); L.bsc = sbt(nc, s1, 'bsc', [128, 4], F32)
        L.gainA = sbt(nc, s1, 'gainA', [128, 640], F32); L.hmask = sbt(nc, s1, 'hmask', [128, 2], F32)
        L.kT = sbt(nc, s1, 'kT', [128, 4608], BF); L.Vt = sbt(nc, s1, 'Vt', [128, 36, 192], BF)
        L.ubT = sbt(nc, s1, 'ubT', [128, 4, 2064], BF); L.hT = sbt(nc, s1, 'hT', [128, 8, 512], BF)
        L.qT = sbt(nc, s1, 'qT', [128, 4, 512], BF); L.gaT = sbt(nc, s1, 'gaT', [128, 4, 512], BF)
        L.gbT = sbt(nc, s1, 'gbT', [128, 4, 512], BF); L.AO = sbt(nc, s1, 'AO', [128, 8, 512], BF)
        L.sq = sbt(nc, s1, 'sq', [128, 640], F32); L.zg = sbt(nc, s1, 'zg', [128, 640], F32)
        L.t1 = sbt(nc, s1, 't1', [128, 640], F32); L.t2 = sbt(nc, s1, 't2', [128, 640], F32)
        L.zb = sbt(nc, s1, 'zb', [128, 640], BF); L.vt32 = sbt(nc, s1, 'vt32', [128, 256], F32)
        L.ropet = sbt(nc, s1, 'ropet', [128, 2, 4, 128], F32); L.prc = sbt(nc, s1, 'prc', [128, 4, 512], F32)
        print('sbuf remaining (l0):', nc.sbuf_bytes_remaining)
        fw.dma('pool', L.bm[:], D['b_mapT'], w=['bm'])
        fw.dma('sp', L.bsc[:], D['b_scaleT'], w=['bsc'])
        fw.dma('sp', L.gainA[:], D['gain_a'], w=['gainA'])
        fw.dma('sp', L.hmask[:], D['hmask'], w=['hmask'])
        fw.op('dve', lambda: V.memset(L.Vt[:, :, 64:128], 1.0), w=[('kv', 'all')])
        fw.dma('pool', L.kT[:, 0:512], D['cak_T'], w=[('kv', 'all')])
        cav = D['cav'].rearrange("(t p) d -> p t d", p=128)
        fw.dma('pool', L.Vt[:, 0:4, 0:64], cav[:, :, 0:64], w=[('kv', 'all')])
        fw.dma('pool', L.Vt[:, 0:4, 128:192], cav[:, :, 64:128], w=[('kv', 'all')])
        rope_src = D['rope_a'].rearrange("(g t p) d -> g p t d", t=4, p=128)
        rb = 0

        def load_rope(grp):
            nonlocal rb
            rb = 1 - rb
            fw.dma('sp', L.ropet[:, rb, :, :], rope_src[grp], w=[('rope', rb)])
            return [(('rope', rb), L.ropet[:, rb, i, :], L.ropet[:, rb, :, :]) for i in range(4)]

        for grp in range(8):
            xts = [load_x(C, D['xs'][grp * 512 + i * 128:grp * 512 + (i + 1) * 128, :]) for i in range(4)]
            prenorm(C, 0, 0, xts, L.hT, 'hT')
            ropes = load_rope(grp)
            l0_kv(C, L, 4, 512 + grp * 512, 4 + grp * 4, ('kv', 'all'), ropes, None)
            if grp < 4:
                fm_proj(C, L.w0, 'w0', 1280, 4, L.hT, 'hT', 512,
                        lambda j, ka, pa: fw.op('act', lambda: A.copy(out=L.ubT[:, j, 8 + grp * 512:8 + (grp + 1) * 512], in_=pa),
                                                r=[ka], w=['ubT']))
            elif grp == 4:
                def ev(j, ka, pa):
                    fw.op('dve', lambda: V.tensor_scalar(out=L.ubT[:, j, 0:8], in0=pa[:, 120:128], scalar1=L.hmask[:, 0:1], scalar2=None,
                                                         op0=ALU.mult), r=[ka, 'hmask'], w=['ubT'])
                    fw.op('dve', lambda: V.tensor_scalar(out=L.ubT[:, j, 2056:2064], in0=pa[:, 0:8], scalar1=L.hmask[:, 1:2], scalar2=None,
                                                         op0=ALU.mult), r=[ka, 'hmask'], w=['ubT'])
                fm_proj(C, L.w0, 'w0', 1280, 4, L.hT, 'hT', 128, ev)
        prs = D['prc_s'].rearrange("p (g t) -> p g t", g=4)
        for t in range(4):
            xts = [load_x(C, D['xs'][t * 512 + i * 128:t * 512 + (i + 1) * 128, :]) for i in range(4)]
            prenorm(C, 0, 0, xts, L.hT, 'hT')
            ropes = load_rope(t)
            l0_q(C, L, 4, ropes)
            fm_proj(C, L.w0, 'w0', 768, 4, L.hT, 'hT', 512,
                    lambda j, ka, pa: fw.op('act', lambda: A.activation(out=L.gaT[:, j, :], in_=pa, func=AF.Silu), r=[ka], w=['gaT']))
            fm_proj(C, L.w0, 'w0', 1792, 4, L.hT, 'hT', 512,
                    lambda j, ka, pa: fw.op('act', lambda: A.activation(out=L.gbT[:, j, :], in_=pa, func=AF.Silu), r=[ka], w=['gbT']))
            fw.dma('sp', L.prc[:], prs[:, :, t * 512:(t + 1) * 512], w=['prc'])
            l0_attn(C, L, 512, list(range(36)), ('kv', 'all'), extra=pool_b_steps(C, L, 8 + t * 512, 512, L.prc))
            outproj_residual(C, 0, 0, L.AO, [('AO', c) for c in range(8)], L.wo, 4, xts,
                             lambda i: D['x1s'][t * 512 + i * 128:t * 512 + (i + 1) * 128, :])
        fw.barrier()
        for o_ in (0, 264, 512, 776):
            fw.op('dve', lambda: V.memset(L.ubT[:, :, o_:o_ + 8], 0.0), w=['ubT'])
        fw.dma('sp', L.prc[:, :, 0:256], D['prc_p'].rearrange("p (g t) -> p g t", g=4), w=['prc'])
        fw.barrier()
        fw.nskeys = {'hT', 'qT', 'gaT', 'gbT', 'AO', 'ubT'}

        def half_ctx(b):
            Lb = Ctx()
            Lb.__dict__.update(L.__dict__)
            for nm in ('hT', 'qT', 'gaT', 'gbT', 'AO'):
                setattr(Lb, nm, getattr(L, nm)[:, :, b * 256:(b + 1) * 256])
            Lb.ubT = L.ubT[:, :, b * 512:b * 512 + 272]
            return Lb

        def prompt_gen(Lb, s):
            xts = [load_x(C, D['xp'][s * 256 + i * 128:s * 256 + (i + 1) * 128, :]) for i in range(2)]
            prenorm(C, 0, 1, xts, Lb.hT, 'hT')
            yield
            reg = s % 2
            l0_kv(C, Lb, 2, reg * 256, reg * 2, ('kv', reg), None, s * 256)
            yield
            fm_proj(C, L.w0, 'w0', 1280, 4, Lb.hT, 'hT', 256,
                    lambda j, ka, pa: fw.op('act', lambda: A.copy(out=Lb.ubT[:, j, 8:264], in_=pa), r=[ka], w=['ubT']))
            yield
            l0_q(C, Lb, 2, None)
            yield
            fm_proj(C, L.w0, 'w0', 768, 4, Lb.hT, 'hT', 256,
                    lambda j, ka, pa: fw.op('act', lambda: A.activation(out=Lb.gaT[:, j, 0:256], in_=pa, func=AF.Silu), r=[ka], w=['gaT']))
            yield
            fm_proj(C, L.w0, 'w0', 1792, 4, Lb.hT, 'hT', 256,
                    lambda j, ka, pa: fw.op('act', lambda: A.activation(out=Lb.gbT[:, j, 0:256], in_=pa, func=AF.Silu), r=[ka], w=['gbT']))
            yield
            l0_attn(C, Lb, 256, [reg * 2, reg * 2 + 1], ('kv', reg), extra=pool_b_steps(C, Lb, 8, 256, L.prc))
            yield
            outproj_residual(C, 0, 1, Lb.AO, [('AO', c) for c in range(8)], L.wo, 2, xts,
                             lambda i: D['x1p'][s * 256 + i * 128:s * 256 + (i + 1) * 128, :])

        for pair in range(2):
            alive = [(b, prompt_gen(half_ctx(b), 2 * pair + b)) for b in range(2)]
            while alive:
                for item in list(alive):
                    fw.ns = 'p%d' % item[0]
                    try:
                        next(item[1])
                    except StopIteration:
                        alive.remove(item)
            fw.ns = None
        fw.nskeys = set()
        fw.barrier()


def _rope_tables(tok, half_dims):
    R = half_dims * 2
    half = R // 2
    inv = (10000.0 ** (-np.arange(0, half, 2, dtype=np.float32) / np.float32(half))).astype(np.float32)
    row = (tok // 64).astype(np.float32)[:, None] * inv[None, :]
    col = (tok % 64).astype(np.float32)[:, None] * inv[None, :]
    cr, sr, cc, sc = np.cos(row), np.sin(row), np.cos(col), np.sin(col)
    cos = np.concatenate([cr, cr, cc, cc], axis=1).astype(np.float32)
    sin = np.concatenate([-sr, sr, -sc, sc], axis=1).astype(np.float32)
    return cos, sin


def _pool_rc(tok, S):
    out = []
    for w in POOL_W:
        lo = np.clip(tok - w // 2, 0, S)
        hi = np.clip(tok + w - w // 2, 0, S)
        out.append((1.0 / (hi - lo).astype(np.float32)).astype(np.float32))
    return np.concatenate(out)


def _rep(v, n=128):
    return np.ascontiguousarray(np.broadcast_to(np.asarray(v, np.float32)[None, :], (n, len(v))))


def _na_bias_tables(rpb, half):
    H = 8
    kc = np.arange(64)
    qc = np.arange(64)
    c0 = np.clip(qc - 8, 0, 48)
    colvalid = (kc[:, None] >= c0[None, :]) & (kc[:, None] < c0[None, :] + 16)
    dc = kc[:, None] - qc[None, :] + 15
    dcc = np.clip(dc, 0, 30)
    out = np.full((H, 2, 64, 4480), NEG, np.float32)

    def fill(dst, delta_ok, dr):
        if not delta_ok:
            return
        vals = rpb[:, dr, :][:, dcc]
        dst[...] = np.where(colvalid[None], vals, NEG)

    for kr2 in range(2):
        for e in range(22):
            delta = kr2 + 10 - e
            fill(out[:, kr2, :, e * 64:(e + 1) * 64], -4 <= delta <= 3, delta + 7 if -4 <= delta <= 3 else 0)
    base = 32 * half

    def exact(dst, qpos, kpos):
        qr = base + qpos
        kr = base + kpos
        if kr < 0 or kr > 63:
            return
        r0 = min(max(qr - 4, 0), 56)
        ok = r0 <= kr < r0 + 8
        fill(dst, ok, kr - qr + 7 if ok else 0)

    for m in range(6):
        for kr2 in range(2):
            for q in range(4):
                exact(out[:, kr2, :, 1408 + m * 256 + q * 64:1408 + m * 256 + (q + 1) * 64], q, -4 + 2 * m + kr2)
    for mi, m in enumerate(range(2, 8)):
        for kr2 in range(2):
            for q in range(4):
                exact(out[:, kr2, :, 2944 + mi * 256 + q * 64:2944 + mi * 256 + (q + 1) * 64], 28 + q, 20 + 2 * m + kr2)
    return np.ascontiguousarray(out.reshape(H, 128, 4480))


def prepare_inputs(inp):
    f = lambda a: np.ascontiguousarray(np.asarray(a), dtype=np.float32)
    xp_, xs_ = f(inp['x_prompt']), f(inp['x_sample'])
    c, c_ctx = f(inp['c']), f(inp['c_ctx'])
    g_pre, g_post = f(inp['g_pre']), f(inp['g_post'])
    com = {}
    com['w_mod'] = f(inp['w_mod']); com['b_mod'] = f(inp['b_mod']); com['g_post'] = g_post
    gpT = np.zeros((128, 32), np.float32)
    for l in range(2):
        for ch in range(8):
            for g in range(2):
                gpT[:, l * 16 + ch * 2 + g] = g_pre[l, ch * 128:(ch + 1) * 128]
    com['g_preT'] = gpT
    sel = np.zeros((2, 256), np.float32); sel[0, 0:128] = 1; sel[1, 128:256] = 1
    com['sel'] = sel
    We = f(inp['w_in_e'])[0]
    com['w_in_e'] = np.ascontiguousarray(np.concatenate(
        [We[:, 0:512][:, PERM_A], We[:, 512:768], We[:, 768:1280][:, PERM_A], We[:, 1280:2304]], axis=1))
    com['gain_a'] = _rep(np.concatenate([np.tile(f(inp['a_q_norm'])[0], 8), np.tile(f(inp['a_k_norm'])[0], 2)]))
    bmap = f(inp['b_map'])[0]
    com['b_mapT'] = np.ascontiguousarray(bmap.transpose(1, 0, 2).reshape(128, 512))
    com['b_scaleT'] = np.ascontiguousarray(f(inp['b_scale'])[0].reshape(4, 128).T)
    Woe = f(inp['w_out_e'])[0]
    com['w_out_e'] = np.ascontiguousarray(np.concatenate([Woe[0:512][PERM_A], Woe[512:1024]], axis=0))
    tokp = np.arange(256)
    com['prc_p'] = _rep(_pool_rc(tokp, 256))
    Wo = f(inp['w_in_o'])[0]
    com['w1_fm'] = np.ascontiguousarray(np.concatenate([Wo[:, 0:512], Wo[:, 512:1024], Wo[:, 1536:2048], Wo[:, 2720:3232]], axis=1))
    com['w1_tm'] = np.ascontiguousarray(np.concatenate([Wo[:, 512:1024], Wo[:, 1024:1536], Wo[:, 2048:2432], Wo[:, 2432:2688],
                                                        Wo[:, 2688:2720]], axis=1))
    kpe_w = Wo[:, 2688:2720]
    swp = np.r_[8:16, 0:8, 24:32, 16:24]
    z64 = np.zeros((1024, 64), np.float32)
    com['w_kpe2'] = np.ascontiguousarray(np.concatenate([z64, kpe_w, z64, kpe_w[:, swp]], axis=1))
    com['gain_q'] = _rep(f(inp['d_q_norm'])[0]); com['gain_kv'] = _rep(f(inp['d_kv_norm'])[0])
    Wuq = f(inp['d_w_uq'])[0].reshape(384, 8, 96)
    Wuq_b = np.concatenate([Wuq[:, :, 0:64], Wuq[:, :, 64:96][:, :, swp]], axis=2)
    com['w_uq_ab'] = np.ascontiguousarray(np.concatenate([Wuq.reshape(384, 768), Wuq_b.reshape(384, 768)], axis=1))
    Wukv = f(inp['d_w_ukv'])[0].reshape(256, 8, 128)
    com['w_uk'] = np.ascontiguousarray(Wukv[:, :, 0:64].reshape(256, 512))
    com['w_uv'] = np.ascontiguousarray(Wukv[:, :, 64:128].reshape(256, 512))
    com['w_out_o'] = f(inp['w_out_o'])[0]
    rpb = f(inp['c_rpb'])[0]
    na = [_na_bias_tables(rpb, 0), _na_bias_tables(rpb, 1)]
    maps = []
    for r in range(8):
        b, half = r // 2, r % 2
        m = dict(com)
        own = np.arange(half * 2048, (half + 1) * 2048)
        other = np.arange(2048, 4096) if half == 0 else np.concatenate([np.arange(1920, 2048), np.arange(0, 1920)])
        tok = np.concatenate([own, other])
        m['xs'] = np.ascontiguousarray(xs_[b][tok])
        m['xp'] = np.ascontiguousarray(xp_[4 * r:4 * r + 4].reshape(1024, 1024))
        cond = np.stack([c[b], c_ctx], axis=0)
        m['condT'] = np.ascontiguousarray(cond.reshape(2, 8, 128).transpose(2, 1, 0).reshape(128, 16))
        cos, sin = _rope_tables(tok, 32)
        m['rope_a'] = np.ascontiguousarray(np.concatenate([cos, sin], axis=1))
        m['prc_s'] = _rep(_pool_rc(own, 4096))
        m['hmask'] = _rep(np.array([0.0, 1.0] if half == 0 else [1.0, 0.0], np.float32))
        m['cak_T'] = np.ascontiguousarray(f(inp['cache_a_k'])[b, 0].reshape(512, 128).T)
        m['cav'] = np.ascontiguousarray(f(inp['cache_a_v'])[b, 0].reshape(512, 128))
        cd, sd = _rope_tables(own, 16)
        rd = np.zeros((128, 2, 2048), np.float32)
        rd[64:96, 0, :] = cd.T; rd[64:96, 1, :] = sd.T
        m['rope_d'] = np.ascontiguousarray(rd.reshape(128, 4096))
        m['cck_T'] = np.ascontiguousarray(f(inp['cache_c_k'])[b, 0].reshape(512, 512).T)
        m['ccv'] = np.ascontiguousarray(f(inp['cache_c_v'])[b, 0].reshape(512, 512))
        m['cckv_T'] = np.ascontiguousarray(f(inp['cache_d_ckv'])[b, 0].T)
        kp = np.zeros((128, 512), np.float32); kp[64:96] = f(inp['cache_d_kpe'])[b, 0].T
        m['ckpe_T'] = kp
        m['nabias'] = na[half]
        maps.append(m)
    return maps


_NC_CACHE = {}


def kernel(**inputs):
    maps = prepare_inputs(inputs)
    if 'nc' not in _NC_CACHE:
        _NC_CACHE['nc'] = build_program()
    res = run_bass_kernel_spmd(_NC_CACHE['nc'], maps, core_ids=list(range(8)))
    R = res.results
    y_p = np.stack([R[r]['y_p'].reshape(4, 256, 1024) for r in range(8)]).reshape(32, 256, 1024)
    y_s = np.stack([R[r]['y_s'] for r in range(8)]).reshape(4, 4096, 1024)
    cat = lambda k, shp: np.ascontiguousarray(np.stack([R[r][k] for r in range(8)]).reshape(shp)).astype(np.float32)
    return (y_p.astype(np.float32), y_s.astype(np.float32),
            cat('nak', (32, 1, 256, 2, 64)), cat('nav', (32, 1, 256, 2, 64)),
            cat('nck', (32, 1, 256, 8, 64)), cat('ncv', (32, 1, 256, 8, 64)),
            cat('nckv', (32, 1, 256, 256)), cat('nkpe', (32, 1, 256, 32)))


def rms_tok(C, ps_ap, pkey, W, gain, gkey, out_ap, okeys):
    nc, fw = C.nc, C.fw
    ks, ss = stat_col(C)
    fw.op('act', lambda: nc.scalar.activation(out=C.junk[:, 0:W], in_=ps_ap, func=AF.Square, accum_out=ss), r=[pkey], w=[ks])
    kr, rs = rstd_col(C, [ks], ss, 1, float(W))
    fw.op('dve', lambda: nc.vector.scalar_tensor_tensor(out=out_ap, in0=ps_ap, scalar=rs, in1=gain, op0=ALU.mult, op1=ALU.mult),
          r=[pkey, gkey] + kr, w=okeys)


def transpose_to(C, src_bf, skey, nblk, dst_fn, dkeys):
    fw, PE, V = C.fw, C.nc.tensor, C.nc.vector
    for k in range(nblk):
        f = lambda: PE.transpose(out=C.psT[:, k * 128:(k + 1) * 128], in_=src_bf[:, k * 128:(k + 1) * 128], identity=C.ident_b[:])
        if k < nblk - 1:
            fw.op_noinc('pe', f, r=[skey, 'ident_b'], w=['psT'])
        else:
            fw.op('pe', f, r=[skey, 'ident_b'], w=['psT'])
    fw.op('dve', lambda: V.tensor_copy(out=dst_fn(), in_=C.psT[:, 0:nblk * 128].rearrange("p (k t) -> p k t", k=nblk)),
          r=['psT'], w=dkeys)


def vpair_copy(C, dst_tile_ap, src_ps, skey, dkeys, npair):
    fw, V = C.fw, C.nc.vector
    d4 = dst_tile_ap.rearrange("p (j a d) -> p j a d", j=npair, a=3, d=64)
    s4 = src_ps.rearrange("p (j a d) -> p j a d", j=npair, a=2, d=64)
    fw.op('dve', lambda: V.tensor_copy(out=d4[:, :, 0, :], in_=s4[:, :, 0, :]), r=[skey], w=dkeys)
    fw.op('dve', lambda: V.tensor_copy(out=d4[:, :, 2, :], in_=s4[:, :, 1, :]), r=[skey], w=dkeys)


def phase_l1(C, stop_after):
    nc, fw, D = C.nc, C.fw, C.D
    V, A, PE, G = nc.vector, nc.scalar, nc.tensor, nc.gpsimd
    L = Ctx()
    xinB_flat = D['xinB'].rearrange("r c -> (r c)")
    KCv = xinB_flat[OFF_KC:OFF_KC + 512 * 512].rearrange("(j p t) -> p j t", j=4, p=128, t=512)
    VCv = xinB_flat[OFF_VC:OFF_VC + 512 * 512].rearrange("(n p c) -> p n c", n=4, p=128, c=512)
    xout_flat = D['xout'].rearrange("r c -> (r c)")
    RK = XIN_ROWS * 2048
    with ExitStack() as sa:
        L.w1 = sbt(nc, sa, 'w1', [128, 8, 2048], BF); L.w1t = sbt(nc, sa, 'w1t', [128, 8, 1696], BF)
        L.wk2 = sbt(nc, sa, 'wk2', [128, 8, 192], BF); L.gq = sbt(nc, sa, 'gq', [128, 384], F32); L.gkv = sbt(nc, sa, 'gkv', [128, 256], F32)
        L.hT = sbt(nc, sa, 'hT1', [128, 8, 512], BF)
        L.cqb = sbt(nc, sa, 'cqb', [128, 384], BF); L.ckvb = sbt(nc, sa, 'ckvb', [128, 256], BF)
        sa1 = sa.enter_context(ExitStack())
        L.stg = sbt(nc, sa1, 'stg', [128, 2, 4, 512], BF); L.stgv = sbt(nc, sa1, 'stgv', [128, 2, 512], BF)
        L.stgq = sbt(nc, sa1, 'stgq', [128, 3, 512], BF); L.stgc = sbt(nc, sa1, 'stgc', [128, 2, 512], BF)
        L.stgk = sbt(nc, sa1, 'stgk', [128, 512], BF); L.rd = sbt(nc, sa1, 'rd', [128, 2, 512], F32)
        L.r1 = sbt(nc, sa1, 'r1', [128, 512], F32); L.r2 = sbt(nc, sa1, 'r2', [128, 512], F32)
        L.f32a = C.ty; f32b = C.rcb[:, 0, 0:288]
        print('sbuf remaining (l1 phase A):', nc.sbuf_bytes_remaining)
        fw.dma('sp', L.gq[:], D['gain_q'], w=['gq'])
        fw.dma('sp', L.gkv[:], D['gain_kv'], w=['gkv'])
        phase_setup(C, (1,), stack=sa1, after_issue=lambda: (load_w_cast(C, L.w1, D['w1_fm'], 8, 2048, 'w1'),
                                                            load_w_cast(C, L.w1t, D['w1_tm'], 8, 1696, 'w1t'),
                                                            load_w_cast(C, L.wk2, D['w_kpe2'], 8, 192, 'wk2')))
        rdv = D['rope_d'].rearrange("p (a t) -> p a t", a=2)
        sg = 0
        for t in range(4):
            tc0, tc1 = t * 512, (t + 1) * 512
            xts = [load_x(C, D['x1s'][tc0 + i * 128:tc0 + (i + 1) * 128, :]) for i in range(4)]
            prenorm(C, 1, 0, xts, L.hT, 'hT')
            fw.dma('sp', L.rd[:], rdv[:, :, tc0:tc1], w=['rd'])
            for grp, (dname, silu) in enumerate((('s_qc', False), ('s_kc', False), ('s_gc', True), ('s_gd', True))):
                sg = 1 - sg
                sgk = ('stg', sg)

                def ev(j, ka, pa, sg=sg, silu=silu, sgk=sgk):
                    if silu:
                        fw.op('act', lambda: A.activation(out=L.stg[:, sg, j, :], in_=pa, func=AF.Silu), r=[ka], w=[sgk])
                    else:
                        fw.op('act', lambda: A.copy(out=L.stg[:, sg, j, :], in_=pa), r=[ka], w=[sgk])
                fm_proj(C, L.w1, 'w1', grp * 512, 4, L.hT, 'hT', 512, ev)
                fw.dma('sp', D[dname].rearrange("(j p) t -> p j t", p=128)[:, :, tc0:tc1], L.stg[:, sg, :, :], r=[sgk], w=[dname])
                if dname == 's_kc' and t == 0:
                    fw.dma('sp', KCv[:, :, 0:256], L.stg[:, sg, :, 0:256], r=[sgk], w=['xinB'])
                if dname == 's_kc' and t == 3:
                    fw.dma('sp', KCv[:, :, 256:512], L.stg[:, sg, :, 256:512], r=[sgk], w=['xinB'])
            k0, p0 = next_A(C)
            mm_acc(C, p0[0:96, :], k0, [(L.wk2[:, c, 0:96], L.hT[:, c, :]) for c in range(8)], hkeys('hT') + ['wk2'])
            k1, p1 = next_A(C)
            mm_acc(C, p1[0:96, :], k1, [(L.wk2[:, c, 96:192], L.hT[:, c, :]) for c in range(8)], hkeys('hT') + ['wk2'])
            fw.op('dve', lambda: V.tensor_tensor(out=L.r1[64:96, :], in0=p0[64:96, :], in1=L.rd[64:96, 0, :], op=ALU.mult), r=[k0, 'rd'], w=['r1'])
            fw.op('dve', lambda: V.tensor_tensor(out=L.r2[64:96, :], in0=p1[64:96, :], in1=L.rd[64:96, 1, :], op=ALU.mult), r=[k1, 'rd'], w=['r2'])
            fw.op('dve', lambda: V.tensor_tensor(out=L.stgk[64:96, :], in0=L.r1[64:96, :], in1=L.r2[64:96, :], op=ALU.add), r=['r1', 'r2'], w=['stgk'])
            fw.dma('sp', D['xin'][256:288, tc0:tc1], L.stgk[64:96, :], r=['stgk'], w=['xin'])
            for i in range(4):
                sv = i % 2
                ka, pa = next_A(C)
                mm_acc(C, pa, ka, [(L.hT[:, c, i * 128:(i + 1) * 128], L.w1t[:, c, 512:1024]) for c in range(8)], hkeys('hT') + ['w1t'])
                fw.op('act', lambda: A.copy(out=L.stgv[:, sv, :], in_=pa), r=[ka], w=[('stgv', sv)])
                fw.dma('sp', D['s_vc'][tc0 + i * 128:tc0 + (i + 1) * 128, :], L.stgv[:, sv, :], r=[('stgv', sv)], w=['s_vc'])
                if (t == 0 and i < 2) or (t == 3 and i >= 2):
                    fw.dma('sp', VCv[:, i, :], L.stgv[:, sv, :], r=[('stgv', sv)], w=['xinB'])
                ka, pa = next_A(C)
                mm_acc(C, pa[:, 0:384], ka, [(L.hT[:, c, i * 128:(i + 1) * 128], L.w1t[:, c, 1024:1408]) for c in range(8)], hkeys('hT') + ['w1t'])
                rms_tok(C, pa[:, 0:384], ka, 384, L.gq[:], 'gq', L.cqb[:], ['cqb'])
                transpose_to(C, L.cqb, 'cqb', 3, lambda: L.stgq[:, :, i * 128:(i + 1) * 128], ['stgq'])
                ka, pa = next_A(C)
                mm_acc(C, pa[:, 0:256], ka, [(L.hT[:, c, i * 128:(i + 1) * 128], L.w1t[:, c, 1408:1664]) for c in range(8)], hkeys('hT') + ['w1t'])
                rms_tok(C, pa[:, 0:256], ka, 256, L.gkv[:], 'gkv', L.ckvb[:], ['ckvb'])
                transpose_to(C, L.ckvb, 'ckvb', 2, lambda: L.stgc[:, :, i * 128:(i + 1) * 128], ['stgc'])
            fw.dma('sp', D['s_cq'].rearrange("(k p) t -> p k t", p=128)[:, :, tc0:tc1], L.stgq[:], r=['stgq'], w=['s_cq'])
            fw.dma('sp', D['xin'][0:256, :].rearrange("(k p) t -> p k t", p=128)[:, :, tc0:tc1], L.stgc[:], r=['stgc'], w=['xin'])
        if stop_after == 'l1a0':
            return
        fw.custom('pool', lambda: G.collective_compute("AllGather", ALU.bypass, replica_groups=[[0, 1], [2, 3], [4, 5], [6, 7]],
                                                       ins=[D['xin']], outs=[D['xout']]), C.cc_sem, 1, r=['xin'], w=['xout'])
        fw.custom('pool', lambda: G.collective_compute("AllGather", ALU.bypass, replica_groups=[[0, 1], [2, 3], [4, 5], [6, 7]],
                                                       ins=[D['xinB']], outs=[D['xoutB']]), C.cc_sem2, 1, r=['xinB'], w=['xoutB'])
        xout_tok = (fw.state['xout'], fw.state['xoutB'])
        if stop_after == 'l1a':
            return
        fw.barrier()
        fw.state['xout'], fw.state['xoutB'] = xout_tok
        sa1.close()
        sa2 = sa.enter_context(ExitStack())
        L.wuq = sbt(nc, sa2, 'wuq', [128, 3, 1536], BF); L.wuk = sbt(nc, sa2, 'wuk', [128, 2, 512], BF); L.wuv = sbt(nc, sa2, 'wuv', [128, 2, 512], BF)
        L.wo = sbt(nc, sa2, 'wo1', [128, 8, 1024], BF)
        L.qcp = sbt(nc, sa2, 'qcp', [128, 4, 256], BF); L.kcp = sbt(nc, sa2, 'kcp', [128, 4, 256], BF)
        L.gcp = sbt(nc, sa2, 'gcp', [128, 4, 256], BF); L.gdp = sbt(nc, sa2, 'gdp', [128, 4, 256], BF)
        L.kpep = sbt(nc, sa2, 'kpep', [128, 256], BF); L.Vcp = sbt(nc, sa2, 'Vcp', [128, 2, 768], BF)
        L.cqTp = sbt(nc, sa2, 'cqTp', [128, 3, 256], BF); L.ckvTp = sbt(nc, sa2, 'ckvTp', [128, 2, 256], BF)
        L.kThp = sbt(nc, sa2, 'kThp', [128, 2, 256], BF); L.Vdp = sbt(nc, sa2, 'Vdp', [128, 2, 192], BF)
        L.qdT = sbt(nc, sa2, 'qdT', [128, 2, 512], BF); L.AOp = sbt(nc, sa2, 'AOp', [128, 8, 256], BF)
        L.f32a = sbt(nc, sa2, 'f32a', [128, 1024], F32)
        print('sbuf remaining (l1 prompts):', nc.sbuf_bytes_remaining)
        load_w_cast(C, L.wuq, D['w_uq_ab'], 3, 1536, 'wuq')
        load_w_cast(C, L.wuk, D['w_uk'], 2, 512, 'wuk')
        load_w_cast(C, L.wuv, D['w_uv'], 2, 512, 'wuv')
        load_w_cast(C, L.wo, D['w_out_o'], 8, 1024, 'wo')
        fw.op('dve', lambda: V.memset(L.Vcp[:], 1.0), w=['Vcp'])
        fw.op('dve', lambda: V.memset(L.Vdp[:], 1.0), w=['Vdp'])
        fw.op('dve', lambda: V.memset(L.kThp[:], 0.0), w=[('kThp', 0), ('kThp', 1)])
        fw.op('dve', lambda: V.memset(L.qdT[:], 0.0), w=[('qdT', 0), ('qdT', 1)])
        for s in range(4):
            r0 = s * 256
            xts = [load_x(C, D['x1p'][r0 + i * 128:r0 + (i + 1) * 128, :]) for i in range(2)]
            prenorm(C, 1, 1, xts, L.hT, 'hT')
            for grp, (dst, silu, dk) in enumerate(((L.qcp, False, 'qcp'), (L.kcp, False, 'kcp'), (L.gcp, True, 'gcp'), (L.gdp, True, 'gdp'))):
                def ev(j, ka, pa, dst=dst, silu=silu, dk=dk):
                    if silu:
                        fw.op('act', lambda: A.activation(out=dst[:, j, :], in_=pa, func=AF.Silu), r=[ka], w=[dk])
                    else:
                        fw.op('act', lambda: A.copy(out=dst[:, j, :], in_=pa), r=[ka], w=[dk])
                fm_proj(C, L.w1, 'w1', grp * 512, 4, L.hT, 'hT', 256, ev)
            k0, p0 = next_A(C)
            mm_acc(C, p0[0:96, 0:256], k0, [(L.wk2[:, c, 0:96], L.hT[:, c, 0:256]) for c in range(8)], hkeys('hT') + ['wk2'])
            fw.op('dve', lambda: V.tensor_copy(out=L.kpep[64:96, :], in_=p0[64:96, 0:256]), r=[k0], w=['kpep'])
            if stop_after == 'l1p1':
                fw.barrier(); return
            for i in range(2):
                tk = slice(i * 128, (i + 1) * 128)
                for half in range(2):
                    mm_acc(C, C.psA[:, half * 512:(half + 1) * 512], ('A', half),
                           [(L.hT[:, c, tk], L.w1t[:, c, half * 512:(half + 1) * 512]) for c in range(8)], hkeys('hT') + ['w1t'])
                fw.op('act', lambda: A.copy(out=L.f32a[:, 0:512], in_=C.psA[:, 0:512]), r=[('A', 0)], w=['f32a'])
                fw.op('act', lambda: A.copy(out=L.f32a[:, 512:1024], in_=C.psA[:, 512:1024]), r=[('A', 1)], w=['f32a'])
                fw.dma('sp', D['nck'][r0 + i * 128:r0 + (i + 1) * 128, :], L.f32a[:, 0:512], r=['f32a'])
                fw.dma('sp', D['ncv'][r0 + i * 128:r0 + (i + 1) * 128, :], L.f32a[:, 512:1024], r=['f32a'])
                vpair_copy(C, L.Vcp[:, i, :], C.psA[:, 512:1024], ('A', 1), ['Vcp'], 4)
                if stop_after == 'l1p1a':
                    fw.barrier(); return
                mm_acc(C, C.psA[:, 0:384], ('A', 0), [(L.hT[:, c, tk], L.w1t[:, c, 1024:1408]) for c in range(8)], hkeys('hT') + ['w1t'])
                rms_tok(C, C.psA[:, 0:384], ('A', 0), 384, L.gq[:], 'gq', L.cqb[:], ['cqb'])
                transpose_to(C, L.cqb, 'cqb', 3, lambda: L.cqTp[:, :, tk], ['cqTp'])
                if stop_after == 'l1p1b':
                    fw.barrier(); return
                mm_acc(C, C.psA[:, 512:800], ('A', 1), [(L.hT[:, c, tk], L.w1t[:, c, 1408:1696]) for c in range(8)], hkeys('hT') + ['w1t'])
                rms_tok(C, C.psA[:, 512:768], ('A', 1), 256, L.gkv[:], 'gkv', f32b[:, 0:256], [('rcb', 0)])
                fw.op('act', lambda: A.copy(out=f32b[:, 256:288], in_=C.psA[:, 768:800]), r=[('A', 1)], w=[('rcb', 0)])
                fw.dma('sp', D['nckv'][r0 + i * 128:r0 + (i + 1) * 128, :], f32b[:, 0:256], r=[('rcb', 0)])
                fw.dma('sp', D['nkpe'][r0 + i * 128:r0 + (i + 1) * 128, :], f32b[:, 256:288], r=[('rcb', 0)])
                fw.op('act', lambda: A.copy(out=L.ckvb[:], in_=f32b[:, 0:256]), r=[('rcb', 0)], w=['ckvb'])
                transpose_to(C, L.ckvb, 'ckvb', 2, lambda: L.ckvTp[:, :, tk], ['ckvTp'])
            if stop_after == 'l1p2':
                fw.barrier(); return
            calls = []
            for h in range(8):
                j, side = h // 2, h % 2
                rows = slice(side * 64, (side + 1) * 64)
                kts = [(L.kcp[:, j, kt * 128:(kt + 1) * 128], L.Vcp[:, kt, j * 192 + side * 64:j * 192 + side * 64 + 128], None,
                        ['kcp', 'Vcp']) for kt in range(2)]
                calls.append(dict(q64=L.qcp[rows, j, :], qkeys=['qcp'], nq=256, kts=kts, scale=0.125, side=side,
                                  dst=L.AOp[rows, j, :], gate=L.gcp[rows, j, :], dkeys=[('AOp', j)], gkeys=['gcp']))
            run_attn_calls(C, calls)
            if stop_after == 'l1p3':
                fw.barrier(); return
            for j in range(4):
                for kt in range(2):
                    ka, pa = next_A(C)
                    mm_acc(C, pa[:, 0:128], ka, [(L.ckvTp[:, c, kt * 128:(kt + 1) * 128], L.wuv[:, c, j * 128:(j + 1) * 128]) for c in range(2)],
                           ['ckvTp', 'wuv'])
                    vpair_copy(C, L.Vdp[:, kt, :], pa[:, 0:128], ka, ['Vdp'], 1)
                for hh in range(2):
                    h = 2 * j + hh
                    ka, pa = next_A(C)
                    mm_acc(C, pa[0:64, 0:256], ka, [(L.wuk[:, c, h * 64:(h + 1) * 64], L.ckvTp[:, c, :]) for c in range(2)], ['ckvTp', 'wuk'])
                    fw.op('act', lambda: A.copy(out=L.kThp[0:64, hh, :], in_=pa[0:64, 0:256]), r=[ka], w=[('kThp', hh)])
                    fw.op('pool', lambda: G.tensor_copy(out=L.kThp[64:96, hh, :], in_=L.kpep[64:96, :]), r=['kpep'], w=[('kThp', hh)])
                    ka, pa = next_A(C)
                    mm_acc(C, pa[0:96, 0:256], ka, [(L.wuq[:, c, h * 96:(h + 1) * 96], L.cqTp[:, c, :]) for c in range(3)], ['cqTp', 'wuq'])
                    fw.op('act', lambda: A.copy(out=L.qdT[0:64, hh, 0:256], in_=pa[0:64, 0:256]), r=[ka], w=[('qdT', hh)])
                    fw.op('act', lambda: A.copy(out=L.qdT[64:96, hh, 0:256], in_=pa[64:96, 0:256]), r=[ka], w=[('qdT', hh)])
                    rows = slice(hh * 64, (hh + 1) * 64)
                    kts = [(L.kThp[:, hh, kt * 128:(kt + 1) * 128], L.Vdp[:, kt, hh * 64:hh * 64 + 128], None, [('kThp', hh), 'Vdp'])
                           for kt in range(2)]
                    attention(C, L.qdT[:, hh, 0:256], 256, [('qdT', hh)], kts, 96.0 ** -0.5, hh, L.AOp[rows, 4 + j, :], L.gdp[rows, j, :],
                              [('AOp', 4 + j)], ['gdp'])
            if stop_after == 'l1p4':
                fw.barrier(); return
            outproj_residual(C, 1, 1, L.AOp, [('AOp', c) for c in range(8)], L.wo, 2, xts,
                             lambda i: D['y_p'][r0 + i * 128:r0 + (i + 1) * 128, :])
        fw.barrier()
        fw.state['xout'], fw.state['xoutB'] = xout_tok
    if stop_after == 'l1p':
        return
    phase_l1_sample(C, xout_flat, RK)


def phase_l1_sample(C, xout_flat, RK):
    nc, fw, D = C.nc, C.fw, C.D
    V, A, PE, G = nc.vector, nc.scalar, nc.tensor, nc.gpsimd
    with ExitStack() as so:
        AOc = sbt(nc, so, 'AOc', [128, 4, 2048], BF)
        AOd = sbt(nc, so, 'AOd', [128, 4, 2048], BF)
        fw.dma('sp', AOc[:], D['s_gc'].rearrange("(j p) t -> p j t", p=128), w=[('AOc', j, t) for j in range(4) for t in range(4)])
        fw.dma('sp', AOd[:], D['s_gd'].rearrange("(j p) t -> p j t", p=128), w=[('AOd', j, t) for j in range(4) for t in range(4)])
        with ExitStack() as sc:
            kc = sbt(nc, sc, 'kcA', [128, 4, 3072], BF)
            Vc = sbt(nc, sc, 'VcA', [128, 24, 768], BF)
            qc = sbt(nc, sc, 'qcA', [128, 4, 2048], BF)
            EB = sbt(nc, sc, 'EB', [128, 2, 4480], BF)
            ebs = sbt(nc, sc, 'ebs', [128, 2, 2240], F32)
            print('sbuf remaining (l1 C):', nc.sbuf_bytes_remaining)
            fw.op('dve', lambda: V.memset(Vc[:], 1.0), w=['Vc'])
            fw.dma('pool', kc[:, :, 0:512], D['cck_T'].rearrange("(j p) k -> p j k", p=128), w=['kc'])
            fw.dma('sp', kc[:, :, 768:2816], D['s_kc'].rearrange("(j p) t -> p j t", p=128), w=['kc'])
            xoB = D['xoutB'].rearrange("r c -> (r c)")
            RKB = XINB_ROWS * 2048
            kc0 = xoB[OFF_KC:OFF_KC + 512 * 512].rearrange("(j p t) -> p j t", j=4, p=128, t=512)
            kc1 = xoB[RKB + OFF_KC:RKB + OFF_KC + 512 * 512].rearrange("(j p t) -> p j t", j=4, p=128, t=512)
            fw.dma('sp', kc[:, :, 512:768], kc0[:, :, 256:512], r=['xoutB'], w=['kc'])
            fw.dma('sp', kc[:, :, 2816:3072], kc1[:, :, 0:256], r=['xoutB'], w=['kc'])
            qq = D['s_qc'].rearrange("(j p) t -> p j t", p=128)
            fw.dma('sp', qc[:], qq, w=['qc'])
            ccv = D['ccv'].rearrange("(n p) c -> p n c", p=128)
            svc = D['s_vc'].rearrange("(n p) c -> p n c", p=128)
            vc0 = xoB[OFF_VC:OFF_VC + 512 * 512].rearrange("(n p c) -> p n c", n=4, p=128, c=512)
            vc1 = xoB[RKB + OFF_VC:RKB + OFF_VC + 512 * 512].rearrange("(n p c) -> p n c", n=4, p=128, c=512)
            for h in range(8):
                dcol = (h // 2) * 192 + (h % 2) * 128
                fw.dma('pool', Vc[:, 0:4, dcol:dcol + 64], ccv[:, :, h * 64:(h + 1) * 64], w=['Vc'])
                fw.dma('sp', Vc[:, 6:22, dcol:dcol + 64], svc[:, :, h * 64:(h + 1) * 64], w=['Vc'])
                fw.dma('sp', Vc[:, 4:6, dcol:dcol + 64], vc0[:, 2:4, h * 64:(h + 1) * 64], r=['xoutB'], w=['Vc'])
                fw.dma('sp', Vc[:, 22:24, dcol:dcol + 64], vc1[:, 0:2, h * 64:(h + 1) * 64], r=['xoutB'], w=['Vc'])
            eng_rr = 0
            for h in range(8):
                j, side = h // 2, h % 2
                rows = slice(side * 64, (side + 1) * 64)
                eb = h % 2
                for part in range(2):
                    fw.dma('sp', ebs[:, part, :], D['nabias'][h, :, part * 2240:(part + 1) * 2240], w=[('ebs', part)])
                    fw.op('act', lambda: A.activation(out=EB[:, eb, part * 2240:(part + 1) * 2240], in_=ebs[:, part, :], func=AF.Exp),
                          r=[('ebs', part)], w=[('EB', eb)])
                calls = []
                for t in range(4):
                    kts = [(kc[:, j, kt * 128:(kt + 1) * 128], Vc[:, kt, j * 192 + side * 64:j * 192 + side * 64 + 128], None, ['kc', 'Vc'])
                           for kt in range(4)]
                    for m in range(8):
                        kt = 4 * t + m
                        c0 = (14 - 2 * m) * 64
                        cr = (0, 512)
                        if t == 0 and m <= 5:
                            spec = [(EB[:, eb, 1408 + m * 256:1408 + (m + 1) * 256], 0, 256), (EB[:, eb, c0 + 256:c0 + 512], 256, 512)]
                        elif t == 3 and m >= 2:
                            spec = [(EB[:, eb, c0:c0 + 256], 0, 256), (EB[:, eb, 2944 + (m - 2) * 256:2944 + (m - 1) * 256], 256, 512)]
                        else:
                            lo, hi = max(0, 2 * m - 7), min(7, 2 * m + 1)
                            cr = (lo * 64, (hi + 1) * 64)
                            spec = [(EB[:, eb, c0 + cr[0]:c0 + cr[1]], cr[0], cr[1])]
                        eng_rr += 1
                        ebl = [(ap_, a0, a1, [('EB', eb)], 'pool' if eng_rr % 2 else 'dve') for (ap_, a0, a1) in spec]
                        kts.append((kc[:, j, 512 + kt * 128:512 + (kt + 1) * 128],
                                    Vc[:, 4 + kt, j * 192 + side * 64:j * 192 + side * 64 + 128], ebl, ['kc', 'Vc'], cr))
                    ao = AOc[rows, j, t * 512:(t + 1) * 512]
                    calls.append(dict(q64=qc[rows, j, t * 512:(t + 1) * 512], qkeys=['qc'], nq=512, kts=kts, scale=0.125, side=side,
                                      dst=ao, gate=ao, dkeys=[('AOc', j, t)], gkeys=[('AOc', j, t)]))
                run_attn_calls(C, calls, la=4, banks=[(C.psS[i][:, :], ('S', i)) for i in range(3)] +
                               [(C.psA[:, i * 512:(i + 1) * 512], ('A', i)) for i in range(2)])
            fw.barrier()
        with ExitStack() as sd:
            ckvT = sbt(nc, sd, 'ckvTA', [128, 2, 4608], BF)
            kpeT = sbt(nc, sd, 'kpeTA', [128, 4608], BF)
            cqT = sbt(nc, sd, 'cqTA', [128, 3, 2048], BF)
            rd = sbt(nc, sd, 'rdA', [128, 2, 2048], F32)
            wuq = sbt(nc, sd, 'wuqA', [128, 3, 1536], BF); wuk = sbt(nc, sd, 'wukA', [128, 2, 512], BF); wuv = sbt(nc, sd, 'wuvA', [128, 2, 512], BF)
            kTh = sbt(nc, sd, 'kThA', [128, 2, 4608], BF)
            Vd = sbt(nc, sd, 'VdA', [128, 36, 192], BF)
            qdT = sbt(nc, sd, 'qdTA', [128, 2, 512], BF)
            r1 = sbt(nc, sd, 'r1A', [128, 512], F32); r2 = sbt(nc, sd, 'r2A', [128, 512], F32)
            print('sbuf remaining (l1 D):', nc.sbuf_bytes_remaining)
            load_w_cast(C, wuq, D['w_uq_ab'], 3, 1536, 'wuq')
            load_w_cast(C, wuk, D['w_uk'], 2, 512, 'wuk')
            load_w_cast(C, wuv, D['w_uv'], 2, 512, 'wuv')
            fw.op('dve', lambda: V.memset(Vd[:], 1.0), w=['Vd'])
            fw.op('dve', lambda: V.memset(kTh[:], 0.0), w=[('kTh', 0), ('kTh', 1)])
            fw.op('dve', lambda: V.memset(qdT[:], 0.0), w=[('qdT', 0), ('qdT', 1)])
            fw.dma('pool', ckvT[:, :, 0:512], D['cckv_T'].rearrange("(k p) t -> p k t", p=128), w=['ckvT'])
            fw.dma('pool', kpeT[64:96, 0:512], D['ckpe_T'][64:96, :], w=['kpeT'])
            for rk in range(2):
                base = rk * RK
                src = xout_flat[base:base + 256 * 2048].rearrange("(k p t) -> p k t", k=2, p=128, t=2048)
                fw.dma('sp', ckvT[:, :, 512 + rk * 2048:512 + (rk + 1) * 2048], src, r=['xout'], w=['ckvT'])
                srck = xout_flat[base + OFF_KPE:base + OFF_KPE + 32 * 2048].rearrange("(p t) -> p t", p=32, t=2048)
                fw.dma('sp', kpeT[64:96, 512 + rk * 2048:512 + (rk + 1) * 2048], srck, r=['xout'], w=['kpeT'])
            fw.dma('sp', cqT[:], D['s_cq'].rearrange("(k p) t -> p k t", p=128), w=['cqT'])
            fw.dma('sp', rd[:], D['rope_d'].rearrange("p (a t) -> p a t", a=2), w=['rd'])
            sc = 96.0 ** -0.5
            for j in range(4):
                for g4 in range(9):
                    ka, pa = next_A(C)
                    for k4 in range(4):
                        kt = g4 * 4 + k4
                        mm_acc(C, pa[:, k4 * 128:(k4 + 1) * 128], ka,
                               [(ckvT[:, c, kt * 128:(kt + 1) * 128], wuv[:, c, j * 128:(j + 1) * 128]) for c in range(2)], ['ckvT', 'wuv'])
                    d4 = Vd[:, g4 * 4:(g4 + 1) * 4, :].rearrange("p k (a d) -> p k a d", a=3, d=64)
                    s4 = pa.rearrange("p (k a d) -> p k a d", k=4, a=2, d=64)
                    fw.op('dve', lambda: V.tensor_copy(out=d4[:, :, 0, :], in_=s4[:, :, 0, :]), r=[ka], w=['Vd'])
                    fw.op('dve', lambda: V.tensor_copy(out=d4[:, :, 2, :], in_=s4[:, :, 1, :]), r=[ka], w=['Vd'])
                for hh in range(2):
                    h = 2 * j + hh
                    for blk in range(9):
                        ka, pa = next_A(C)
                        mm_acc(C, pa[0:64, :], ka, [(wuk[:, c, h * 64:(h + 1) * 64], ckvT[:, c, blk * 512:(blk + 1) * 512]) for c in range(2)],
                               ['ckvT', 'wuk'])
                        fw.op('dve', lambda: V.tensor_copy(out=kTh[0:64, hh, blk * 512:(blk + 1) * 512], in_=pa[0:64, :]), r=[ka], w=[('kTh', hh)])
                    fw.op('pool', lambda: G.tensor_copy(out=kTh[64:96, hh, :], in_=kpeT[64:96, :]), r=['kpeT'], w=[('kTh', hh)])
                def emit_q(t, hh, j=j):
                    h = 2 * j + hh
                    tq = slice(t * 512, (t + 1) * 512)
                    k0, p0 = next_A(C)
                    mm_acc(C, p0[0:96, :], k0, [(wuq[:, c, h * 96:(h + 1) * 96], cqT[:, c, tq]) for c in range(3)], ['cqT', 'wuq'])
                    k1, p1 = next_A(C)
                    mm_acc(C, p1[0:96, :], k1, [(wuq[:, c, 768 + h * 96:768 + (h + 1) * 96], cqT[:, c, tq]) for c in range(3)], ['cqT', 'wuq'])
                    fw.op('dve', lambda: V.tensor_copy(out=qdT[0:64, hh, :], in_=p0[0:64, :]), r=[k0], w=[('qdT', hh)])
                    fw.op('dve', lambda: V.tensor_tensor(out=r1[64:96, :], in0=p0[64:96, :], in1=rd[64:96, 0, tq], op=ALU.mult), r=[k0, 'rd'], w=['r1'])
                    fw.op('dve', lambda: V.tensor_tensor(out=r2[64:96, :], in0=p1[64:96, :], in1=rd[64:96, 1, tq], op=ALU.mult), r=[k1, 'rd'], w=['r2'])
                    fw.op('dve', lambda: V.tensor_tensor(out=qdT[64:96, hh, :], in0=r1[64:96, :], in1=r2[64:96, :], op=ALU.add),
                          r=['r1', 'r2'], w=[('qdT', hh)])

                seq = [(t, hh) for t in range(4) for hh in range(2)]
                emit_q(*seq[0])
                for i, (t, hh) in enumerate(seq):
                    tq = slice(t * 512, (t + 1) * 512)
                    rows = slice(hh * 64, (hh + 1) * 64)
                    kts = [(kTh[:, hh, kt * 128:(kt + 1) * 128], Vd[:, kt, hh * 64:hh * 64 + 128], None, [('kTh', hh), 'Vd'])
                           for kt in range(36)]
                    ao = AOd[rows, j, tq]
                    fl = (lambda nx=seq[i + 1]: emit_q(*nx)) if i + 1 < len(seq) else None
                    attention(C, qdT[:, hh, :], 512, [('qdT', hh)], kts, sc, hh, ao, ao, [('AOd', j, t)], [('AOd', j, t)], filler=fl)
            fw.barrier()
        with ExitStack() as sp_:
            wo = sbt(nc, sp_, 'wo1A', [128, 8, 1024], BF)
            load_w_cast(C, wo, D['w_out_o'], 8, 1024, 'wo')
            for t in range(4):
                xts = [load_x(C, D['x1s'][t * 512 + i * 128:t * 512 + (i + 1) * 128, :]) for i in range(4)]
                keys = [('AOc', j, t) for j in range(4)] + [('AOd', j, t) for j in range(4)]
                outproj_residual(C, 1, 0, None, keys, wo, 4, xts,
                                 lambda i: D['y_s'][t * 512 + i * 128:t * 512 + (i + 1) * 128, :],
                                 ao_fn=lambda c, i: (AOc[:, c, t * 512 + i * 128:t * 512 + (i + 1) * 128] if c < 4
                                                     else AOd[:, c - 4, t * 512 + i * 128:t * 512 + (i + 1) * 128]))
            fw.barrier()
```
